# Optimizing a Trainium2 kernel written in Bass

```python
import jax, jax.numpy as jnp
from jax import lax
import numpy as np

D_MODEL = 1024
BATCH = 8
SEQ = 2048
DEPTH = 1
DEC_BATCH = 128
DEC_SEQ = 1
PAST_LEN = 16384
PAGE_SIZE = 128

D_A = D_MODEL
G_A = 8
D_B = D_MODEL
G_B = 8
DG_B = D_B // G_B
CHUNK = 128
CONV_W = 3
D_FF = 2816
P_DIM = 256
N_IN = 3 * D_A + 2 * D_B + 2 * D_MODEL
EPS = 1e-6

kernel_name = "hybrid_shortconv_gmlp_convffn_step"


def rmsnorm(x, g):
    xf = x.astype(jnp.float32)
    xf = xf * lax.rsqrt(jnp.mean(xf * xf, axis=-1, keepdims=True) + EPS)
    return (xf * g.astype(jnp.float32)).astype(x.dtype)


def causal_dwconv(x, w, prev):
    L = x.shape[1]
    xp = jnp.concatenate([prev, x], axis=1)
    y = xp[:, 0:L] * w[0]
    for k in range(1, CONV_W):
        y = y + xp[:, k:k + L] * w[k]
    return y, xp[:, -(CONV_W - 1):]


def chunk_spatial(v, w_s, b_s):
    B, L, _ = v.shape
    lc = min(L, CHUNK)
    n_c = -(-L // lc)
    lp = n_c * lc
    vp = jnp.pad(v, ((0, 0), (0, lp - L), (0, 0))).reshape(B, n_c, lc, G_B, DG_B)
    w = jnp.tril(w_s[:, :lc, :lc])
    out = jnp.einsum("gts,bcsgd->bctgd", w, vp) + b_s[:, :lc].T[None, None, :, :, None]
    return out.reshape(B, lp, D_B)[:, :L]


def layer(h, p, prev_a, prev_ffn, g_mix, w_in, w_conv_a, w_out_a, g_v, w_spatial, b_spatial,
          w_out_b, w_o, g_ffn, w_up, w_conv_ffn, w_down, g_ple, w_ple_gate, w_ple):
    L = h.shape[1]
    xn = rmsnorm(h, g_mix)
    z = xn @ w_in
    x_a, b_a, c_a, uv, ga, gb = jnp.split(
        z, [D_A, 2 * D_A, 3 * D_A, 3 * D_A + 2 * D_B, 3 * D_A + 2 * D_B + D_MODEL], axis=-1)
    a_conv, new_a = causal_dwconv(c_a * x_a, w_conv_a, prev_a)
    y_a = (b_a * a_conv) @ w_out_a
    uv = jax.nn.gelu(uv)
    u, v = jnp.split(uv, 2, axis=-1)
    v = rmsnorm(v, g_v)
    s = chunk_spatial(v, w_spatial, b_spatial)
    y_b = (u * s) @ w_out_b
    start = ((L - 1) // CHUNK) * CHUNK
    chunk_v = v[:, start:]
    m = jax.nn.sigmoid(ga) * y_a + jax.nn.sigmoid(gb) * y_b
    h = h + m @ w_o
    up = rmsnorm(h, g_ffn) @ w_up
    up_c, new_ffn = causal_dwconv(up, w_conv_ffn, prev_ffn)
    fa, fb = jnp.split(up_c, 2, axis=-1)
    h = h + (jax.nn.gelu(fa) * fb) @ w_down
    gate = jax.nn.sigmoid(rmsnorm(h, g_ple) @ w_ple_gate)
    h = h + gate * (p @ w_ple)
    return h, new_a, chunk_v, new_ffn


def setup_inputs(seed: int = 0) -> dict:
    key = jax.random.key(seed)
    ks = jax.random.split(key, 24)
    f32 = jnp.float32

    def nrm(k, shape, scale):
        return jax.random.normal(k, shape, f32) * scale

    def gain(k, shape):
        return 1.0 + 0.05 * jax.random.normal(k, shape, f32)

    return {
        "x_prompt": nrm(ks[0], (BATCH, SEQ, D_MODEL), 1.0),
        "x_sample": nrm(ks[1], (DEC_BATCH, DEC_SEQ, D_MODEL), 1.0),
        "p_prompt": nrm(ks[2], (DEPTH, BATCH, SEQ, P_DIM), 1.0),
        "p_sample": nrm(ks[3], (DEPTH, DEC_BATCH, DEC_SEQ, P_DIM), 1.0),
        "state_conv_a": nrm(ks[4], (DEPTH, DEC_BATCH, CONV_W - 1, D_A), 1.0),
        "state_conv_ffn": nrm(ks[5], (DEPTH, DEC_BATCH, CONV_W - 1, 2 * D_FF), 1.0),
        "g_mix": gain(ks[6], (DEPTH, D_MODEL)),
        "w_in": nrm(ks[7], (DEPTH, D_MODEL, N_IN), D_MODEL ** -0.5),
        "w_conv_a": nrm(ks[8], (DEPTH, CONV_W, D_A), CONV_W ** -0.5),
        "w_out_a": nrm(ks[9], (DEPTH, D_A, D_MODEL), D_A ** -0.5),
        "g_v": gain(ks[10], (DEPTH, D_B)),
        "w_spatial": nrm(ks[11], (DEPTH, G_B, CHUNK, CHUNK), CHUNK ** -0.5),
        "b_spatial": 1.0 + 0.05 * jax.random.normal(ks[12], (DEPTH, G_B, CHUNK), f32),
        "w_out_b": nrm(ks[13], (DEPTH, D_B, D_MODEL), D_B ** -0.5),
        "w_o": nrm(ks[14], (DEPTH, D_MODEL, D_MODEL), 0.5 * D_MODEL ** -0.5),
        "g_ffn": gain(ks[15], (DEPTH, D_MODEL)),
        "w_up": nrm(ks[16], (DEPTH, D_MODEL, 2 * D_FF), D_MODEL ** -0.5),
        "w_conv_ffn": nrm(ks[17], (DEPTH, CONV_W, 2 * D_FF), CONV_W ** -0.5),
        "w_down": nrm(ks[18], (DEPTH, D_FF, D_MODEL), 0.5 * D_FF ** -0.5),
        "g_ple": gain(ks[19], (DEPTH, D_MODEL)),
        "w_ple_gate": nrm(ks[20], (DEPTH, D_MODEL, D_MODEL), D_MODEL ** -0.5),
        "w_ple": nrm(ks[21], (DEPTH, P_DIM, D_MODEL), P_DIM ** -0.5),
        "g_final": gain(ks[22], (D_MODEL,)),
    }


def reference(x_prompt, x_sample, p_prompt, p_sample, state_conv_a, state_conv_ffn,
              g_mix, w_in, w_conv_a, w_out_a, g_v, w_spatial, b_spatial, w_out_b, w_o,
              g_ffn, w_up, w_conv_ffn, w_down, g_ple, w_ple_gate, w_ple, g_final):
    hp, hs = x_prompt, x_sample
    ca_p, ca_s, cv_p, cv_s, cf_p, cf_s = [], [], [], [], [], []
    for i in range(DEPTH):
        params = (g_mix[i], w_in[i], w_conv_a[i], w_out_a[i], g_v[i], w_spatial[i], b_spatial[i],
                  w_out_b[i], w_o[i], g_ffn[i], w_up[i], w_conv_ffn[i], w_down[i], g_ple[i],
                  w_ple_gate[i], w_ple[i])
        zero_a = jnp.zeros((hp.shape[0], CONV_W - 1, D_A), hp.dtype)
        zero_f = jnp.zeros((hp.shape[0], CONV_W - 1, 2 * D_FF), hp.dtype)
        hp, na, vp, nf = layer(hp, p_prompt[i], zero_a, zero_f, *params)
        ca_p.append(na); cv_p.append(vp); cf_p.append(nf)
        hs, na, vs, nf = layer(hs, p_sample[i], state_conv_a[i], state_conv_ffn[i], *params)
        ca_s.append(na); cv_s.append(vs); cf_s.append(nf)
    y_prompt = rmsnorm(hp, g_final)
    y_sample = rmsnorm(hs, g_final)
    new_conv_a_prompt = jnp.stack(ca_p)
    new_conv_a_sample = jnp.stack(ca_s)
    chunk_v_prompt = jnp.stack(cv_p)
    chunk_v_sample = jnp.stack(cv_s)
    new_conv_ffn_prompt = jnp.stack(cf_p)
    new_conv_ffn_sample = jnp.stack(cf_s)
    return (y_prompt, y_sample, new_conv_a_prompt, new_conv_a_sample, chunk_v_prompt,
            chunk_v_sample, new_conv_ffn_prompt, new_conv_ffn_sample)
```

```python
import contextlib
import numpy as np
import concourse.bass as bass
import concourse.mybir as mybir
from concourse.bass_utils import run_bass_kernel_spmd

F32 = mybir.dt.float32
BF16 = mybir.dt.bfloat16
AF = mybir.ActivationFunctionType
ALU = mybir.AluOpType

D = 1024
SEQ = 2048
NS = 16
DFF = 2816
NIN = 7168
PD = 256
EPS = 1e-6
NCORES = 8
TT = 1040


class Prog:
    ENGS = ("pe", "act", "dve", "pool", "sp")

    def __init__(self):
        self.ops = {e: [] for e in self.ENGS}
        self.ticket = {e: 0 for e in self.ENGS}
        self.pending = {e: False for e in self.ENGS}
        self.last_write = {}
        self.readers = {}
        self.known = {e: {} for e in self.ENGS}
        self.dma_count = {}

    def _deps(self, eng, reads, writes):
        deps = {}
        for r in reads:
            t = self.last_write.get(r)
            if t is not None and deps.get(t[0], 0) < t[1]:
                deps[t[0]] = t[1]
        for w in writes:
            t = self.last_write.get(w)
            if t is not None and deps.get(t[0], 0) < t[1]:
                deps[t[0]] = t[1]
            rd = self.readers.get(w)
            if rd:
                for k, v in rd.items():
                    if deps.get(k, 0) < v:
                        deps[k] = v
        waits = []
        kn = self.known[eng]
        for k, v in deps.items():
            if k == "pe" and eng == "pe":
                continue
            if kn.get(k, 0) >= v:
                continue
            kn[k] = v
            waits.append((k, v))
        return waits

    def _register(self, tok, reads, writes):
        for r in reads:
            d = self.readers.setdefault(r, {})
            if d.get(tok[0], 0) < tok[1]:
                d[tok[0]] = tok[1]
        for w in writes:
            self.last_write[w] = tok
            self.readers[w] = {}

    @staticmethod
    def _flat(seq):
        out = []
        for x in seq:
            if isinstance(x, list):
                out.extend(x)
            else:
                out.append(x)
        return out

    def op(self, eng, fn, reads=(), writes=(), signal=True):
        reads = self._flat(reads); writes = self._flat(writes)
        waits = self._deps(eng, reads, writes)
        if signal:
            self.ticket[eng] += 1
            tok = (eng, self.ticket[eng])
            self.pending[eng] = False
        else:
            tok = (eng, self.ticket[eng] + 1)
            self.pending[eng] = True
        self.ops[eng].append((fn, waits, ("eng", eng) if signal else None))
        self._register(tok, reads, writes)
        return tok

    def dma(self, eng, fn, sem, reads=(), writes=(), tok_override=None):
        reads = self._flat(reads); writes = self._flat(writes)
        waits = self._deps(eng, reads, writes)
        self.dma_count[sem] = self.dma_count.get(sem, 0) + 1
        tok = tok_override or (sem, 16 * self.dma_count[sem])
        self.ops[eng].append((fn, waits, ("dma", sem)))
        self._register(tok, reads, writes)
        return tok

    def final_wait(self, eng, toks):
        deps = {}
        for k, v in toks:
            deps[k] = max(deps.get(k, 0), v)
        self.ops[eng].append((None, list(deps.items()), None))

    def sem_names(self):
        names = set(self.ENGS)
        names.update(self.dma_count.keys())
        return sorted(names)

    def emit(self, block, sems):
        engmap = {"pe": block.tensor, "act": block.scalar, "dve": block.vector,
                  "pool": block.gpsimd, "sp": block.sync}
        for e in self.ENGS:
            ops = self.ops[e]
            if not ops:
                continue
            assert not self.pending[e], e

            def body(engine, ops=ops):
                for fn, waits, sig in ops:
                    for k, v in waits:
                        engine.wait_ge(sems[k], v)
                    if fn is None:
                        continue
                    ins = fn(engine)
                    if sig is not None:
                        if sig[0] == "eng":
                            ins.then_inc(sems[sig[1]], 1)
                        else:
                            ins.then_inc(sems[sig[1]], 16)
            engmap[e](body)


class Rot:
    def __init__(self, items):
        self.items = items
        self.i = 0

    def get(self):
        it = self.items[self.i % len(self.items)]
        self.i += 1
        return it


def bcast_mid(ap, reps):
    pairs = [list(x) for x in ap.ap]
    assert len(pairs) == 2, pairs
    return bass.AP(ap.tensor, ap.offset, [pairs[0], [0, reps], pairs[1]])


def build_program():
    nc = bass.Bass("TRN2", target_bir_lowering=False)
    P = Prog()

    def din(name, shape):
        return nc.dram_tensor(name, list(shape), F32, kind="ExternalInput").ap()

    def dout(name, shape):
        return nc.dram_tensor(name, list(shape), F32, kind="ExternalOutput").ap()

    x_d = din("x", [SEQ, D]); xs_d = din("xs", [NS, D])
    p_d = din("p", [SEQ, PD]); ps_d = din("ps", [NS, PD])
    sa_d = din("sa", [NS, 2, D]); sf_d = din("sf", [NS, 2, 2 * DFF])
    g_mix = din("g_mix", [D]); g_v = din("g_v", [D]); g_ffn = din("g_ffn", [D])
    g_ple = din("g_ple", [D]); g_final = din("g_final", [D])
    w_in = din("w_in", [D, NIN]); w_conv_a = din("w_conv_a", [3, D])
    w_out_a = din("w_out_a", [D, D]); w_out_b = din("w_out_b", [D, D]); w_o = din("w_o", [D, D])
    w_spatial = din("w_spatial", [8, 128, 128]); b_spatial = din("b_spatial", [8, 128])
    w_up = din("w_up", [D, 2 * DFF]); w_conv_ffn = din("w_conv_ffn", [3, 2 * DFF])
    w_down = din("w_down", [DFF, D]); w_ple_gate = din("w_ple_gate", [D, D]); w_ple = din("w_ple", [PD, D])
    ident_d = din("ident", [128, 128]); tril_d = din("tril", [128, 128])

    y_d = dout("y", [SEQ, D]); ys_d = dout("ys", [NS, D])
    ca_d = dout("ca", [2, D]); cas_d = dout("cas", [NS, 2, D])
    cv_d = dout("cv", [128, D]); cvs_d = dout("cvs", [NS, D])
    cf_d = dout("cf", [2, 2 * DFF]); cfs_d = dout("cfs", [NS, 2, 2 * DFF])

    es = contextlib.ExitStack()
    with es:
        def sb(name, shape, dt):
            return es.enter_context(nc.sbuf_tensor("s_" + name, list(shape), dt))

        h = sb("h", [128, 8, TT], F32)
        xn = sb("xn", [128, 8, TT], BF16)
        R1 = sb("R1", [128, 9216 + 2 * 8 * TT], BF16)
        vn3 = R1[:, 0:9216].rearrange("p (t f) -> p t f", t=9)
        m3 = R1[:, 0:8 * TT].rearrange("p (k t) -> p k t", k=8)
        ab = R1[:, 9216:9216 + 8 * TT].rearrange("p (k t) -> p k t", k=8)
        us = R1[:, 9216 + 8 * TT:9216 + 16 * TT].rearrange("p (k t) -> p k t", k=8)
        act = R1[:, 0:22 * TT].rearrange("p (k t) -> p k t", k=22)
        R1f = R1[:, :].bitcast(F32)
        xst = R1f[:, 0:8192].rearrange("p (t f) -> p t f", t=8)
        pst_all = R1f[:, 8192:10240].rearrange("p (t f) -> p t f", t=8)
        xss = R1f[0:NS, 10240:11264]
        pss_all = R1f[0:NS, 11264:11520]
        pT = sb("pT", [128, 2, TT], BF16)
        ring = sb("ring", [128, 8, 2048], BF16)
        rs = sb("rs", [128, TT], F32)
        idf = sb("idf", [128, 128], F32); idb = sb("idb", [128, 128], BF16)
        tril = sb("tril", [128, 128], F32)
        ones = sb("ones", [128, 128], BF16)
        WsT = sb("WsT", [128, 8, 128], BF16)
        Ds = sb("Ds", [16, 8, 16], BF16)
        w00 = sb("w00", [16, 8], F32)
        bbcw = sb("bbcw", [128, 1032], F32)
        bbc = bbcw[:, 0:1024].rearrange("p (j t) -> p j t", j=8)
        gvbc = sb("gvbc", [128, 1024], F32)
        gfbc = sb("gfbc", [128, 1024], F32)
        rtok = sb("rtok", [128, 16], F32)
        cols1 = sb("cols1", [128, 64], F32)
        cols2 = sb("cols2", [128, 128], F32)
        wcfT = sb("wcfT", [128, 132], F32)
        carry_a = sb("carry_a", [128, 2, 8], F32)
        carry_f = sb("carry_f", [128, 2, 44], F32)
        saT = sb("saT", [128, NS, 2, 8], F32)
        sfT = sb("sfT", [128, NS, 2, 44], F32)
        oAs = sb("oAs", [128, NS, 2, 8], F32)
        oFs = sb("oFs", [128, NS, 2, 44], F32)
        ssv = sb("ssv", [128, 16], F32)
        rv = sb("rv", [128, 16], F32)
        big = [sb(f"big{i}", [128, 1024], F32) for i in range(4)]
        pbf = [sb(f"pbf{i}", [128, 256], BF16) for i in range(2)]
        sqall = sb("sqall", [128, 2064], BF16)
        sqb = [sqall[:, i * 512:(i + 1) * 512] for i in range(4)]
        sqwide = sqall[:, :].bitcast(F32)
        NTMP = 8
        tmpall = sb("tmpall", [128, NTMP * 516], F32)
        tmpf = [tmpall[:, i * 516:(i + 1) * 516] for i in range(NTMP)]
        widef = [tmpall[:, 2 * j * 516: 2 * j * 516 + 1032] for j in range(NTMP // 2)]
        ps = [es.enter_context(nc.psum_tensor(f"pq{i}", [128, 512], F32)) for i in range(8)]

        bigR = Rot([(big[i], ("big", i)) for i in range(4)])
        pinR = Rot([(None, pbf[i], ("pin", i), ("pbf", i)) for i in range(2)])
        sqR = Rot([(sqb[i], ("sq", i)) for i in range(4)])
        tmpR = Rot([(tmpf[i], [("tmp", i, "a"), ("tmp", i, "b")]) for i in range(NTMP)])
        wideR = Rot([(widef[j], [("tmp", 2 * j, "a"), ("tmp", 2 * j, "b"), ("tmp", 2 * j + 1, "a"), ("tmp", 2 * j + 1, "b")])
                     for j in range(NTMP // 2)] + [(bbcw, ["bbc"]), (sqwide, [("sq", i) for i in range(4)])])
        class PsRot:
            def __init__(self):
                self.i = 0
                self.reserved = set()

            def get(self):
                while True:
                    b = self.i % 8
                    self.i += 1
                    if b not in self.reserved:
                        return ps[b], ("ps", b)

            def reserve(self):
                t, r = self.get()
                self.reserved.add(r[1])
                return t, r

            def release(self, r):
                self.reserved.discard(r[1])

        psR = PsRot()
        scr = sb("scr", [128, 2], F32)
        epsT = sb("epsT", [128, 1], F32)

        def pstag(i):
            return ("ps", i)

        def gcol(gi, k):
            return cols1[:, gi * 8 + k: gi * 8 + k + 1]

        def wca(tap, j):
            return cols1[:, 32 + tap * 8 + j: 32 + tap * 8 + j + 1]

        def wcf(tap, c):
            r = tap * 44 + c
            if r < 128:
                return cols2[:, r:r + 1]
            return cols1[:, 56 + r - 128: 56 + r - 128 + 1]

        def stage_inputs(pss):
            r0 = pss * 1024
            P.dma("sp", (lambda e: e.dma_start(out=pst_all, in_=p_d[r0:r0 + 1024, :].rearrange("(t q) f -> q t f", q=128))),
                  "xp", reads=["R1busy"], writes=[("pst",)])
            for hf in range(4):
                P.dma("sp", (lambda e, hf=hf: e.dma_start(
                    out=xst[:, hf * 2:(hf + 1) * 2, :],
                    in_=x_d[r0 + hf * 256: r0 + (hf + 1) * 256, :].rearrange("(t q) f -> q t f", q=128))),
                    f"xs{hf}", reads=["R1busy"], writes=[("xst", hf)])
            if pss == 1:
                P.dma("sp", (lambda e: e.dma_start(out=xss, in_=xs_d[:, :])), "xq0", reads=["R1busy"], writes=[("xss",)])
                P.dma("sp", (lambda e: e.dma_start(out=pss_all, in_=ps_d[:, :])), "xq1", reads=["R1busy"], writes=[("pss",)])

        outs_tok = []
        stg0, stg0r = big[0], ("big", 0)
        stg1, stg1r = big[1], ("big", 1)
        setup = []

        def sdma(eng, out, in_, writes, **kw):
            setup.append((eng, out, in_, writes, kw))

        sdma("sp", idf[:], ident_d[:, :], ["idf"])
        sdma("sp", tril[:], tril_d[:, :], ["tril"])
        sdma("sp", bbcw[:, 0:1024], b_spatial.rearrange("j t -> (j t)").partition_broadcast(128), ["bbc"])
        sdma("sp", gvbc[:], g_v.partition_broadcast(128), ["gvbc"])
        sdma("sp", gfbc[:], g_final.partition_broadcast(128), ["gfbc"])
        sdma("sp", stg0[0:8, 0:128], g_mix.rearrange("(j p) -> j p", p=128), [stg0r])
        sdma("sp", stg0[8:16, 0:128], g_ffn.rearrange("(j p) -> j p", p=128), [stg0r])
        sdma("sp", stg0[16:24, 0:128], g_ple.rearrange("(j p) -> j p", p=128), [stg0r])
        sdma("sp", stg0[24:32, 0:128], g_final.rearrange("(j p) -> j p", p=128), [stg0r])
        sdma("sp", stg0[32:56, 0:128], w_conv_a.rearrange("k (j p) -> (k j) p", p=128), [stg0r])
        wcf_rows = w_conv_ffn.rearrange("k (c p) -> (k c) p", p=128)
        sdma("sp", stg0[56:60, 0:128], wcf_rows[128:132, :], [stg0r])
        sdma("sp", stg0[:, 128:256], wcf_rows[0:128, :], [stg0r])
        sdma("sp", stg1[:, :].rearrange("p (j s) -> p j s", j=8), w_spatial.rearrange("j t s -> t j s"), [stg1r])
        sdma("sp", w00[:], w_spatial.rearrange("j t s -> j (t s)")[:, 0].partition_broadcast(16), ["w00"],
             allow_slow_non_contiguous=True)
        early = [x for x in setup if x[3] in (["idf"], [stg0r])]
        late = [x for x in setup if x not in early]
        for grp, sem in ((early, "setupA"), (None, None), (late, "setupB")):
            if grp is None:
                stage_inputs(0)
                continue
            stok = (sem, 16 * len(grp))
            for eng, out, in_, writes, kw in grp:
                P.dma(eng, (lambda e, out=out, in_=in_, kw=kw: e.dma_start(out=out, in_=in_, **kw)), sem,
                      tok_override=stok)
            for eng, out, in_, writes, kw in grp:
                P._register(stok, (), writes)

        P.op("dve", lambda e: e.tensor_copy(out=idb[:], in_=idf[:]), reads=["idf"], writes=["idb"])
        P.op("pool", lambda e: e.memset(ones[:], 1.0 / 1024.0), writes=["ones"])
        P.op("pool", lambda e: e.memset(scr[:], 1.0), writes=["scr"])
        P.op("pool", lambda e: e.memset(epsT[:], EPS), writes=["epsT"])
        P.op("pool", lambda e: e.memset(carry_a[:], 0.0), writes=["carry_a"])
        P.op("pool", lambda e: e.memset(carry_f[:], 0.0), writes=["carry_f"])
        pa, par = psR.get()
        P.op("pe", lambda e: e.transpose(out=pa[:, 0:60], in_=stg0[0:60, 0:128], identity=idf[0:60, 0:60]),
             reads=[stg0r, "idf"], writes=[par])
        P.op("dve", lambda e: e.tensor_copy(out=cols1[:, 0:60], in_=pa[:, 0:60]), reads=[par], writes=["cols"])
        pa2, par2 = psR.get()
        P.op("pe", lambda e: e.transpose(out=pa2[:, 0:128], in_=stg0[:, 128:256], identity=idf[:]),
             reads=[stg0r, "idf"], writes=[par2])
        P.op("dve", lambda e: e.tensor_copy(out=cols2[:], in_=pa2[:, 0:128]), reads=[par2], writes=["cols"])
        P.op("dve", lambda e: e.tensor_copy(out=wcfT[:, 0:128], in_=cols2[:]), reads=["cols"], writes=["wcfT"])
        P.op("dve", lambda e: e.tensor_copy(out=wcfT[:, 128:132], in_=cols1[:, 56:60]), reads=["cols"], writes=["wcfT"])
        def setup_spatial():
            for j in range(8):
                P.op("dve", lambda e, j=j: e.tensor_tensor(out=stg1[:, j * 128:(j + 1) * 128], in0=stg1[:, j * 128:(j + 1) * 128],
                                                            in1=tril[:], op=ALU.mult),
                     reads=[stg1r, "tril"], writes=[stg1r])
            for jj in range(2):
                pw, pwr = psR.get()
                for q in range(4):
                    j = jj * 4 + q
                    P.op("pe", lambda e, j=j, q=q, pw=pw: e.transpose(out=pw[:, q * 128:(q + 1) * 128],
                                                                        in_=stg1[:, j * 128:(j + 1) * 128], identity=idf[:]),
                         reads=[stg1r, "idf"], writes=[pwr], signal=(q == 3))
                P.op("act", lambda e, jj=jj, pw=pw: e.copy(out=WsT[:, jj * 4:(jj + 1) * 4, :].rearrange("p j t -> p (j t)"), in_=pw[:, :]),
                     reads=[pwr], writes=["WsT"])
            for j in range(8):
                P.op("dve", lambda e, j=j: e.tensor_scalar_mul(out=Ds[:, j, :], in0=idf[0:16, 0:16], scalar1=w00[:, j:j + 1]),
                     reads=["idf", "w00"], writes=["Ds"])

        def load_rounds(src_rows_ap, nrow_tiles, dst, dst_res, eng_sem):
            rounds = []
            for a0 in range(0, nrow_tiles, 4):
                na = min(4, nrow_tiles - a0)
                st = {}

                def issue(a0=a0, na=na, st=st):
                    bt, btr = bigR.get()
                    st["bt"] = (bt, btr)
                    P.dma("sp", (lambda e: e.dma_start(
                        out=bt[:, 0:na * 128].rearrange("q (a p) -> q a p", a=na),
                        in_=src_rows_ap[a0 * 128:(a0 + na) * 128, :].rearrange("(a q) p -> q a p", q=128))),
                        eng_sem + str(btr[1]), writes=[btr])

                def finish(a0=a0, na=na, st=st):
                    bt, btr = st["bt"]
                    pz, pzr = psR.get()
                    for a in range(na):
                        P.op("pe", lambda e, a=a: e.transpose(out=pz[:, a * 128:(a + 1) * 128],
                                                                in_=bt[:, a * 128:(a + 1) * 128], identity=idf[:]),
                             reads=[btr, "idf"], writes=[pzr], signal=(a == na - 1))
                    P.op("dve", lambda e: e.tensor_copy(out=dst[:, a0 * 128:(a0 + na) * 128], in_=pz[:, 0:na * 128]),
                         reads=[pzr], writes=[dst_res])
                rounds.append((issue, finish))
            return rounds

        saT_flat = saT[:].rearrange("p s r j -> p (s r j)")
        sfT_flat = sfT[:].rearrange("p s r c -> p (s r c)")

        def slab_items():
            items = []

            def std(w, c0):
                items.append(("std", w, c0))
            for q in range(4):
                std(w_in, 4096 + 256 * q)
            for jp in range(4):
                std(w_in, jp * 256); std(w_in, 1024 + jp * 256); std(w_in, 2048 + jp * 256)
            for jp in range(4):
                std(w_in, 3072 + jp * 256)
            for ip in range(4):
                std(w_in, 5120 + ip * 256); std(w_in, 6144 + ip * 256)
                std(w_out_a, ip * 256); std(w_out_b, ip * 256)
            for ip in range(4):
                std(w_o, ip * 256)
            for cp in range(11):
                std(w_up, cp * 256); std(w_up, DFF + cp * 256)
            for i in range(8):
                items.append(("down", w_down, i * 128))
            items.append(("ple", w_ple, 0))
            for ip in range(4):
                std(w_ple_gate, ip * 256)
            return items

        per_pass = slab_items()
        all_items = per_pass + per_pass
        NIT = len(all_items)

        class Ring:
            def __init__(self):
                self.next_load = 0
                self.slot_ptr = 0
                self.occupant = [None] * 8
                self.released = set()
                self.item_slot = {}

            def _try_load(self):
                i = self.next_load
                if i >= NIT:
                    return False
                kind, w, c0 = all_items[i]
                ns = 2 if kind == "down" else 1
                s = self.slot_ptr
                if ns == 2 and s % 2 == 1:
                    s = (s + 1) % 8
                for q in range(ns):
                    occ = self.occupant[(s + q) % 8]
                    if occ is not None and occ not in self.released:
                        return False
                if kind == "std":
                    dst = ring[:, s, :].rearrange("p (k n) -> p k n", k=8)
                    src = w[:, c0:c0 + 256].rearrange("(k p) n -> p k n", p=128)
                elif kind == "down":
                    dst = ring[:, s:s + 2, :].rearrange("p a b -> p (a b)")[:, 0:2816].rearrange("p (c n) -> p c n", c=22)
                    src = w[:, c0:c0 + 128].rearrange("(c p) n -> p c n", p=128)
                else:
                    dst = ring[:, s, :].rearrange("p (k n) -> p k n", k=2)
                    src = w[:, :].rearrange("(k p) n -> p k n", p=128)
                res = [("ring", (s + q) % 8) for q in range(ns)]
                xtra = [("xst", 0), ("xst", 1), ("xst", 2), ("xst", 3), ("pst",)] if i == 0 else []
                P.dma("pool", (lambda e, dst=dst, src=src: e.dma_start(out=dst, in_=src)), f"wr{s}", reads=xtra, writes=res)
                for q in range(ns):
                    self.occupant[(s + q) % 8] = i
                self.item_slot[i] = (s, dst, res)
                self.slot_ptr = (s + ns) % 8
                self.next_load += 1
                return True

            def prefetch(self):
                while self._try_load():
                    pass

            def get(self, i):
                self.prefetch()
                assert i in self.item_slot, (i, self.next_load)
                return self.item_slot[i]

            def release(self, i):
                self.released.add(i)
                self.prefetch()

        R = Ring()
        item_ctr = [0]

        def next_item():
            i = item_ctr[0]
            item_ctr[0] += 1
            s, dst, res = R.get(i)
            return i, dst, res

        class NormAcc:
            def __init__(self, blocks):
                self.blocks = blocks
                self.bank = {}
                self.count = {}
                self.pend_sq = []
                self.pend_mm = []
                for b in blocks:
                    self.count[b["idx"]] = 0

            def add(self, k, b):
                while len(self.pend_mm) >= 2:
                    self.pend_mm.pop(0)[1]()
                if self.pend_sq:
                    self.pend_sq.pop(0)[1]()
                c0, n, bi = b["c0"], b["n"], b["idx"]
                if bi not in self.bank:
                    self.bank[bi] = psR.reserve()
                pst, pstr = self.bank[bi]
                cnt = self.count[bi]
                self.count[bi] += 1

                def emit_sq():
                    sq, sqr = sqR.get()
                    P.op("act", lambda e: e.activation(out=sq[:, 0:n], in_=h[:, k, c0:c0 + n], func=AF.Square),
                         reads=[("h", k, bi)], writes=[sqr])
                    self.pend_mm.append((bi, lambda: P.op(
                        "pe", lambda e: e.matmul(pst[:, 0:n], lhsT=ones[:], rhs=sq[:, 0:n], start=(cnt == 0), stop=(cnt == 7)),
                        reads=[sqr, "ones"], writes=[pstr], signal=True)))
                self.pend_sq.append((bi, emit_sq))

            def flush(self, bi):
                mine = [fn for tag, fn in self.pend_sq if tag == bi]
                self.pend_sq = [(tag, fn) for tag, fn in self.pend_sq if tag != bi]
                for fn in mine:
                    fn()
                mine = [fn for tag, fn in self.pend_mm if tag == bi]
                self.pend_mm = [(tag, fn) for tag, fn in self.pend_mm if tag != bi]
                for fn in mine:
                    fn()

            def finish_block(self, b, gi, final=False):
                first = (b["idx"] == self.blocks[0]["idx"])
                last = (b["idx"] == self.blocks[-1]["idx"])
                if first:
                    P.op("act", lambda e: e.activation(out=scr[:, 1:2], in_=scr[:, 0:1], func=AF.Ln), reads=["scr"], writes=["scr1"])
                self.flush(b["idx"])
                c0, n, bi = b["c0"], b["n"], b["idx"]
                assert self.count[bi] == 8
                pst, pstr = self.bank[bi]
                P.op("act", lambda e: e.activation(out=rs[:, c0:c0 + n], in_=pst[:, 0:n], func=AF.Ln, bias=epsT[:, 0:1]),
                     reads=[pstr, "epsT"], writes=[("rs", bi)])
                psR.release(pstr)
                P.op("act", lambda e: e.activation(out=rs[:, c0:c0 + n], in_=rs[:, c0:c0 + n], func=AF.Exp, scale=-0.5),
                     reads=[("rs", bi)], writes=[("rs", bi)])
                if last:
                    P.op("act", lambda e: e.activation(out=scr[:, 1:2], in_=scr[:, 0:1], func=AF.Tanh), reads=["scr"], writes=["scr1"])
                if final:
                    return
                for k in range(8):
                    P.op("dve", lambda e, k=k: e.scalar_tensor_tensor(
                        out=xn[:, k, c0:c0 + n], in0=h[:, k, c0:c0 + n], scalar=gcol(gi, k), in1=rs[:, c0:c0 + n],
                        op0=ALU.mult, op1=ALU.mult),
                        reads=[("h", k, bi), ("rs", bi), "cols"], writes=[("xn", k, bi)])

        def proj(pt, ptr, wv, wres, u, src, src_name, b, nk=8):
            c0, n = b["c0"], b["n"]
            for k in range(nk):
                P.op("pe", lambda e, k=k: e.matmul(pt[:, 0:n], lhsT=wv[:, k, u * 128:(u + 1) * 128], rhs=src[:, k, c0:c0 + n],
                                                   start=(k == 0), stop=(k == nk - 1)),
                     reads=wres + [(src_name, k, b["idx"])], writes=[ptr], signal=(k == nk - 1))

        def store_colsT(src_flat, ncol, src_reads, dst_rows_ap):
            for a0 in range(0, ncol, 512):
                w = min(512, ncol - a0)
                ob, obr = bigR.get()
                pz, pzr = psR.get()
                nt = (w + 127) // 128
                for a in range(nt):
                    wa_ = min(128, w - a * 128)
                    P.op("pe", lambda e, a=a, wa_=wa_, a0=a0, pz=pz: e.transpose(
                        out=pz[0:wa_, a * 128:(a + 1) * 128], in_=src_flat[:, a0 + a * 128: a0 + a * 128 + wa_], identity=idf[:]),
                        reads=src_reads + ["idf"], writes=[pzr], signal=(a == nt - 1))
                full = w // 128
                if full:
                    P.op("dve", lambda e, pz=pz, ob=ob, full=full: e.tensor_copy(out=ob[:, 0:full * 128], in_=pz[:, 0:full * 128]),
                         reads=[pzr], writes=[obr])
                    outs_tok.append(P.dma("sp", (lambda e, ob=ob, a0=a0, full=full: e.dma_start(
                        out=dst_rows_ap[a0:a0 + full * 128, :].rearrange("(a q) p -> q a p", q=128),
                        in_=ob[:, 0:full * 128].rearrange("q (a p) -> q a p", a=full))), "st" + str(obr[1]), reads=[obr]))
                rem = w - full * 128
                if rem:
                    P.op("dve", lambda e, pz=pz, ob=ob, full=full, rem=rem: e.tensor_copy(
                        out=ob[0:rem, 512:640], in_=pz[0:rem, full * 128:(full + 1) * 128]), reads=[pzr], writes=[obr])
                    outs_tok.append(P.dma("sp", (lambda e, ob=ob, a0=a0, full=full, rem=rem: e.dma_start(
                        out=dst_rows_ap[a0 + full * 128: a0 + full * 128 + rem, :], in_=ob[0:rem, 512:640])),
                        "st" + str(obr[1]), reads=[obr]))

        for pss in range(2):
            if pss == 0:
                blocks = [dict(idx=0, c0=0, n=512, kind="p", first=True, last=False),
                          dict(idx=1, c0=512, n=512, kind="p", first=False, last=False)]
                ncols = 1024
            else:
                blocks = [dict(idx=0, c0=0, n=512, kind="p", first=False, last=False),
                          dict(idx=1, c0=512, n=512, kind="p", first=False, last=True),
                          dict(idx=2, c0=1024, n=NS, kind="s", first=False, last=False)]
                ncols = TT
            tiles = [dict(t=t, rows=128, c0=t * 128, blk=t // 4, src=x_d[pss * 1024 + t * 128: pss * 1024 + (t + 1) * 128, :],
                          psrc=p_d[pss * 1024 + t * 128: pss * 1024 + (t + 1) * 128, :]) for t in range(8)]
            if pss == 1:
                tiles.append(dict(t=8, rows=NS, c0=1024, blk=2, src=xs_d[:, :], psrc=ps_d[:, :]))

            STAGE_MARKS.append(("A", pss, len(P.ops["pe"])))
            if pss == 0:
                R.prefetch()
            nacc = NormAcc(blocks)
            pend_fin = []
            def p_cast(tl):
                if tl["t"] < 8:
                    pi, pir = pst_all[:, tl["t"], :], ("pst",)
                else:
                    pi, pir = pss_all, ("pss",)
                _, pb, _, pbr = pinR.get()
                rows = tl["rows"]
                P.op("dve", lambda e: e.tensor_copy(out=pb[0:rows, :], in_=pi[0:rows, :]), reads=[pir], writes=[pbr])
                return pb, pbr

            nxt_cast = p_cast(tiles[0])
            for ti, tl in enumerate(tiles):
                rows, c0, blk = tl["rows"], tl["c0"], tl["blk"]
                if tl["t"] < 8:
                    xt, xtr = xst[:, tl["t"], :], ("xst", tl["t"] // 2)
                else:
                    xt, xtr = xss, ("xss",)
                pb, pbr = nxt_cast
                if ti + 1 < len(tiles):
                    nxt_cast = p_cast(tiles[ti + 1])
                pz, pzr = psR.get()
                pzb = pz[:, :].bitcast(BF16)
                for kk in range(2):
                    P.op("pe", lambda e, kk=kk, pb=pb, pzb=pzb, rows=rows: e.transpose(
                        out=pzb[:, kk * 128: kk * 128 + rows], in_=pb[0:rows, kk * 128:(kk + 1) * 128], identity=idb[0:rows, 0:rows]),
                        reads=[pbr, "idb"], writes=[pzr], signal=(kk == 1))
                P.op("act", lambda e, pzb=pzb, c0=c0, rows=rows: e.copy(
                    out=pT[:, :, c0:c0 + rows], in_=pzb[:, 0:256].rearrange("p (k t) -> p k t", k=2)[:, :, 0:rows]),
                    reads=[pzr], writes=[("pT", blk)])
                for hf in range(2):
                    pz, pzr = psR.get()
                    for kk in range(4):
                        k = hf * 4 + kk
                        P.op("pe", lambda e, k=k, kk=kk, xt=xt, pz=pz, rows=rows: e.transpose(
                            out=pz[:, kk * 128: kk * 128 + rows], in_=xt[0:rows, k * 128:(k + 1) * 128], identity=idf[0:rows, 0:rows]),
                            reads=[xtr, "idf"], writes=[pzr], signal=(kk == 3))
                    eng = "act" if hf == 0 else "dve"
                    if eng == "act":
                        P.op("act", lambda e, hf=hf, pz=pz, c0=c0, rows=rows: e.copy(
                            out=h[:, hf * 4:(hf + 1) * 4, c0:c0 + rows], in_=pz[:, :].rearrange("p (k t) -> p k t", k=4)[:, :, 0:rows]),
                            reads=[pzr], writes=[("h", k, blk) for k in range(hf * 4, hf * 4 + 4)])
                    else:
                        P.op("dve", lambda e, hf=hf, pz=pz, c0=c0, rows=rows: e.tensor_copy(
                            out=h[:, hf * 4:(hf + 1) * 4, c0:c0 + rows], in_=pz[:, :].rearrange("p (k t) -> p k t", k=4)[:, :, 0:rows]),
                            reads=[pzr], writes=[("h", k, blk) for k in range(hf * 4, hf * 4 + 4)])
                if tl["t"] % 4 == 3 or tl["t"] == 8:
                    for k in range(8):
                        nacc.add(k, blocks[blk])
                    pend_fin.append(blocks[blk])
                if tl["t"] % 4 == 1 and pend_fin:
                    nacc.finish_block(pend_fin.pop(0), 0)
            while pend_fin:
                nacc.finish_block(pend_fin.pop(0), 0)

            STAGE_MARKS.append(("B", pss, len(P.ops["pe"])))
            P.op("pool", lambda e: e.memset(ssv[:], 0.0), reads=["rv"], writes=["ssv"])
            vit = [next_item() for _ in range(4)]
            for tl in tiles:
                t, rows, c0, blk = tl["t"], tl["rows"], tl["c0"], tl["blk"]
                for hf in range(2):
                    pz, pzr = psR.get()
                    for q2 in range(2):
                        q = hf * 2 + q2
                        _, wv, wres = vit[q]
                        for k in range(8):
                            P.op("pe", lambda e, k=k, q2=q2, wv=wv, pz=pz, c0=c0, rows=rows: e.matmul(
                                pz[0:rows, q2 * 256:(q2 + 1) * 256], lhsT=xn[:, k, c0:c0 + rows], rhs=wv[:, k, :],
                                start=(k == 0), stop=(k == 7)),
                                reads=wres + [("xn", k, blk)], writes=[pzr], signal=(k == 7 and q2 == 1))
                    P.op("act", lambda e, hf=hf, pz=pz, t=t, rows=rows: e.activation(
                        out=vn3[0:rows, t, hf * 512:(hf + 1) * 512], in_=pz[0:rows, :], func=AF.Gelu_apprx_tanh),
                        reads=[pzr], writes=[("vn", t)])
                jk, jkr = tmpR.get()
                jkb = jk[:, 0:512].bitcast(BF16)
                P.op("act", lambda e, t=t, rows=rows, jkb=jkb: e.activation(out=jkb[0:rows, :], in_=vn3[0:rows, t, :], func=AF.Square,
                                                                               scale=1.0 / 32.0, accum_out=ssv[0:rows, t:t + 1]),
                     reads=[("vn", t)], writes=[jkr, "ssv"])
            for (i, _, _) in vit:
                R.release(i)
            ntl = len(tiles)
            P.op("act", lambda e, ntl=ntl: e.activation(out=rv[:, 0:ntl], in_=ssv[:, 0:ntl], func=AF.Ln, bias=epsT[:, 0:1]), reads=["ssv", "epsT"], writes=["rv"])
            P.op("act", lambda e, ntl=ntl: e.activation(out=rv[:, 0:ntl], in_=rv[:, 0:ntl], func=AF.Exp, scale=-0.5), reads=["rv"], writes=["rv"])
            P.op("act", lambda e: e.activation(out=scr[:, 1:2], in_=scr[:, 0:1], func=AF.Tanh), reads=["scr"], writes=["scr1"])
            vn_jobs = []
            for tl in tiles:
                def vn_job(tl=tl):
                    t, rows = tl["t"], tl["rows"]
                    is_out = (pss == 1 and t >= 7)
                    if is_out:
                        ob, obr = bigR.get()
                        P.op("dve", lambda e: e.scalar_tensor_tensor(
                            out=ob[0:rows, :], in0=vn3[0:rows, t, :], scalar=rv[0:rows, t:t + 1], in1=gvbc[0:rows, :],
                            op0=ALU.mult, op1=ALU.mult), reads=[("vn", t), "rv", "gvbc"], writes=[obr])
                        dst = cv_d[:, :] if t == 7 else cvs_d[:, :]
                        outs_tok.append(P.dma("sp", (lambda e: e.dma_start(out=dst, in_=ob[0:rows, :])),
                                              "st" + str(obr[1]), reads=[obr]))
                    P.op("dve", lambda e: e.scalar_tensor_tensor(
                        out=vn3[0:rows, t, :], in0=vn3[0:rows, t, :], scalar=rv[0:rows, t:t + 1], in1=gvbc[0:rows, :],
                        op0=ALU.mult, op1=ALU.mult), reads=[("vn", t), "rv", "gvbc"], writes=[("vn", t)])
                vn_jobs.append(vn_job)

            if pss == 0:
                setup_spatial()
            STAGE_MARKS.append(("C", pss, len(P.ops["pe"])))
            if pss == 1:
                xcbs = R1f[:, 12544:12928]
            st_rounds = []
            if pss == 0:
                st_rounds = (load_rounds(sa_d.rearrange("s r (j p) -> (s r j) p", p=128), 2, saT_flat, "saT", "ld") +
                             load_rounds(sf_d.rearrange("s r (c p) -> (s r c) p", p=128), 11, sfT_flat, "sfT", "ld"))
                assert len(st_rounds) == 4
                st_rounds[0][0]()
            for jp in range(4):
                if st_rounds:
                    st_rounds[jp][1]()
                    if jp + 1 < 4:
                        st_rounds[jp + 1][0]()
                ix, wx, wxr = next_item()
                ib, wb, wbr = next_item()
                ic, wc, wcr = next_item()
                for u in range(2):
                    j = jp * 2 + u
                    for b in blocks:
                        c0, n, bi = b["c0"], b["n"], b["idx"]
                        if b["kind"] == "s":
                            pz, pzr = psR.get()
                            proj(pz[:, 0:NS], pzr, wx, wxr, u, xn, "xn", b)
                            proj(pz[:, NS:2 * NS], pzr, wc, wcr, u, xn, "xn", b)
                            proj(pz[:, 2 * NS:3 * NS], pzr, wb, wbr, u, xn, "xn", b)
                            P.op("act", lambda e, pz=pz, j=j: e.copy(out=xcbs[:, j * 48:(j + 1) * 48], in_=pz[:, 0:48]),
                                 reads=[pzr], writes=[("xcbs", j)])
                            continue
                        pX, pXr = psR.get(); pB, pBr = psR.get(); pC, pCr = psR.get()
                        proj(pX, pXr, wx, wxr, u, xn, "xn", b)
                        proj(pC, pCr, wc, wcr, u, xn, "xn", b)
                        proj(pB, pBr, wb, wbr, u, xn, "xn", b)
                        tx, txr = tmpR.get(); cx, cxr = tmpR.get(); acc, accr = tmpR.get()
                        P.op("act", lambda e, tx=tx, pX=pX, n=n: e.copy(out=tx[:, 0:n], in_=pX[:, 0:n]), reads=[pXr], writes=[txr])
                        if b["kind"] == "p":
                            if b["first"]:
                                P.op("pool", lambda e, cx=cx: e.memset(cx[:, 0:2], 0.0), writes=[cxr[0]])
                            else:
                                P.op("pool", lambda e, cx=cx, j=j: e.tensor_copy(out=cx[:, 0:2], in_=carry_a[:, :, j]),
                                     reads=[("carry_a", j)], writes=[cxr[0]])
                            P.op("dve", lambda e, cx=cx, pC=pC, tx=tx, n=n: e.tensor_tensor(out=cx[:, 2:2 + n], in0=pC[:, 0:n], in1=tx[:, 0:n], op=ALU.mult),
                                 reads=[pCr, txr], writes=[cxr[1]])
                            P.op("pool", lambda e, cx=cx, j=j, n=n: e.tensor_copy(out=carry_a[:, :, j], in_=cx[:, n:n + 2]),
                                 reads=[cxr[1]], writes=[("carry_a", j)])
                            P.op("act", lambda e, acc=acc, cx=cx, j=j, n=n: e.activation(out=acc[:, 0:n], in_=cx[:, 0:n], func=AF.Identity, scale=wca(0, j)),
                                 reads=[cxr, "cols"], writes=[accr])
                            P.op("dve", lambda e, acc=acc, cx=cx, j=j, n=n: e.scalar_tensor_tensor(
                                out=acc[:, 0:n], in0=cx[:, 1:1 + n], scalar=wca(1, j), in1=acc[:, 0:n], op0=ALU.mult, op1=ALU.add),
                                reads=[cxr, accr, "cols"], writes=[accr])
                            P.op("dve", lambda e, acc=acc, cx=cx, j=j, n=n: e.scalar_tensor_tensor(
                                out=acc[:, 0:n], in0=cx[:, 2:2 + n], scalar=wca(2, j), in1=acc[:, 0:n], op0=ALU.mult, op1=ALU.add),
                                reads=[cxr[1], accr, "cols"], writes=[accr])
                        else:
                            P.op("dve", lambda e, cx=cx, pC=pC, tx=tx, n=n: e.tensor_tensor(out=cx[:, 0:n], in0=pC[:, 0:n], in1=tx[:, 0:n], op=ALU.mult),
                                 reads=[pCr, txr], writes=[cxr])
                            P.op("pool", lambda e, j=j, cx=cx, n=n: e.tensor_copy(out=oAs[:, :, 1, j], in_=cx[:, 0:n]), reads=[cxr], writes=[("oAs", j)])
                            P.op("act", lambda e, acc=acc, j=j, n=n: e.activation(out=acc[:, 0:n], in_=saT[:, :, 0, j], func=AF.Identity, scale=wca(0, j)),
                                 reads=["saT", "cols"], writes=[accr])
                            P.op("dve", lambda e, acc=acc, j=j, n=n: e.scalar_tensor_tensor(
                                out=acc[:, 0:n], in0=saT[:, :, 1, j], scalar=wca(1, j), in1=acc[:, 0:n], op0=ALU.mult, op1=ALU.add),
                                reads=["saT", accr, "cols"], writes=[accr])
                            P.op("dve", lambda e, acc=acc, cx=cx, j=j, n=n: e.scalar_tensor_tensor(
                                out=acc[:, 0:n], in0=cx[:, 0:n], scalar=wca(2, j), in1=acc[:, 0:n], op0=ALU.mult, op1=ALU.add),
                                reads=[cxr, accr, "cols"], writes=[accr])
                        P.op("dve", lambda e, acc=acc, pB=pB, j=j, c0=c0, n=n: e.tensor_tensor(out=ab[:, j, c0:c0 + n], in0=pB[:, 0:n], in1=acc[:, 0:n], op=ALU.mult),
                             reads=[pBr, accr], writes=[("ab", j, bi)])
                        if vn_jobs:
                            vn_jobs.pop(0)()
                R.release(ix); R.release(ib); R.release(ic)
            while vn_jobs:
                vn_jobs.pop(0)()
            if pss == 1:
                sblk = blocks[2]
                sc0 = sblk["c0"]
                x4 = xcbs[:, 0:8 * 48].rearrange("p (j a s) -> p j a s", j=8, a=3)
                xres = [("xcbs", j) for j in range(8)]
                ores = [("oAs", j) for j in range(8)]
                cxv = oAs[:, :, 1, :].rearrange("p s j -> p j s")
                acc, accr = tmpR.get(); t1, t1r = tmpR.get()
                a3 = acc[:, 0:8 * NS].rearrange("p (j s) -> p j s", j=8)
                t3 = t1[:, 0:8 * NS].rearrange("p (j s) -> p j s", j=8)

                def wav(tap):
                    w = cols1[:, 32 + tap * 8: 32 + tap * 8 + 8]
                    pr = [list(x) for x in w.ap]
                    return bass.AP(w.tensor, w.offset, [pr[0], pr[1], [0, NS]])
                P.op("dve", lambda e, cxv=cxv, x4=x4, a3=a3, t3=t3, wav=wav, sc0=sc0: e.tensor_tensor(out=cxv, in0=x4[:, :, 1, :], in1=x4[:, :, 0, :], op=ALU.mult),
                     reads=xres, writes=ores)
                P.op("dve", lambda e, cxv=cxv, x4=x4, a3=a3, t3=t3, wav=wav, sc0=sc0: e.tensor_tensor(out=a3, in0=saT[:, :, 0, :].rearrange("p s j -> p j s"), in1=wav(0), op=ALU.mult),
                     reads=["saT", "cols"], writes=[accr])
                P.op("dve", lambda e, cxv=cxv, x4=x4, a3=a3, t3=t3, wav=wav, sc0=sc0: e.tensor_tensor(out=t3, in0=saT[:, :, 1, :].rearrange("p s j -> p j s"), in1=wav(1), op=ALU.mult),
                     reads=["saT", "cols"], writes=[t1r])
                P.op("dve", lambda e, cxv=cxv, x4=x4, a3=a3, t3=t3, wav=wav, sc0=sc0: e.tensor_tensor(out=a3, in0=a3, in1=t3, op=ALU.add), reads=[accr, t1r], writes=[accr])
                P.op("dve", lambda e, cxv=cxv, x4=x4, a3=a3, t3=t3, wav=wav, sc0=sc0: e.tensor_tensor(out=t3, in0=cxv, in1=wav(2), op=ALU.mult), reads=ores + ["cols"], writes=[t1r])
                P.op("dve", lambda e, cxv=cxv, x4=x4, a3=a3, t3=t3, wav=wav, sc0=sc0: e.tensor_tensor(out=a3, in0=a3, in1=t3, op=ALU.add), reads=[accr, t1r], writes=[accr])
                P.op("dve", lambda e, cxv=cxv, x4=x4, a3=a3, t3=t3, wav=wav, sc0=sc0: e.tensor_tensor(out=ab[:, :, sc0:sc0 + NS], in0=x4[:, :, 2, :], in1=a3, op=ALU.mult),
                     reads=xres + [accr], writes=[("ab", j, 2) for j in range(8)])

            if pss == 0:
                P.op("pool", lambda e: e.tensor_copy(out=oAs[:, :, 0, :], in_=saT[:, :, 1, :]), reads=["saT"], writes=[("oAs", j) for j in range(8)])
                P.op("pool", lambda e: e.tensor_copy(out=oFs[:, :, 0, :], in_=sfT[:, :, 1, :]), reads=["sfT"], writes=[("oFs", c) for c in range(44)])
            STAGE_MARKS.append(("D", pss, len(P.ops["pe"])))
            for jp in range(4):
                iu, wu, wur = next_item()
                for u in range(2):
                    j = jp * 2 + u
                    for b in blocks:
                        c0, n, bi = b["c0"], b["n"], b["idx"]
                        pU, pUr = psR.get(); pS, pSr = psR.get()
                        proj(pU, pUr, wu, wur, u, xn, "xn", b)
                        tu, tur = tmpR.get()
                        P.op("act", lambda e, tu=tu, pU=pU, n=n: e.activation(out=tu[:, 0:n], in_=pU[:, 0:n], func=AF.Gelu_apprx_tanh),
                             reads=[pUr], writes=[tur])
                        if b["kind"] == "p":
                            for q in range(4):
                                t = c0 // 128 + q
                                P.op("pe", lambda e, q=q, t=t, j=j, pS=pS: e.matmul(
                                    pS[:, q * 128:(q + 1) * 128], lhsT=vn3[:, t, j * 128:(j + 1) * 128], rhs=WsT[:, j, :], start=True, stop=True),
                                    reads=[("vn", t), "WsT"], writes=[pSr], signal=(q == 3))
                            ts_, tsr = tmpR.get()
                            P.op("dve", lambda e, ts_=ts_, pS=pS, j=j: e.tensor_tensor(
                                out=ts_[:, 0:512].rearrange("p (a t) -> p a t", a=4), in0=pS[:, :].rearrange("p (a t) -> p a t", a=4),
                                in1=bcast_mid(bbc[:, j, :], 4), op=ALU.add), reads=[pSr, "bbc"], writes=[tsr])
                            P.op("dve", lambda e, ts_=ts_, tu=tu, j=j, c0=c0, n=n: e.tensor_tensor(
                                out=us[:, j, c0:c0 + n], in0=ts_[:, 0:n], in1=tu[:, 0:n], op=ALU.mult),
                                reads=[tsr, tur], writes=[("us", j, bi)])
                        else:
                            P.op("pe", lambda e, j=j, pS=pS, n=n: e.matmul(
                                pS[:, 0:n], lhsT=vn3[0:NS, 8, j * 128:(j + 1) * 128], rhs=Ds[:, j, :], start=True, stop=True),
                                reads=[("vn", 8), "Ds"], writes=[pSr])
                            P.op("dve", lambda e, pS=pS, tu=tu, j=j, c0=c0, n=n: e.scalar_tensor_tensor(
                                out=us[:, j, c0:c0 + n], in0=pS[:, 0:n], scalar=bbc[:, j, 0:1], in1=tu[:, 0:n], op0=ALU.add, op1=ALU.mult),
                                reads=[pSr, tur, "bbc"], writes=[("us", j, bi)])
                R.release(iu)

            if pss == 1:
                store_colsT(carry_a[:].rearrange("p r j -> p (r j)"), 16, [("carry_a", j) for j in range(8)],
                            ca_d.rearrange("r (j p) -> (r j) p", p=128))
                store_colsT(oAs[:].rearrange("p s r j -> p (s r j)"), 256, [("oAs", j) for j in range(8)],
                            cas_d.rearrange("s r (j p) -> (s r j) p", p=128))
            STAGE_MARKS.append(("E", pss, len(P.ops["pe"])))
            for ip in range(4):
                iga, wga, wgar = next_item()
                igb, wgb, wgbr = next_item()
                ioa, woa, woar = next_item()
                iob, wob, wobr = next_item()
                for u in range(2):
                    i = ip * 2 + u
                    for b in blocks:
                        c0, n, bi = b["c0"], b["n"], b["idx"]
                        pGA, pGAr = psR.get(); pGB, pGBr = psR.get(); pYA, pYAr = psR.get(); pYB, pYBr = psR.get()
                        proj(pGA, pGAr, wga, wgar, u, xn, "xn", b)
                        proj(pGB, pGBr, wgb, wgbr, u, xn, "xn", b)
                        proj(pYA, pYAr, woa, woar, u, ab, "ab", b)
                        proj(pYB, pYBr, wob, wobr, u, us, "us", b)
                        ta, tar = tmpR.get(); tb, tbr = tmpR.get()
                        P.op("act", lambda e, ta=ta, pGA=pGA, n=n: e.activation(out=ta[:, 0:n], in_=pGA[:, 0:n], func=AF.Tanh, scale=0.5),
                             reads=[pGAr], writes=[tar])
                        P.op("act", lambda e, tb=tb, pGB=pGB, n=n: e.activation(out=tb[:, 0:n], in_=pGB[:, 0:n], func=AF.Tanh, scale=0.5),
                             reads=[pGBr], writes=[tbr])
                        P.op("dve", lambda e, ta=ta, pYA=pYA, n=n: e.scalar_tensor_tensor(
                            out=ta[:, 0:n], in0=ta[:, 0:n], scalar=1.0, in1=pYA[:, 0:n], op0=ALU.add, op1=ALU.mult),
                            reads=[tar, pYAr], writes=[tar])
                        P.op("dve", lambda e, tb=tb, pYB=pYB, n=n: e.scalar_tensor_tensor(
                            out=tb[:, 0:n], in0=tb[:, 0:n], scalar=1.0, in1=pYB[:, 0:n], op0=ALU.add, op1=ALU.mult),
                            reads=[tbr, pYBr], writes=[tbr])
                        P.op("pool", lambda e, ta=ta, tb=tb, i=i, c0=c0, n=n: e.tensor_tensor(
                            out=m3[:, i, c0:c0 + n], in0=ta[:, 0:n], in1=tb[:, 0:n], op=ALU.add),
                            reads=[tar, tbr], writes=[("m", i, bi)])
                for it in (iga, igb, ioa, iob):
                    R.release(it)

            STAGE_MARKS.append(("F", pss, len(P.ops["pe"])))
            nacc = NormAcc(blocks)
            wos = [next_item() for _ in range(4)]
            for b in blocks:
                c0, n, bi = b["c0"], b["n"], b["idx"]
                for ip in range(4):
                    io, wo, wor = wos[ip]
                    for u in range(2):
                        i = ip * 2 + u
                        pO, pOr = psR.get()
                        proj(pO, pOr, wo, wor, u, m3, "m", b)
                        P.op("dve", lambda e, pO=pO, i=i, c0=c0, n=n: e.scalar_tensor_tensor(
                            out=h[:, i, c0:c0 + n], in0=pO[:, 0:n], scalar=0.5, in1=h[:, i, c0:c0 + n], op0=ALU.mult, op1=ALU.add),
                            reads=[pOr, ("h", i, bi)], writes=[("h", i, bi)])
                        nacc.add(i, b)
                    if ip == 1 and bi == 1:
                        nacc.finish_block(blocks[0], 1)
            for (io, _, _) in wos:
                R.release(io)
            for b in blocks[1:]:
                nacc.finish_block(b, 1)

            STAGE_MARKS.append(("G", pss, len(P.ops["pe"])))
            deferredG = [None]
            pblocks = [b_ for b_ in blocks if b_["kind"] == "p"]
            sblocks = [b_ for b_ in blocks if b_["kind"] == "s"]
            NP2 = 1024
            accR = Rot([(big[i], [("big", i)]) for i in range(4)] + [(gvbc, ["gvbc"]), (gfbc, ["gfbc"])])
            for cp in range(11):
                ia, wa, war = next_item()
                ib_, wb_, wbr_ = next_item()
                for u in range(2):
                    c = cp * 2 + u
                    banks = []
                    for b in pblocks:
                        pA, pAr = psR.get(); pB, pBr = psR.get()
                        proj(pA, pAr, wa, war, u, xn, "xn", b)
                        proj(pB, pBr, wb_, wbr_, u, xn, "xn", b)
                        banks.append(((pA, pAr), (pB, pBr)))
                    accs = []
                    for half in range(2):
                        cc = c + 22 * half
                        ux, uxr = wideR.get(); acc, accr = accR.get()
                        P.op("act", lambda e, ux=ux, cc=cc: e.copy(out=ux[:, 0:2], in_=carry_f[:, :, cc]),
                             reads=[("carry_f", cc), "carry_f"], writes=uxr)
                        for q, b in enumerate(pblocks):
                            pp, ppr = banks[q][half]
                            P.op("act", lambda e, ux=ux, pp=pp, q=q: e.copy(out=ux[:, 2 + q * 512: 2 + (q + 1) * 512], in_=pp[:, 0:512]),
                                 reads=[ppr], writes=uxr)
                            P.op("act", lambda e, acc=acc, pp=pp, cc=cc, q=q: e.activation(
                                out=acc[:, q * 512:(q + 1) * 512], in_=pp[:, 0:512], func=AF.Identity, scale=wcf(2, cc)),
                                reads=[ppr, "cols"], writes=accr)
                        P.op("pool", lambda e, ux=ux, cc=cc: e.tensor_copy(out=carry_f[:, :, cc], in_=ux[:, NP2:NP2 + 2]),
                             reads=uxr, writes=[("carry_f", cc)])
                        for tap in (0, 1):
                            P.op("dve", lambda e, acc=acc, ux=ux, cc=cc, tap=tap: e.scalar_tensor_tensor(
                                out=acc[:, 0:NP2], in0=ux[:, tap:tap + NP2], scalar=wcf(tap, cc), in1=acc[:, 0:NP2], op0=ALU.mult, op1=ALU.add),
                                reads=uxr + accr + ["cols"], writes=accr)
                        accs.append((acc, accr))
                    (aa, aar), (ab_, abr_) = accs

                    def phase2(aa=aa, aar=aar, ab_=ab_, abr_=abr_, c=c):
                        P.op("act", lambda e: e.activation(out=aa[:, 0:NP2], in_=aa[:, 0:NP2], func=AF.Gelu_apprx_tanh),
                             reads=aar, writes=aar)
                        for q in range(2):
                            P.op("pool", lambda e, q=q: e.tensor_tensor(
                                out=act[:, c, q * 512:(q + 1) * 512], in0=aa[:, q * 512:(q + 1) * 512], in1=ab_[:, q * 512:(q + 1) * 512], op=ALU.mult),
                                reads=aar + abr_, writes=[("act", c, q)])
                    if deferredG[0] is not None:
                        deferredG[0]()
                    deferredG[0] = phase2
                    for b in sblocks:
                        pz, pzr = psR.get()
                        proj(pz[:, 0:NS], pzr, wa, war, u, xn, "xn", b)
                        proj(pz[:, NS:2 * NS], pzr, wb_, wbr_, u, xn, "xn", b)
                        P.op("act", lambda e, pz=pz, c=c: e.copy(
                            out=bass.AP(oFs[:, 0, 1, c:c + 1].tensor, oFs[:, 0, 1, c:c + 1].offset, [[NS * 2 * 44, 128], [22, 2], [88, NS]]),
                            in_=pz[:, 0:2 * NS].rearrange("p (a s) -> p a s", a=2)),
                            reads=[pzr], writes=[("oFs", c), ("oFs", c + 22)])
                R.release(ia); R.release(ib_)
            if deferredG[0] is not None:
                deferredG[0]()
                deferredG[0] = None
            P.dma("sp", (lambda e: e.dma_start(out=gfbc[:], in_=g_final.partition_broadcast(128))), "rl0", writes=["gfbc"])
            if pss == 0:
                P.dma("sp", (lambda e: e.dma_start(out=gvbc[:], in_=g_v.partition_broadcast(128))), "rl1", writes=["gvbc"])
                P.dma("sp", (lambda e: e.dma_start(out=bbcw[:, 0:1024], in_=b_spatial.rearrange("j t -> (j t)").partition_broadcast(128))),
                      "rl2", writes=["bbc"])
            if pss == 1:
                sblk = blocks[2]
                sc0 = sblk["c0"]
                accs = []
                for half in range(2):
                    acc, accr = tmpR.get(); t1, t1r = tmpR.get()
                    a3 = acc[:, 0:22 * NS].rearrange("p (c s) -> p c s", c=22)
                    t3 = t1[:, 0:22 * NS].rearrange("p (c s) -> p c s", c=22)

                    def wv(tap, half=half):
                        o = tap * 44 + half * 22
                        w = wcfT[:, o:o + 22]
                        pr = [list(x) for x in w.ap]
                        return bass.AP(w.tensor, w.offset, [pr[0], pr[1], [0, NS]])

                    def sv(r, half=half):
                        return sfT[:, :, r, half * 22:(half + 1) * 22].rearrange("p s c -> p c s") if r < 2 else \
                            oFs[:, :, 1, half * 22:(half + 1) * 22].rearrange("p s c -> p c s")
                    ores = [("oFs", cq) for cq in range(half * 22, half * 22 + 22)]
                    P.op("dve", lambda e, a3=a3, sv=sv, wv=wv: e.tensor_tensor(out=a3, in0=sv(0), in1=wv(0), op=ALU.mult),
                         reads=["sfT", "wcfT"], writes=[accr])
                    P.op("dve", lambda e, t3=t3, sv=sv, wv=wv: e.tensor_tensor(out=t3, in0=sv(1), in1=wv(1), op=ALU.mult),
                         reads=["sfT", "wcfT"], writes=[t1r])
                    P.op("dve", lambda e, a3=a3, t3=t3: e.tensor_tensor(out=a3, in0=a3, in1=t3, op=ALU.add), reads=[accr, t1r], writes=[accr])
                    P.op("dve", lambda e, t3=t3, sv=sv, wv=wv: e.tensor_tensor(out=t3, in0=sv(2), in1=wv(2), op=ALU.mult),
                         reads=ores + ["wcfT"], writes=[t1r])
                    P.op("dve", lambda e, a3=a3, t3=t3: e.tensor_tensor(out=a3, in0=a3, in1=t3, op=ALU.add), reads=[accr, t1r], writes=[accr])
                    accs.append((a3, accr))
                (aa3, aar), (ab3, abr) = accs
                P.op("act", lambda e, aa3=aa3: e.activation(out=aa3, in_=aa3, func=AF.Gelu_apprx_tanh), reads=[aar], writes=[aar])
                P.op("dve", lambda e, aa3=aa3, ab3=ab3, sc0=sc0: e.tensor_tensor(out=act[:, :, sc0:sc0 + NS], in0=aa3, in1=ab3, op=ALU.mult),
                     reads=[aar, abr], writes=[("act", cq, 2) for cq in range(22)])

            STAGE_MARKS.append(("H", pss, len(P.ops["pe"])))
            nacc = NormAcc(blocks)

            def h_part(i, b, wd, wdr, pD, pDr, c_lo, c_hi):
                c0, n, bi = b["c0"], b["n"], b["idx"]
                for c in range(c_lo, c_hi):
                    P.op("pe", lambda e, c=c: e.matmul(
                        pD[:, 0:n], lhsT=wd[:, c, :], rhs=act[:, c, c0:c0 + n], start=(c == 0), stop=(c == 21)),
                        reads=wdr + [("act", c, bi)], writes=[pDr] + (["R1busy"] if c == 21 else []), signal=(c == 21))

            def h_tail(i, b, pD, pDr):
                c0, n, bi = b["c0"], b["n"], b["idx"]
                P.op("dve", lambda e: e.tensor_tensor(
                    out=h[:, i, c0:c0 + n], in0=pD[:, 0:n], in1=h[:, i, c0:c0 + n], op=ALU.add),
                    reads=[pDr, ("h", i, bi)], writes=[("h", i, bi)])
                nacc.add(i, b)

            NLATE = 2
            first_units = [next_item() for _ in range(2)]
            grp = []
            for i, (idn, wd, wdr) in enumerate(first_units):
                for b in blocks:
                    if b["kind"] != "p":
                        continue
                    pD, pDr = psR.get()
                    h_part(i, b, wd, wdr, pD, pDr, 0, 22 - NLATE)
                    grp.append((i, b, wd, wdr, pD, pDr))
            for (i, b, wd, wdr, pD, pDr) in grp:
                h_part(i, b, wd, wdr, pD, pDr, 22 - NLATE, 22)
                h_tail(i, b, pD, pDr)
            for i, (idn, wd, wdr) in enumerate(first_units):
                for b in blocks:
                    if b["kind"] == "p":
                        continue
                    pD, pDr = psR.get()
                    h_part(i, b, wd, wdr, pD, pDr, 0, 22)
                    h_tail(i, b, pD, pDr)
                R.release(idn)
            for i in range(2, 6):
                idn, wd, wdr = next_item()
                for b in blocks:
                    pD, pDr = psR.get()
                    h_part(i, b, wd, wdr, pD, pDr, 0, 22)
                    h_tail(i, b, pD, pDr)
                R.release(idn)
            last_units = [next_item() for _ in range(2)]
            prevb = None
            for b in blocks:
                for i, (idn, wd, wdr) in zip((6, 7), last_units):
                    pD, pDr = psR.get()
                    h_part(i, b, wd, wdr, pD, pDr, 0, 22)
                    h_tail(i, b, pD, pDr)
                if prevb is not None:
                    nacc.finish_block(prevb, 2)
                prevb = b
            for (idn, _, _) in last_units:
                R.release(idn)

            if pss == 1:
                store_colsT(carry_f[:].rearrange("p r c -> p (r c)"), 88, [("carry_f", c) for c in range(44)],
                            cf_d.rearrange("r (c p) -> (r c) p", p=128))
                store_colsT(oFs[:].rearrange("p s r c -> p (s r c)"), 1408, [("oFs", c) for c in range(44)],
                            cfs_d.rearrange("s r (c p) -> (s r c) p", p=128))
            if pss == 0:
                stage_inputs(1)
            nacc.finish_block(prevb, 2)

            STAGE_MARKS.append(("I", pss, len(P.ops["pe"])))
            nacc = NormAcc(blocks)
            early_done = set()
            ipl, wpl, wplr = next_item()
            wgs = [next_item() for _ in range(4)]
            for b in blocks:
                c0, n, bi = b["c0"], b["n"], b["idx"]
                for ip in range(4):
                    ig, wg, wgr = wgs[ip]
                    for u in range(2):
                        i = ip * 2 + u
                        pG, pGr = psR.get(); pE, pEr = psR.get()
                        proj(pG, pGr, wg, wgr, u, xn, "xn", b)
                        for k in range(2):
                            P.op("pe", lambda e, k=k, pE=pE, i=i, c0=c0, n=n, wpl=wpl: e.matmul(
                                pE[:, 0:n], lhsT=wpl[:, k, i * 128:(i + 1) * 128], rhs=pT[:, k, c0:c0 + n], start=(k == 0), stop=(k == 1)),
                                reads=wplr + [("pT", bi)], writes=[pEr], signal=(k == 1))
                        tg, tgr = tmpR.get()
                        P.op("act", lambda e, tg=tg, pG=pG, n=n: e.activation(out=tg[:, 0:n], in_=pG[:, 0:n], func=AF.Tanh, scale=0.5),
                             reads=[pGr], writes=[tgr])
                        P.op("dve", lambda e, tg=tg, pE=pE, n=n: e.scalar_tensor_tensor(
                            out=tg[:, 0:n], in0=tg[:, 0:n], scalar=1.0, in1=pE[:, 0:n], op0=ALU.add, op1=ALU.mult),
                            reads=[tgr, pEr], writes=[tgr])
                        P.op("dve", lambda e, tg=tg, i=i, c0=c0, n=n: e.scalar_tensor_tensor(
                            out=h[:, i, c0:c0 + n], in0=tg[:, 0:n], scalar=0.5, in1=h[:, i, c0:c0 + n], op0=ALU.mult, op1=ALU.add),
                            reads=[tgr, ("h", i, bi)], writes=[("h", i, bi)])
                        nacc.add(i, b)
                    if ip == 1 and b["idx"] == 1:
                        nacc.finish_block(blocks[0], 3, final=True)
                        early_done.add(0)
            for (ig, _, _) in wgs:
                R.release(ig)
            R.release(ipl)

            STAGE_MARKS.append(("O", pss, len(P.ops["pe"])))
            for b in blocks:
                btiles = [tl for tl in tiles if tl["blk"] == b["idx"]]
                if b["idx"] not in early_done:
                    nacc.finish_block(b, 3, final=True)

                def transposes(tl):
                    t, rows, c0, blk = tl["t"], tl["rows"], tl["c0"], tl["blk"]
                    banks = []
                    for hf in range(2):
                        pz, pzr = psR.get()
                        for kk in range(4):
                            k = hf * 4 + kk
                            P.op("pe", lambda e, k=k, kk=kk, pz=pz, rows=rows, c0=c0: e.transpose(
                                out=pz[0:rows, kk * 128:(kk + 1) * 128], in_=h[:, k, c0:c0 + rows], identity=idf[:]),
                                reads=[("h", k, blk), "idf"], writes=[pzr], signal=(kk == 3))
                        banks.append((pz, pzr))
                    return banks

                pend = transposes(btiles[0])
                pr, prr = psR.get()
                for q, tl in enumerate(btiles):
                    P.op("pe", lambda e, q=q, tl=tl, pr=pr: e.transpose(
                        out=pr[0:tl["rows"], q:q + 1], in_=rs[0:1, tl["c0"]:tl["c0"] + tl["rows"]], identity=idf[0:1, 0:1]),
                        reads=[("rs", b["idx"]), "idf"], writes=[prr], signal=(q == len(btiles) - 1))
                t0_ = btiles[0]["t"]
                rws = btiles[0]["rows"]
                P.op("dve", lambda e, t0_=t0_, nq=len(btiles), rws=rws, pr=pr: e.tensor_copy(out=rtok[0:rws, t0_:t0_ + nq], in_=pr[0:rws, 0:nq]),
                     reads=[prr], writes=[("rtok", b["idx"])])
                for q, tl in enumerate(btiles):
                    t, rows = tl["t"], tl["rows"]
                    banks = pend
                    if q + 1 < len(btiles):
                        pend = transposes(btiles[q + 1])
                    ob, obr = bigR.get()
                    for hf, (pz, pzr) in enumerate(banks):
                        P.op("dve", lambda e, pz=pz, ob=ob, hf=hf, t=t, rows=rows: e.scalar_tensor_tensor(
                            out=ob[0:rows, hf * 512:(hf + 1) * 512], in0=pz[0:rows, :], scalar=rtok[0:rows, t:t + 1],
                            in1=gfbc[0:rows, hf * 512:(hf + 1) * 512], op0=ALU.mult, op1=ALU.mult),
                            reads=[pzr, ("rtok", b["idx"]), "gfbc"], writes=[obr])
                    dst = ys_d[:, :] if t == 8 else y_d[pss * 1024 + t * 128: pss * 1024 + (t + 1) * 128, :]
                    outs_tok.append(P.dma("sp", (lambda e, ob=ob, rows=rows, dst=dst: e.dma_start(out=dst, in_=ob[0:rows, :])),
                                          "st" + str(obr[1]), reads=[obr]))

        P.final_wait("sp", outs_tok)

        sems = {n: es.enter_context(nc.semaphore(n)) for n in P.sem_names()}
        with nc.Block() as block:
            P.emit(block, sems)
    return nc


_CACHE = {}
STAGE_MARKS = []


def make_in_maps(x_prompt, x_sample, p_prompt, p_sample, state_conv_a, state_conv_ffn,
                 g_mix, w_in, w_conv_a, w_out_a, g_v, w_spatial, b_spatial, w_out_b, w_o,
                 g_ffn, w_up, w_conv_ffn, w_down, g_ple, w_ple_gate, w_ple, g_final):
    f = lambda a: np.ascontiguousarray(np.asarray(a, dtype=np.float32))
    shared = {
        "g_mix": f(g_mix[0]), "g_v": f(g_v[0]), "g_ffn": f(g_ffn[0]), "g_ple": f(g_ple[0]), "g_final": f(g_final),
        "w_in": f(w_in[0]), "w_conv_a": f(w_conv_a[0]), "w_out_a": f(w_out_a[0]), "w_out_b": f(w_out_b[0]),
        "w_o": f(w_o[0]), "w_spatial": f(w_spatial[0]), "b_spatial": f(b_spatial[0]), "w_up": f(w_up[0]),
        "w_conv_ffn": f(w_conv_ffn[0]), "w_down": f(w_down[0]), "w_ple_gate": f(w_ple_gate[0]), "w_ple": f(w_ple[0]),
        "ident": np.eye(128, dtype=np.float32), "tril": np.tril(np.ones((128, 128), dtype=np.float32)),
    }
    in_maps = []
    for c in range(NCORES):
        sl = slice(c * NS, (c + 1) * NS)
        d = dict(shared)
        d["x"] = f(x_prompt[c]); d["xs"] = f(x_sample[sl, 0])
        d["p"] = f(p_prompt[0, c]); d["ps"] = f(p_sample[0, sl, 0])
        d["sa"] = f(state_conv_a[0, sl]); d["sf"] = f(state_conv_ffn[0, sl])
        in_maps.append(d)
    return in_maps


def kernel(x_prompt, x_sample, p_prompt, p_sample, state_conv_a, state_conv_ffn,
           g_mix, w_in, w_conv_a, w_out_a, g_v, w_spatial, b_spatial, w_out_b, w_o,
           g_ffn, w_up, w_conv_ffn, w_down, g_ple, w_ple_gate, w_ple, g_final):
    if "nc" not in _CACHE:
        _CACHE["nc"] = build_program()
    nc = _CACHE["nc"]
    in_maps = make_in_maps(x_prompt, x_sample, p_prompt, p_sample, state_conv_a, state_conv_ffn,
                           g_mix, w_in, w_conv_a, w_out_a, g_v, w_spatial, b_spatial, w_out_b, w_o,
                           g_ffn, w_up, w_conv_ffn, w_down, g_ple, w_ple_gate, w_ple, g_final)
    res = run_bass_kernel_spmd(nc, in_maps, core_ids=list(range(NCORES)))
    rs_ = res.results
    y_prompt = np.stack([rs_[c]["y"] for c in range(NCORES)])
    y_sample = np.concatenate([rs_[c]["ys"] for c in range(NCORES)])[:, None, :]
    ca_p = np.stack([rs_[c]["ca"] for c in range(NCORES)])[None]
    ca_s = np.concatenate([rs_[c]["cas"] for c in range(NCORES)])[None]
    cv_p = np.stack([rs_[c]["cv"] for c in range(NCORES)])[None]
    cv_s = np.concatenate([rs_[c]["cvs"] for c in range(NCORES)])[None, :, None, :]
    cf_p = np.stack([rs_[c]["cf"] for c in range(NCORES)])[None]
    cf_s = np.concatenate([rs_[c]["cfs"] for c in range(NCORES)])[None]
    return (y_prompt.astype(np.float32), y_sample.astype(np.float32), ca_p.astype(np.float32), ca_s.astype(np.float32),
            cv_p.astype(np.float32), cv_s.astype(np.float32), cf_p.astype(np.float32), cf_s.astype(np.float32))
```

```python
import contextlib
import numpy as np
import concourse.bass as bass
import concourse.mybir as mybir
from concourse.bass_utils import run_bass_kernel_spmd

F32 = mybir.dt.float32
BF16 = mybir.dt.bfloat16
AF = mybir.ActivationFunctionType
ALU = mybir.AluOpType

D = 1024
SEQ = 2048
NS = 16
DFF = 2816
NIN = 7168
PD = 256
EPS = 1e-6
NCORES = 8
TT = 1040


class Prog:
    ENGS = ("pe", "act", "dve", "pool", "sp")

    def __init__(self):
        self.ops = {e: [] for e in self.ENGS}
        self.ticket = {e: 0 for e in self.ENGS}
        self.pending = {e: False for e in self.ENGS}
        self.last_write = {}
        self.readers = {}
        self.known = {e: {} for e in self.ENGS}
        self.dma_count = {}

    def _deps(self, eng, reads, writes):
        deps = {}
        for r in reads:
            t = self.last_write.get(r)
            if t is not None and deps.get(t[0], 0) < t[1]:
                deps[t[0]] = t[1]
        for w in writes:
            t = self.last_write.get(w)
            if t is not None and deps.get(t[0], 0) < t[1]:
                deps[t[0]] = t[1]
            rd = self.readers.get(w)
            if rd:
                for k, v in rd.items():
                    if deps.get(k, 0) < v:
                        deps[k] = v
        waits = []
        kn = self.known[eng]
        for k, v in deps.items():
            if k == "pe" and eng == "pe":
                continue
            if kn.get(k, 0) >= v:
                continue
            kn[k] = v
            waits.append((k, v))
        return waits

    def _register(self, tok, reads, writes):
        for r in reads:
            d = self.readers.setdefault(r, {})
            if d.get(tok[0], 0) < tok[1]:
                d[tok[0]] = tok[1]
        for w in writes:
            self.last_write[w] = tok
            self.readers[w] = {}

    @staticmethod
    def _flat(seq):
        out = []
        for x in seq:
            if isinstance(x, list):
                out.extend(x)
            else:
                out.append(x)
        return out

    def op(self, eng, fn, reads=(), writes=(), signal=True):
        reads = self._flat(reads); writes = self._flat(writes)
        waits = self._deps(eng, reads, writes)
        if signal:
            self.ticket[eng] += 1
            tok = (eng, self.ticket[eng])
            self.pending[eng] = False
        else:
            tok = (eng, self.ticket[eng] + 1)
            self.pending[eng] = True
        self.ops[eng].append((fn, waits, ("eng", eng) if signal else None))
        self._register(tok, reads, writes)
        return tok

    def dma(self, eng, fn, sem, reads=(), writes=(), tok_override=None):
        reads = self._flat(reads); writes = self._flat(writes)
        waits = self._deps(eng, reads, writes)
        self.dma_count[sem] = self.dma_count.get(sem, 0) + 1
        tok = tok_override or (sem, 16 * self.dma_count[sem])
        self.ops[eng].append((fn, waits, ("dma", sem)))
        self._register(tok, reads, writes)
        return tok

    def final_wait(self, eng, toks):
        deps = {}
        for k, v in toks:
            deps[k] = max(deps.get(k, 0), v)
        self.ops[eng].append((None, list(deps.items()), None))

    def sem_names(self):
        names = set(self.ENGS)
        names.update(self.dma_count.keys())
        return sorted(names)

    def emit(self, block, sems):
        engmap = {"pe": block.tensor, "act": block.scalar, "dve": block.vector,
                  "pool": block.gpsimd, "sp": block.sync}
        for e in self.ENGS:
            ops = self.ops[e]
            if not ops:
                continue
            assert not self.pending[e], e

            def body(engine, ops=ops):
                for fn, waits, sig in ops:
                    for k, v in waits:
                        engine.wait_ge(sems[k], v)
                    if fn is None:
                        continue
                    ins = fn(engine)
                    if sig is not None:
                        if sig[0] == "eng":
                            ins.then_inc(sems[sig[1]], 1)
                        else:
                            ins.then_inc(sems[sig[1]], 16)
            engmap[e](body)


class Rot:
    def __init__(self, items):
        self.items = items
        self.i = 0

    def get(self):
        it = self.items[self.i % len(self.items)]
        self.i += 1
        return it


def bcast_mid(ap, reps):
    pairs = [list(x) for x in ap.ap]
    assert len(pairs) == 2, pairs
    return bass.AP(ap.tensor, ap.offset, [pairs[0], [0, reps], pairs[1]])


def build_program():
    nc = bass.Bass("TRN2", target_bir_lowering=False)
    P = Prog()

    def din(name, shape):
        return nc.dram_tensor(name, list(shape), F32, kind="ExternalInput").ap()

    def dout(name, shape):
        return nc.dram_tensor(name, list(shape), F32, kind="ExternalOutput").ap()

    x_d = din("x", [SEQ, D]); xs_d = din("xs", [NS, D])
    p_d = din("p", [SEQ, PD]); ps_d = din("ps", [NS, PD])
    sa_d = din("sa", [NS, 2, D]); sf_d = din("sf", [NS, 2, 2 * DFF])
    g_mix = din("g_mix", [D]); g_v = din("g_v", [D]); g_ffn = din("g_ffn", [D])
    g_ple = din("g_ple", [D]); g_final = din("g_final", [D])
    w_in = din("w_in", [D, NIN]); w_conv_a = din("w_conv_a", [3, D])
    w_out_a = din("w_out_a", [D, D]); w_out_b = din("w_out_b", [D, D]); w_o = din("w_o", [D, D])
    w_spatial = din("w_spatial", [8, 128, 128]); b_spatial = din("b_spatial", [8, 128])
    w_up = din("w_up", [D, 2 * DFF]); w_conv_ffn = din("w_conv_ffn", [3, 2 * DFF])
    w_down = din("w_down", [DFF, D]); w_ple_gate = din("w_ple_gate", [D, D]); w_ple = din("w_ple", [PD, D])
    ident_d = din("ident", [128, 128]); tril_d = din("tril", [128, 128])

    y_d = dout("y", [SEQ, D]); ys_d = dout("ys", [NS, D])
    ca_d = dout("ca", [2, D]); cas_d = dout("cas", [NS, 2, D])
    cv_d = dout("cv", [128, D]); cvs_d = dout("cvs", [NS, D])
    cf_d = dout("cf", [2, 2 * DFF]); cfs_d = dout("cfs", [NS, 2, 2 * DFF])

    es = contextlib.ExitStack()
    with es:
        def sb(name, shape, dt):
            return es.enter_context(nc.sbuf_tensor("s_" + name, list(shape), dt))

        h = sb("h", [128, 8, TT], F32)
        xn = sb("xn", [128, 8, TT], BF16)
        R1 = sb("R1", [128, 9216 + 2 * 8 * TT], BF16)
        vn3 = R1[:, 0:9216].rearrange("p (t f) -> p t f", t=9)
        m3 = R1[:, 0:8 * TT].rearrange("p (k t) -> p k t", k=8)
        ab = R1[:, 9216:9216 + 8 * TT].rearrange("p (k t) -> p k t", k=8)
        us = R1[:, 9216 + 8 * TT:9216 + 16 * TT].rearrange("p (k t) -> p k t", k=8)
        act = R1[:, 0:22 * TT].rearrange("p (k t) -> p k t", k=22)
        R1f = R1[:, :].bitcast(F32)
        xst = R1f[:, 0:8192].rearrange("p (t f) -> p t f", t=8)
        pst_all = R1f[:, 8192:10240].rearrange("p (t f) -> p t f", t=8)
        xss = R1f[0:NS, 10240:11264]
        pss_all = R1f[0:NS, 11264:11520]
        pT = sb("pT", [128, 2, TT], BF16)
        ring = sb("ring", [128, 8, 2048], BF16)
        rs = sb("rs", [128, TT], F32)
        idf = sb("idf", [128, 128], F32); idb = sb("idb", [128, 128], BF16)
        tril = sb("tril", [128, 128], F32)
        ones = sb("ones", [128, 128], BF16)
        WsT = sb("WsT", [128, 8, 128], BF16)
        Ds = sb("Ds", [16, 8, 16], BF16)
        w00 = sb("w00", [16, 8], F32)
        bbcw = sb("bbcw", [128, 1032], F32)
        bbc = bbcw[:, 0:1024].rearrange("p (j t) -> p j t", j=8)
        gvbc = sb("gvbc", [128, 1024], F32)
        gfbc = sb("gfbc", [128, 1024], F32)
        rtok = sb("rtok", [128, 16], F32)
        cols1 = sb("cols1", [128, 64], F32)
        cols2 = sb("cols2", [128, 128], F32)
        wcfT = sb("wcfT", [128, 132], F32)
        carry_a = sb("carry_a", [128, 2, 8], F32)
        carry_f = sb("carry_f", [128, 2, 44], F32)
        saT = sb("saT", [128, NS, 2, 8], F32)
        sfT = sb("sfT", [128, NS, 2, 44], F32)
        oAs = sb("oAs", [128, NS, 2, 8], F32)
        oFs = sb("oFs", [128, NS, 2, 44], F32)
        ssv = sb("ssv", [128, 16], F32)
        rv = sb("rv", [128, 16], F32)
        big = [sb(f"big{i}", [128, 1024], F32) for i in range(4)]
        pbf = [sb(f"pbf{i}", [128, 256], BF16) for i in range(2)]
        sqall = sb("sqall", [128, 2064], BF16)
        sqb = [sqall[:, i * 512:(i + 1) * 512] for i in range(4)]
        sqwide = sqall[:, :].bitcast(F32)
        NTMP = 8
        tmpall = sb("tmpall", [128, NTMP * 516], F32)
        tmpf = [tmpall[:, i * 516:(i + 1) * 516] for i in range(NTMP)]
        widef = [tmpall[:, 2 * j * 516: 2 * j * 516 + 1032] for j in range(NTMP // 2)]
        ps = [es.enter_context(nc.psum_tensor(f"pq{i}", [128, 512], F32)) for i in range(8)]

        bigR = Rot([(big[i], ("big", i)) for i in range(4)])
        pinR = Rot([(None, pbf[i], ("pin", i), ("pbf", i)) for i in range(2)])
        sqR = Rot([(sqb[i], ("sq", i)) for i in range(4)])
        tmpR = Rot([(tmpf[i], [("tmp", i, "a"), ("tmp", i, "b")]) for i in range(NTMP)])
        wideR = Rot([(widef[j], [("tmp", 2 * j, "a"), ("tmp", 2 * j, "b"), ("tmp", 2 * j + 1, "a"), ("tmp", 2 * j + 1, "b")])
                     for j in range(NTMP // 2)] + [(bbcw, ["bbc"]), (sqwide, [("sq", i) for i in range(4)])])
        class PsRot:
            def __init__(self):
                self.i = 0
                self.reserved = set()

            def get(self):
                while True:
                    b = self.i % 8
                    self.i += 1
                    if b not in self.reserved:
                        return ps[b], ("ps", b)

            def reserve(self):
                t, r = self.get()
                self.reserved.add(r[1])
                return t, r

            def release(self, r):
                self.reserved.discard(r[1])

        psR = PsRot()
        scr = sb("scr", [128, 2], F32)
        epsT = sb("epsT", [128, 1], F32)

        def pstag(i):
            return ("ps", i)

        def gcol(gi, k):
            return cols1[:, gi * 8 + k: gi * 8 + k + 1]

        def wca(tap, j):
            return cols1[:, 32 + tap * 8 + j: 32 + tap * 8 + j + 1]

        def wcf(tap, c):
            r = tap * 44 + c
            if r < 128:
                return cols2[:, r:r + 1]
            return cols1[:, 56 + r - 128: 56 + r - 128 + 1]

        def stage_inputs(pss):
            r0 = pss * 1024
            P.dma("sp", (lambda e: e.dma_start(out=pst_all, in_=p_d[r0:r0 + 1024, :].rearrange("(t q) f -> q t f", q=128))),
                  "xp", reads=["R1busy"], writes=[("pst",)])
            for hf in range(4):
                P.dma("sp", (lambda e, hf=hf: e.dma_start(
                    out=xst[:, hf * 2:(hf + 1) * 2, :],
                    in_=x_d[r0 + hf * 256: r0 + (hf + 1) * 256, :].rearrange("(t q) f -> q t f", q=128))),
                    f"xs{hf}", reads=["R1busy"], writes=[("xst", hf)])
            if pss == 1:
                P.dma("sp", (lambda e: e.dma_start(out=xss, in_=xs_d[:, :])), "xq0", reads=["R1busy"], writes=[("xss",)])
                P.dma("sp", (lambda e: e.dma_start(out=pss_all, in_=ps_d[:, :])), "xq1", reads=["R1busy"], writes=[("pss",)])

        outs_tok = []
        stg0, stg0r = big[0], ("big", 0)
        stg1, stg1r = big[1], ("big", 1)
        setup = []

        def sdma(eng, out, in_, writes, **kw):
            setup.append((eng, out, in_, writes, kw))

        sdma("sp", idf[:], ident_d[:, :], ["idf"])
        sdma("sp", tril[:], tril_d[:, :], ["tril"])
        sdma("sp", bbcw[:, 0:1024], b_spatial.rearrange("j t -> (j t)").partition_broadcast(128), ["bbc"])
        sdma("sp", gvbc[:], g_v.partition_broadcast(128), ["gvbc"])
        sdma("sp", gfbc[:], g_final.partition_broadcast(128), ["gfbc"])
        sdma("sp", stg0[0:8, 0:128], g_mix.rearrange("(j p) -> j p", p=128), [stg0r])
        sdma("sp", stg0[8:16, 0:128], g_ffn.rearrange("(j p) -> j p", p=128), [stg0r])
        sdma("sp", stg0[16:24, 0:128], g_ple.rearrange("(j p) -> j p", p=128), [stg0r])
        sdma("sp", stg0[24:32, 0:128], g_final.rearrange("(j p) -> j p", p=128), [stg0r])
        sdma("sp", stg0[32:56, 0:128], w_conv_a.rearrange("k (j p) -> (k j) p", p=128), [stg0r])
        wcf_rows = w_conv_ffn.rearrange("k (c p) -> (k c) p", p=128)
        sdma("sp", stg0[56:60, 0:128], wcf_rows[128:132, :], [stg0r])
        sdma("sp", stg0[:, 128:256], wcf_rows[0:128, :], [stg0r])
        sdma("sp", stg1[:, :].rearrange("p (j s) -> p j s", j=8), w_spatial.rearrange("j t s -> t j s"), [stg1r])
        sdma("sp", w00[:], w_spatial.rearrange("j t s -> j (t s)")[:, 0].partition_broadcast(16), ["w00"],
             allow_slow_non_contiguous=True)
        early = [x for x in setup if x[3] in (["idf"], [stg0r])]
        late = [x for x in setup if x not in early]
        for grp, sem in ((early, "setupA"), (None, None), (late, "setupB")):
            if grp is None:
                stage_inputs(0)
                continue
            stok = (sem, 16 * len(grp))
            for eng, out, in_, writes, kw in grp:
                P.dma(eng, (lambda e, out=out, in_=in_, kw=kw: e.dma_start(out=out, in_=in_, **kw)), sem,
                      tok_override=stok)
            for eng, out, in_, writes, kw in grp:
                P._register(stok, (), writes)

        P.op("dve", lambda e: e.tensor_copy(out=idb[:], in_=idf[:]), reads=["idf"], writes=["idb"])
        P.op("pool", lambda e: e.memset(ones[:], 1.0 / 1024.0), writes=["ones"])
        P.op("pool", lambda e: e.memset(scr[:], 1.0), writes=["scr"])
        P.op("pool", lambda e: e.memset(epsT[:], EPS), writes=["epsT"])
        P.op("pool", lambda e: e.memset(carry_a[:], 0.0), writes=["carry_a"])
        P.op("pool", lambda e: e.memset(carry_f[:], 0.0), writes=["carry_f"])
        pa, par = psR.get()
        P.op("pe", lambda e: e.transpose(out=pa[:, 0:60], in_=stg0[0:60, 0:128], identity=idf[0:60, 0:60]),
             reads=[stg0r, "idf"], writes=[par])
        P.op("dve", lambda e: e.tensor_copy(out=cols1[:, 0:60], in_=pa[:, 0:60]), reads=[par], writes=["cols"])
        pa2, par2 = psR.get()
        P.op("pe", lambda e: e.transpose(out=pa2[:, 0:128], in_=stg0[:, 128:256], identity=idf[:]),
             reads=[stg0r, "idf"], writes=[par2])
        P.op("dve", lambda e: e.tensor_copy(out=cols2[:], in_=pa2[:, 0:128]), reads=[par2], writes=["cols"])
        P.op("dve", lambda e: e.tensor_copy(out=wcfT[:, 0:128], in_=cols2[:]), reads=["cols"], writes=["wcfT"])
        P.op("dve", lambda e: e.tensor_copy(out=wcfT[:, 128:132], in_=cols1[:, 56:60]), reads=["cols"], writes=["wcfT"])
        def setup_spatial():
            for j in range(8):
                P.op("dve", lambda e, j=j: e.tensor_tensor(out=stg1[:, j * 128:(j + 1) * 128], in0=stg1[:, j * 128:(j + 1) * 128],
                                                            in1=tril[:], op=ALU.mult),
                     reads=[stg1r, "tril"], writes=[stg1r])
            for jj in range(2):
                pw, pwr = psR.get()
                for q in range(4):
                    j = jj * 4 + q
                    P.op("pe", lambda e, j=j, q=q, pw=pw: e.transpose(out=pw[:, q * 128:(q + 1) * 128],
                                                                        in_=stg1[:, j * 128:(j + 1) * 128], identity=idf[:]),
                         reads=[stg1r, "idf"], writes=[pwr], signal=(q == 3))
                P.op("act", lambda e, jj=jj, pw=pw: e.copy(out=WsT[:, jj * 4:(jj + 1) * 4, :].rearrange("p j t -> p (j t)"), in_=pw[:, :]),
                     reads=[pwr], writes=["WsT"])
            for j in range(8):
                P.op("dve", lambda e, j=j: e.tensor_scalar_mul(out=Ds[:, j, :], in0=idf[0:16, 0:16], scalar1=w00[:, j:j + 1]),
                     reads=["idf", "w00"], writes=["Ds"])

        def load_rounds(src_rows_ap, nrow_tiles, dst, dst_res, eng_sem):
            rounds = []
            for a0 in range(0, nrow_tiles, 4):
                na = min(4, nrow_tiles - a0)
                st = {}

                def issue(a0=a0, na=na, st=st):
                    bt, btr = bigR.get()
                    st["bt"] = (bt, btr)
                    P.dma("sp", (lambda e: e.dma_start(
                        out=bt[:, 0:na * 128].rearrange("q (a p) -> q a p", a=na),
                        in_=src_rows_ap[a0 * 128:(a0 + na) * 128, :].rearrange("(a q) p -> q a p", q=128))),
                        eng_sem + str(btr[1]), writes=[btr])

                def finish(a0=a0, na=na, st=st):
                    bt, btr = st["bt"]
                    pz, pzr = psR.get()
                    for a in range(na):
                        P.op("pe", lambda e, a=a: e.transpose(out=pz[:, a * 128:(a + 1) * 128],
                                                                in_=bt[:, a * 128:(a + 1) * 128], identity=idf[:]),
                             reads=[btr, "idf"], writes=[pzr], signal=(a == na - 1))
                    P.op("dve", lambda e: e.tensor_copy(out=dst[:, a0 * 128:(a0 + na) * 128], in_=pz[:, 0:na * 128]),
                         reads=[pzr], writes=[dst_res])
                rounds.append((issue, finish))
            return rounds

        saT_flat = saT[:].rearrange("p s r j -> p (s r j)")
        sfT_flat = sfT[:].rearrange("p s r c -> p (s r c)")

        def slab_items():
            items = []

            def std(w, c0):
                items.append(("std", w, c0))
            for q in range(4):
                std(w_in, 4096 + 256 * q)
            for jp in range(4):
                std(w_in, jp * 256); std(w_in, 1024 + jp * 256); std(w_in, 2048 + jp * 256)
            for jp in range(4):
                std(w_in, 3072 + jp * 256)
            for ip in range(4):
                std(w_in, 5120 + ip * 256); std(w_in, 6144 + ip * 256)
                std(w_out_a, ip * 256); std(w_out_b, ip * 256)
            for ip in range(4):
                std(w_o, ip * 256)
            for cp in range(11):
                std(w_up, cp * 256); std(w_up, DFF + cp * 256)
            for i in range(8):
                items.append(("down", w_down, i * 128))
            items.append(("ple", w_ple, 0))
            for ip in range(4):
                std(w_ple_gate, ip * 256)
            return items

        per_pass = slab_items()
        all_items = per_pass + per_pass
        NIT = len(all_items)

        class Ring:
            def __init__(self):
                self.next_load = 0
                self.slot_ptr = 0
                self.occupant = [None] * 8
                self.released = set()
                self.item_slot = {}

            def _try_load(self):
                i = self.next_load
                if i >= NIT:
                    return False
                kind, w, c0 = all_items[i]
                ns = 2 if kind == "down" else 1
                s = self.slot_ptr
                if ns == 2 and s % 2 == 1:
                    s = (s + 1) % 8
                for q in range(ns):
                    occ = self.occupant[(s + q) % 8]
                    if occ is not None and occ not in self.released:
                        return False
                if kind == "std":
                    dst = ring[:, s, :].rearrange("p (k n) -> p k n", k=8)
                    src = w[:, c0:c0 + 256].rearrange("(k p) n -> p k n", p=128)
                elif kind == "down":
                    dst = ring[:, s:s + 2, :].rearrange("p a b -> p (a b)")[:, 0:2816].rearrange("p (c n) -> p c n", c=22)
                    src = w[:, c0:c0 + 128].rearrange("(c p) n -> p c n", p=128)
                else:
                    dst = ring[:, s, :].rearrange("p (k n) -> p k n", k=2)
                    src = w[:, :].rearrange("(k p) n -> p k n", p=128)
                res = [("ring", (s + q) % 8) for q in range(ns)]
                xtra = [("xst", 0), ("xst", 1), ("xst", 2), ("xst", 3), ("pst",)] if i == 0 else []
                P.dma("pool", (lambda e, dst=dst, src=src: e.dma_start(out=dst, in_=src)), f"wr{s}", reads=xtra, writes=res)
                for q in range(ns):
                    self.occupant[(s + q) % 8] = i
                self.item_slot[i] = (s, dst, res)
                self.slot_ptr = (s + ns) % 8
                self.next_load += 1
                return True

            def prefetch(self):
                while self._try_load():
                    pass

            def get(self, i):
                self.prefetch()
                assert i in self.item_slot, (i, self.next_load)
                return self.item_slot[i]

            def release(self, i):
                self.released.add(i)
                self.prefetch()

        R = Ring()
        item_ctr = [0]

        def next_item():
            i = item_ctr[0]
            item_ctr[0] += 1
            s, dst, res = R.get(i)
            return i, dst, res

        class NormAcc:
            def __init__(self, blocks):
                self.blocks = blocks
                self.bank = {}
                self.count = {}
                self.pend_sq = []
                self.pend_mm = []
                for b in blocks:
                    self.count[b["idx"]] = 0

            def add(self, k, b):
                while len(self.pend_mm) >= 2:
                    self.pend_mm.pop(0)[1]()
                if self.pend_sq:
                    self.pend_sq.pop(0)[1]()
                c0, n, bi = b["c0"], b["n"], b["idx"]
                if bi not in self.bank:
                    self.bank[bi] = psR.reserve()
                pst, pstr = self.bank[bi]
                cnt = self.count[bi]
                self.count[bi] += 1

                def emit_sq():
                    sq, sqr = sqR.get()
                    P.op("act", lambda e: e.activation(out=sq[:, 0:n], in_=h[:, k, c0:c0 + n], func=AF.Square),
                         reads=[("h", k, bi)], writes=[sqr])
                    self.pend_mm.append((bi, lambda: P.op(
                        "pe", lambda e: e.matmul(pst[:, 0:n], lhsT=ones[:], rhs=sq[:, 0:n], start=(cnt == 0), stop=(cnt == 7)),
                        reads=[sqr, "ones"], writes=[pstr], signal=True)))
                self.pend_sq.append((bi, emit_sq))

            def flush(self, bi):
                mine = [fn for tag, fn in self.pend_sq if tag == bi]
                self.pend_sq = [(tag, fn) for tag, fn in self.pend_sq if tag != bi]
                for fn in mine:
                    fn()
                mine = [fn for tag, fn in self.pend_mm if tag == bi]
                self.pend_mm = [(tag, fn) for tag, fn in self.pend_mm if tag != bi]
                for fn in mine:
                    fn()

            def finish_block(self, b, gi, final=False):
                first = (b["idx"] == self.blocks[0]["idx"])
                last = (b["idx"] == self.blocks[-1]["idx"])
                if first:
                    P.op("act", lambda e: e.activation(out=scr[:, 1:2], in_=scr[:, 0:1], func=AF.Ln), reads=["scr"], writes=["scr1"])
                self.flush(b["idx"])
                c0, n, bi = b["c0"], b["n"], b["idx"]
                assert self.count[bi] == 8
                pst, pstr = self.bank[bi]
                P.op("act", lambda e: e.activation(out=rs[:, c0:c0 + n], in_=pst[:, 0:n], func=AF.Ln, bias=epsT[:, 0:1]),
                     reads=[pstr, "epsT"], writes=[("rs", bi)])
                psR.release(pstr)
                P.op("act", lambda e: e.activation(out=rs[:, c0:c0 + n], in_=rs[:, c0:c0 + n], func=AF.Exp, scale=-0.5),
                     reads=[("rs", bi)], writes=[("rs", bi)])
                if last:
                    P.op("act", lambda e: e.activation(out=scr[:, 1:2], in_=scr[:, 0:1], func=AF.Tanh), reads=["scr"], writes=["scr1"])
                if final:
                    return
                for k in range(8):
                    P.op("dve", lambda e, k=k: e.scalar_tensor_tensor(
                        out=xn[:, k, c0:c0 + n], in0=h[:, k, c0:c0 + n], scalar=gcol(gi, k), in1=rs[:, c0:c0 + n],
                        op0=ALU.mult, op1=ALU.mult),
                        reads=[("h", k, bi), ("rs", bi), "cols"], writes=[("xn", k, bi)])

        def proj(pt, ptr, wv, wres, u, src, src_name, b, nk=8):
            c0, n = b["c0"], b["n"]
            for k in range(nk):
                P.op("pe", lambda e, k=k: e.matmul(pt[:, 0:n], lhsT=wv[:, k, u * 128:(u + 1) * 128], rhs=src[:, k, c0:c0 + n],
                                                   start=(k == 0), stop=(k == nk - 1)),
                     reads=wres + [(src_name, k, b["idx"])], writes=[ptr], signal=(k == nk - 1))

        def store_colsT(src_flat, ncol, src_reads, dst_rows_ap):
            for a0 in range(0, ncol, 512):
                w = min(512, ncol - a0)
                ob, obr = bigR.get()
                pz, pzr = psR.get()
                nt = (w + 127) // 128
                for a in range(nt):
                    wa_ = min(128, w - a * 128)
                    P.op("pe", lambda e, a=a, wa_=wa_, a0=a0, pz=pz: e.transpose(
                        out=pz[0:wa_, a * 128:(a + 1) * 128], in_=src_flat[:, a0 + a * 128: a0 + a * 128 + wa_], identity=idf[:]),
                        reads=src_reads + ["idf"], writes=[pzr], signal=(a == nt - 1))
                full = w // 128
                if full:
                    P.op("dve", lambda e, pz=pz, ob=ob, full=full: e.tensor_copy(out=ob[:, 0:full * 128], in_=pz[:, 0:full * 128]),
                         reads=[pzr], writes=[obr])
                    outs_tok.append(P.dma("sp", (lambda e, ob=ob, a0=a0, full=full: e.dma_start(
                        out=dst_rows_ap[a0:a0 + full * 128, :].rearrange("(a q) p -> q a p", q=128),
                        in_=ob[:, 0:full * 128].rearrange("q (a p) -> q a p", a=full))), "st" + str(obr[1]), reads=[obr]))
                rem = w - full * 128
                if rem:
                    P.op("dve", lambda e, pz=pz, ob=ob, full=full, rem=rem: e.tensor_copy(
                        out=ob[0:rem, 512:640], in_=pz[0:rem, full * 128:(full + 1) * 128]), reads=[pzr], writes=[obr])
                    outs_tok.append(P.dma("sp", (lambda e, ob=ob, a0=a0, full=full, rem=rem: e.dma_start(
                        out=dst_rows_ap[a0 + full * 128: a0 + full * 128 + rem, :], in_=ob[0:rem, 512:640])),
                        "st" + str(obr[1]), reads=[obr]))

        for pss in range(2):
            if pss == 0:
                blocks = [dict(idx=0, c0=0, n=512, kind="p", first=True, last=False),
                          dict(idx=1, c0=512, n=512, kind="p", first=False, last=False)]
                ncols = 1024
            else:
                blocks = [dict(idx=0, c0=0, n=512, kind="p", first=False, last=False),
                          dict(idx=1, c0=512, n=512, kind="p", first=False, last=True),
                          dict(idx=2, c0=1024, n=NS, kind="s", first=False, last=False)]
                ncols = TT
            tiles = [dict(t=t, rows=128, c0=t * 128, blk=t // 4, src=x_d[pss * 1024 + t * 128: pss * 1024 + (t + 1) * 128, :],
                          psrc=p_d[pss * 1024 + t * 128: pss * 1024 + (t + 1) * 128, :]) for t in range(8)]
            if pss == 1:
                tiles.append(dict(t=8, rows=NS, c0=1024, blk=2, src=xs_d[:, :], psrc=ps_d[:, :]))

            STAGE_MARKS.append(("A", pss, len(P.ops["pe"])))
            if pss == 0:
                R.prefetch()
            nacc = NormAcc(blocks)
            prevb = None
            def p_cast(tl):
                if tl["t"] < 8:
                    pi, pir = pst_all[:, tl["t"], :], ("pst",)
                else:
                    pi, pir = pss_all, ("pss",)
                _, pb, _, pbr = pinR.get()
                rows = tl["rows"]
                P.op("dve", lambda e: e.tensor_copy(out=pb[0:rows, :], in_=pi[0:rows, :]), reads=[pir], writes=[pbr])
                return pb, pbr

            nxt_cast = p_cast(tiles[0])
            for ti, tl in enumerate(tiles):
                rows, c0, blk = tl["rows"], tl["c0"], tl["blk"]
                if tl["t"] < 8:
                    xt, xtr = xst[:, tl["t"], :], ("xst", tl["t"] // 2)
                else:
                    xt, xtr = xss, ("xss",)
                pb, pbr = nxt_cast
                if ti + 1 < len(tiles):
                    nxt_cast = p_cast(tiles[ti + 1])
                pz, pzr = psR.get()
                pzb = pz[:, :].bitcast(BF16)
                for kk in range(2):
                    P.op("pe", lambda e, kk=kk, pb=pb, pzb=pzb, rows=rows: e.transpose(
                        out=pzb[:, kk * 128: kk * 128 + rows], in_=pb[0:rows, kk * 128:(kk + 1) * 128], identity=idb[0:rows, 0:rows]),
                        reads=[pbr, "idb"], writes=[pzr], signal=(kk == 1))
                P.op("act", lambda e, pzb=pzb, c0=c0, rows=rows: e.copy(
                    out=pT[:, :, c0:c0 + rows], in_=pzb[:, 0:256].rearrange("p (k t) -> p k t", k=2)[:, :, 0:rows]),
                    reads=[pzr], writes=[("pT", blk)])
                for hf in range(2):
                    pz, pzr = psR.get()
                    for kk in range(4):
                        k = hf * 4 + kk
                        P.op("pe", lambda e, k=k, kk=kk, xt=xt, pz=pz, rows=rows: e.transpose(
                            out=pz[:, kk * 128: kk * 128 + rows], in_=xt[0:rows, k * 128:(k + 1) * 128], identity=idf[0:rows, 0:rows]),
                            reads=[xtr, "idf"], writes=[pzr], signal=(kk == 3))
                    eng = "act" if hf == 0 else "dve"
                    if eng == "act":
                        P.op("act", lambda e, hf=hf, pz=pz, c0=c0, rows=rows: e.copy(
                            out=h[:, hf * 4:(hf + 1) * 4, c0:c0 + rows], in_=pz[:, :].rearrange("p (k t) -> p k t", k=4)[:, :, 0:rows]),
                            reads=[pzr], writes=[("h", k, blk) for k in range(hf * 4, hf * 4 + 4)])
                    else:
                        P.op("dve", lambda e, hf=hf, pz=pz, c0=c0, rows=rows: e.tensor_copy(
                            out=h[:, hf * 4:(hf + 1) * 4, c0:c0 + rows], in_=pz[:, :].rearrange("p (k t) -> p k t", k=4)[:, :, 0:rows]),
                            reads=[pzr], writes=[("h", k, blk) for k in range(hf * 4, hf * 4 + 4)])
                if tl["t"] % 4 == 3 or tl["t"] == 8:
                    if prevb is not None:
                        nacc.finish_block(prevb, 0)
                    for k in range(8):
                        nacc.add(k, blocks[blk])
                    prevb = blocks[blk]
            nacc.finish_block(prevb, 0)

            STAGE_MARKS.append(("B", pss, len(P.ops["pe"])))
            P.op("pool", lambda e: e.memset(ssv[:], 0.0), reads=["rv"], writes=["ssv"])
            vit = [next_item() for _ in range(4)]
            for tl in tiles:
                t, rows, c0, blk = tl["t"], tl["rows"], tl["c0"], tl["blk"]
                for hf in range(2):
                    pz, pzr = psR.get()
                    for q2 in range(2):
                        q = hf * 2 + q2
                        _, wv, wres = vit[q]
                        for k in range(8):
                            P.op("pe", lambda e, k=k, q2=q2, wv=wv, pz=pz, c0=c0, rows=rows: e.matmul(
                                pz[0:rows, q2 * 256:(q2 + 1) * 256], lhsT=xn[:, k, c0:c0 + rows], rhs=wv[:, k, :],
                                start=(k == 0), stop=(k == 7)),
                                reads=wres + [("xn", k, blk)], writes=[pzr], signal=(k == 7 and q2 == 1))
                    P.op("act", lambda e, hf=hf, pz=pz, t=t, rows=rows: e.activation(
                        out=vn3[0:rows, t, hf * 512:(hf + 1) * 512], in_=pz[0:rows, :], func=AF.Gelu_apprx_tanh),
                        reads=[pzr], writes=[("vn", t)])
                jk, jkr = tmpR.get()
                jkb = jk[:, 0:512].bitcast(BF16)
                P.op("act", lambda e, t=t, rows=rows, jkb=jkb: e.activation(out=jkb[0:rows, :], in_=vn3[0:rows, t, :], func=AF.Square,
                                                                               scale=1.0 / 32.0, accum_out=ssv[0:rows, t:t + 1]),
                     reads=[("vn", t)], writes=[jkr, "ssv"])
            for (i, _, _) in vit:
                R.release(i)
            ntl = len(tiles)
            P.op("act", lambda e, ntl=ntl: e.activation(out=rv[:, 0:ntl], in_=ssv[:, 0:ntl], func=AF.Ln, bias=epsT[:, 0:1]), reads=["ssv", "epsT"], writes=["rv"])
            P.op("act", lambda e, ntl=ntl: e.activation(out=rv[:, 0:ntl], in_=rv[:, 0:ntl], func=AF.Exp, scale=-0.5), reads=["rv"], writes=["rv"])
            P.op("act", lambda e: e.activation(out=scr[:, 1:2], in_=scr[:, 0:1], func=AF.Tanh), reads=["scr"], writes=["scr1"])
            vn_jobs = []
            for tl in tiles:
                def vn_job(tl=tl):
                    t, rows = tl["t"], tl["rows"]
                    is_out = (pss == 1 and t >= 7)
                    if is_out:
                        ob, obr = bigR.get()
                        P.op("dve", lambda e: e.scalar_tensor_tensor(
                            out=ob[0:rows, :], in0=vn3[0:rows, t, :], scalar=rv[0:rows, t:t + 1], in1=gvbc[0:rows, :],
                            op0=ALU.mult, op1=ALU.mult), reads=[("vn", t), "rv", "gvbc"], writes=[obr])
                        dst = cv_d[:, :] if t == 7 else cvs_d[:, :]
                        outs_tok.append(P.dma("sp", (lambda e: e.dma_start(out=dst, in_=ob[0:rows, :])),
                                              "st" + str(obr[1]), reads=[obr]))
                    P.op("dve", lambda e: e.scalar_tensor_tensor(
                        out=vn3[0:rows, t, :], in0=vn3[0:rows, t, :], scalar=rv[0:rows, t:t + 1], in1=gvbc[0:rows, :],
                        op0=ALU.mult, op1=ALU.mult), reads=[("vn", t), "rv", "gvbc"], writes=[("vn", t)])
                vn_jobs.append(vn_job)

            if pss == 0:
                setup_spatial()
            STAGE_MARKS.append(("C", pss, len(P.ops["pe"])))
            if pss == 1:
                xcbs = R1f[:, 12544:12928]
            st_rounds = []
            if pss == 0:
                st_rounds = (load_rounds(sa_d.rearrange("s r (j p) -> (s r j) p", p=128), 2, saT_flat, "saT", "ld") +
                             load_rounds(sf_d.rearrange("s r (c p) -> (s r c) p", p=128), 11, sfT_flat, "sfT", "ld"))
                assert len(st_rounds) == 4
                st_rounds[0][0]()
            for jp in range(4):
                if st_rounds:
                    st_rounds[jp][1]()
                    if jp + 1 < 4:
                        st_rounds[jp + 1][0]()
                ix, wx, wxr = next_item()
                ib, wb, wbr = next_item()
                ic, wc, wcr = next_item()
                for u in range(2):
                    j = jp * 2 + u
                    for b in blocks:
                        c0, n, bi = b["c0"], b["n"], b["idx"]
                        if b["kind"] == "s":
                            pz, pzr = psR.get()
                            proj(pz[:, 0:NS], pzr, wx, wxr, u, xn, "xn", b)
                            proj(pz[:, NS:2 * NS], pzr, wc, wcr, u, xn, "xn", b)
                            proj(pz[:, 2 * NS:3 * NS], pzr, wb, wbr, u, xn, "xn", b)
                            P.op("act", lambda e, pz=pz, j=j: e.copy(out=xcbs[:, j * 48:(j + 1) * 48], in_=pz[:, 0:48]),
                                 reads=[pzr], writes=[("xcbs", j)])
                            continue
                        pX, pXr = psR.get(); pB, pBr = psR.get(); pC, pCr = psR.get()
                        proj(pX, pXr, wx, wxr, u, xn, "xn", b)
                        proj(pC, pCr, wc, wcr, u, xn, "xn", b)
                        proj(pB, pBr, wb, wbr, u, xn, "xn", b)
                        tx, txr = tmpR.get(); cx, cxr = tmpR.get(); acc, accr = tmpR.get()
                        P.op("act", lambda e, tx=tx, pX=pX, n=n: e.copy(out=tx[:, 0:n], in_=pX[:, 0:n]), reads=[pXr], writes=[txr])
                        if b["kind"] == "p":
                            if b["first"]:
                                P.op("pool", lambda e, cx=cx: e.memset(cx[:, 0:2], 0.0), writes=[cxr[0]])
                            else:
                                P.op("pool", lambda e, cx=cx, j=j: e.tensor_copy(out=cx[:, 0:2], in_=carry_a[:, :, j]),
                                     reads=[("carry_a", j)], writes=[cxr[0]])
                            P.op("dve", lambda e, cx=cx, pC=pC, tx=tx, n=n: e.tensor_tensor(out=cx[:, 2:2 + n], in0=pC[:, 0:n], in1=tx[:, 0:n], op=ALU.mult),
                                 reads=[pCr, txr], writes=[cxr[1]])
                            P.op("pool", lambda e, cx=cx, j=j, n=n: e.tensor_copy(out=carry_a[:, :, j], in_=cx[:, n:n + 2]),
                                 reads=[cxr[1]], writes=[("carry_a", j)])
                            P.op("act", lambda e, acc=acc, cx=cx, j=j, n=n: e.activation(out=acc[:, 0:n], in_=cx[:, 0:n], func=AF.Identity, scale=wca(0, j)),
                                 reads=[cxr, "cols"], writes=[accr])
                            P.op("dve", lambda e, acc=acc, cx=cx, j=j, n=n: e.scalar_tensor_tensor(
                                out=acc[:, 0:n], in0=cx[:, 1:1 + n], scalar=wca(1, j), in1=acc[:, 0:n], op0=ALU.mult, op1=ALU.add),
                                reads=[cxr, accr, "cols"], writes=[accr])
                            P.op("dve", lambda e, acc=acc, cx=cx, j=j, n=n: e.scalar_tensor_tensor(
                                out=acc[:, 0:n], in0=cx[:, 2:2 + n], scalar=wca(2, j), in1=acc[:, 0:n], op0=ALU.mult, op1=ALU.add),
                                reads=[cxr[1], accr, "cols"], writes=[accr])
                        else:
                            P.op("dve", lambda e, cx=cx, pC=pC, tx=tx, n=n: e.tensor_tensor(out=cx[:, 0:n], in0=pC[:, 0:n], in1=tx[:, 0:n], op=ALU.mult),
                                 reads=[pCr, txr], writes=[cxr])
                            P.op("pool", lambda e, j=j, cx=cx, n=n: e.tensor_copy(out=oAs[:, :, 1, j], in_=cx[:, 0:n]), reads=[cxr], writes=[("oAs", j)])
                            P.op("act", lambda e, acc=acc, j=j, n=n: e.activation(out=acc[:, 0:n], in_=saT[:, :, 0, j], func=AF.Identity, scale=wca(0, j)),
                                 reads=["saT", "cols"], writes=[accr])
                            P.op("dve", lambda e, acc=acc, j=j, n=n: e.scalar_tensor_tensor(
                                out=acc[:, 0:n], in0=saT[:, :, 1, j], scalar=wca(1, j), in1=acc[:, 0:n], op0=ALU.mult, op1=ALU.add),
                                reads=["saT", accr, "cols"], writes=[accr])
                            P.op("dve", lambda e, acc=acc, cx=cx, j=j, n=n: e.scalar_tensor_tensor(
                                out=acc[:, 0:n], in0=cx[:, 0:n], scalar=wca(2, j), in1=acc[:, 0:n], op0=ALU.mult, op1=ALU.add),
                                reads=[cxr, accr, "cols"], writes=[accr])
                        P.op("dve", lambda e, acc=acc, pB=pB, j=j, c0=c0, n=n: e.tensor_tensor(out=ab[:, j, c0:c0 + n], in0=pB[:, 0:n], in1=acc[:, 0:n], op=ALU.mult),
                             reads=[pBr, accr], writes=[("ab", j, bi)])
                        if vn_jobs:
                            vn_jobs.pop(0)()
                R.release(ix); R.release(ib); R.release(ic)
            while vn_jobs:
                vn_jobs.pop(0)()
            if pss == 1:
                sblk = blocks[2]
                sc0 = sblk["c0"]
                x4 = xcbs[:, 0:8 * 48].rearrange("p (j a s) -> p j a s", j=8, a=3)
                xres = [("xcbs", j) for j in range(8)]
                ores = [("oAs", j) for j in range(8)]
                cxv = oAs[:, :, 1, :].rearrange("p s j -> p j s")
                acc, accr = tmpR.get(); t1, t1r = tmpR.get()
                a3 = acc[:, 0:8 * NS].rearrange("p (j s) -> p j s", j=8)
                t3 = t1[:, 0:8 * NS].rearrange("p (j s) -> p j s", j=8)

                def wav(tap):
                    w = cols1[:, 32 + tap * 8: 32 + tap * 8 + 8]
                    pr = [list(x) for x in w.ap]
                    return bass.AP(w.tensor, w.offset, [pr[0], pr[1], [0, NS]])
                P.op("dve", lambda e, cxv=cxv, x4=x4, a3=a3, t3=t3, wav=wav, sc0=sc0: e.tensor_tensor(out=cxv, in0=x4[:, :, 1, :], in1=x4[:, :, 0, :], op=ALU.mult),
                     reads=xres, writes=ores)
                P.op("dve", lambda e, cxv=cxv, x4=x4, a3=a3, t3=t3, wav=wav, sc0=sc0: e.tensor_tensor(out=a3, in0=saT[:, :, 0, :].rearrange("p s j -> p j s"), in1=wav(0), op=ALU.mult),
                     reads=["saT", "cols"], writes=[accr])
                P.op("dve", lambda e, cxv=cxv, x4=x4, a3=a3, t3=t3, wav=wav, sc0=sc0: e.tensor_tensor(out=t3, in0=saT[:, :, 1, :].rearrange("p s j -> p j s"), in1=wav(1), op=ALU.mult),
                     reads=["saT", "cols"], writes=[t1r])
                P.op("dve", lambda e, cxv=cxv, x4=x4, a3=a3, t3=t3, wav=wav, sc0=sc0: e.tensor_tensor(out=a3, in0=a3, in1=t3, op=ALU.add), reads=[accr, t1r], writes=[accr])
                P.op("dve", lambda e, cxv=cxv, x4=x4, a3=a3, t3=t3, wav=wav, sc0=sc0: e.tensor_tensor(out=t3, in0=cxv, in1=wav(2), op=ALU.mult), reads=ores + ["cols"], writes=[t1r])
                P.op("dve", lambda e, cxv=cxv, x4=x4, a3=a3, t3=t3, wav=wav, sc0=sc0: e.tensor_tensor(out=a3, in0=a3, in1=t3, op=ALU.add), reads=[accr, t1r], writes=[accr])
                P.op("dve", lambda e, cxv=cxv, x4=x4, a3=a3, t3=t3, wav=wav, sc0=sc0: e.tensor_tensor(out=ab[:, :, sc0:sc0 + NS], in0=x4[:, :, 2, :], in1=a3, op=ALU.mult),
                     reads=xres + [accr], writes=[("ab", j, 2) for j in range(8)])

            if pss == 0:
                P.op("pool", lambda e: e.tensor_copy(out=oAs[:, :, 0, :], in_=saT[:, :, 1, :]), reads=["saT"], writes=[("oAs", j) for j in range(8)])
                P.op("pool", lambda e: e.tensor_copy(out=oFs[:, :, 0, :], in_=sfT[:, :, 1, :]), reads=["sfT"], writes=[("oFs", c) for c in range(44)])
            STAGE_MARKS.append(("D", pss, len(P.ops["pe"])))
            for jp in range(4):
                iu, wu, wur = next_item()
                for u in range(2):
                    j = jp * 2 + u
                    for b in blocks:
                        c0, n, bi = b["c0"], b["n"], b["idx"]
                        pU, pUr = psR.get(); pS, pSr = psR.get()
                        proj(pU, pUr, wu, wur, u, xn, "xn", b)
                        tu, tur = tmpR.get()
                        P.op("act", lambda e, tu=tu, pU=pU, n=n: e.activation(out=tu[:, 0:n], in_=pU[:, 0:n], func=AF.Gelu_apprx_tanh),
                             reads=[pUr], writes=[tur])
                        if b["kind"] == "p":
                            for q in range(4):
                                t = c0 // 128 + q
                                P.op("pe", lambda e, q=q, t=t, j=j, pS=pS: e.matmul(
                                    pS[:, q * 128:(q + 1) * 128], lhsT=vn3[:, t, j * 128:(j + 1) * 128], rhs=WsT[:, j, :], start=True, stop=True),
                                    reads=[("vn", t), "WsT"], writes=[pSr], signal=(q == 3))
                            ts_, tsr = tmpR.get()
                            P.op("dve", lambda e, ts_=ts_, pS=pS, j=j: e.tensor_tensor(
                                out=ts_[:, 0:512].rearrange("p (a t) -> p a t", a=4), in0=pS[:, :].rearrange("p (a t) -> p a t", a=4),
                                in1=bcast_mid(bbc[:, j, :], 4), op=ALU.add), reads=[pSr, "bbc"], writes=[tsr])
                            P.op("dve", lambda e, ts_=ts_, tu=tu, j=j, c0=c0, n=n: e.tensor_tensor(
                                out=us[:, j, c0:c0 + n], in0=ts_[:, 0:n], in1=tu[:, 0:n], op=ALU.mult),
                                reads=[tsr, tur], writes=[("us", j, bi)])
                        else:
                            P.op("pe", lambda e, j=j, pS=pS, n=n: e.matmul(
                                pS[:, 0:n], lhsT=vn3[0:NS, 8, j * 128:(j + 1) * 128], rhs=Ds[:, j, :], start=True, stop=True),
                                reads=[("vn", 8), "Ds"], writes=[pSr])
                            P.op("dve", lambda e, pS=pS, tu=tu, j=j, c0=c0, n=n: e.scalar_tensor_tensor(
                                out=us[:, j, c0:c0 + n], in0=pS[:, 0:n], scalar=bbc[:, j, 0:1], in1=tu[:, 0:n], op0=ALU.add, op1=ALU.mult),
                                reads=[pSr, tur, "bbc"], writes=[("us", j, bi)])
                R.release(iu)

            if pss == 1:
                store_colsT(carry_a[:].rearrange("p r j -> p (r j)"), 16, [("carry_a", j) for j in range(8)],
                            ca_d.rearrange("r (j p) -> (r j) p", p=128))
                store_colsT(oAs[:].rearrange("p s r j -> p (s r j)"), 256, [("oAs", j) for j in range(8)],
                            cas_d.rearrange("s r (j p) -> (s r j) p", p=128))
            STAGE_MARKS.append(("E", pss, len(P.ops["pe"])))
            for ip in range(4):
                iga, wga, wgar = next_item()
                igb, wgb, wgbr = next_item()
                ioa, woa, woar = next_item()
                iob, wob, wobr = next_item()
                for u in range(2):
                    i = ip * 2 + u
                    for b in blocks:
                        c0, n, bi = b["c0"], b["n"], b["idx"]
                        pGA, pGAr = psR.get(); pGB, pGBr = psR.get(); pYA, pYAr = psR.get(); pYB, pYBr = psR.get()
                        proj(pGA, pGAr, wga, wgar, u, xn, "xn", b)
                        proj(pGB, pGBr, wgb, wgbr, u, xn, "xn", b)
                        proj(pYA, pYAr, woa, woar, u, ab, "ab", b)
                        proj(pYB, pYBr, wob, wobr, u, us, "us", b)
                        ta, tar = tmpR.get(); tb, tbr = tmpR.get()
                        P.op("act", lambda e, ta=ta, pGA=pGA, n=n: e.activation(out=ta[:, 0:n], in_=pGA[:, 0:n], func=AF.Tanh, scale=0.5),
                             reads=[pGAr], writes=[tar])
                        P.op("act", lambda e, tb=tb, pGB=pGB, n=n: e.activation(out=tb[:, 0:n], in_=pGB[:, 0:n], func=AF.Tanh, scale=0.5),
                             reads=[pGBr], writes=[tbr])
                        P.op("dve", lambda e, ta=ta, pYA=pYA, n=n: e.scalar_tensor_tensor(
                            out=ta[:, 0:n], in0=ta[:, 0:n], scalar=1.0, in1=pYA[:, 0:n], op0=ALU.add, op1=ALU.mult),
                            reads=[tar, pYAr], writes=[tar])
                        P.op("dve", lambda e, tb=tb, pYB=pYB, n=n: e.scalar_tensor_tensor(
                            out=tb[:, 0:n], in0=tb[:, 0:n], scalar=1.0, in1=pYB[:, 0:n], op0=ALU.add, op1=ALU.mult),
                            reads=[tbr, pYBr], writes=[tbr])
                        P.op("pool", lambda e, ta=ta, tb=tb, i=i, c0=c0, n=n: e.tensor_tensor(
                            out=m3[:, i, c0:c0 + n], in0=ta[:, 0:n], in1=tb[:, 0:n], op=ALU.add),
                            reads=[tar, tbr], writes=[("m", i, bi)])
                for it in (iga, igb, ioa, iob):
                    R.release(it)

            STAGE_MARKS.append(("F", pss, len(P.ops["pe"])))
            nacc = NormAcc(blocks)
            wos = [next_item() for _ in range(4)]
            for b in blocks:
                c0, n, bi = b["c0"], b["n"], b["idx"]
                for ip in range(4):
                    io, wo, wor = wos[ip]
                    for u in range(2):
                        i = ip * 2 + u
                        pO, pOr = psR.get()
                        proj(pO, pOr, wo, wor, u, m3, "m", b)
                        P.op("dve", lambda e, pO=pO, i=i, c0=c0, n=n: e.scalar_tensor_tensor(
                            out=h[:, i, c0:c0 + n], in0=pO[:, 0:n], scalar=0.5, in1=h[:, i, c0:c0 + n], op0=ALU.mult, op1=ALU.add),
                            reads=[pOr, ("h", i, bi)], writes=[("h", i, bi)])
                        nacc.add(i, b)
                    if ip == 0 and bi == 1:
                        nacc.finish_block(blocks[0], 1)
            for (io, _, _) in wos:
                R.release(io)
            for b in blocks[1:]:
                nacc.finish_block(b, 1)

            STAGE_MARKS.append(("G", pss, len(P.ops["pe"])))
            deferredG = [None]
            pblocks = [b_ for b_ in blocks if b_["kind"] == "p"]
            sblocks = [b_ for b_ in blocks if b_["kind"] == "s"]
            NP2 = 1024
            accR = Rot([(big[i], [("big", i)]) for i in range(4)] + [(gvbc, ["gvbc"]), (gfbc, ["gfbc"])])
            for cp in range(11):
                ia, wa, war = next_item()
                ib_, wb_, wbr_ = next_item()
                for u in range(2):
                    c = cp * 2 + u
                    banks = []
                    for b in pblocks:
                        pA, pAr = psR.get(); pB, pBr = psR.get()
                        proj(pA, pAr, wa, war, u, xn, "xn", b)
                        proj(pB, pBr, wb_, wbr_, u, xn, "xn", b)
                        banks.append(((pA, pAr), (pB, pBr)))
                    accs = []
                    for half in range(2):
                        cc = c + 22 * half
                        ux, uxr = wideR.get(); acc, accr = accR.get()
                        P.op("act", lambda e, ux=ux, cc=cc: e.copy(out=ux[:, 0:2], in_=carry_f[:, :, cc]),
                             reads=[("carry_f", cc), "carry_f"], writes=uxr)
                        for q, b in enumerate(pblocks):
                            pp, ppr = banks[q][half]
                            P.op("act", lambda e, ux=ux, pp=pp, q=q: e.copy(out=ux[:, 2 + q * 512: 2 + (q + 1) * 512], in_=pp[:, 0:512]),
                                 reads=[ppr], writes=uxr)
                            P.op("act", lambda e, acc=acc, pp=pp, cc=cc, q=q: e.activation(
                                out=acc[:, q * 512:(q + 1) * 512], in_=pp[:, 0:512], func=AF.Identity, scale=wcf(2, cc)),
                                reads=[ppr, "cols"], writes=accr)
                        P.op("pool", lambda e, ux=ux, cc=cc: e.tensor_copy(out=carry_f[:, :, cc], in_=ux[:, NP2:NP2 + 2]),
                             reads=uxr, writes=[("carry_f", cc)])
                        for tap in (0, 1):
                            P.op("dve", lambda e, acc=acc, ux=ux, cc=cc, tap=tap: e.scalar_tensor_tensor(
                                out=acc[:, 0:NP2], in0=ux[:, tap:tap + NP2], scalar=wcf(tap, cc), in1=acc[:, 0:NP2], op0=ALU.mult, op1=ALU.add),
                                reads=uxr + accr + ["cols"], writes=accr)
                        accs.append((acc, accr))
                    (aa, aar), (ab_, abr_) = accs

                    def phase2(aa=aa, aar=aar, ab_=ab_, abr_=abr_, c=c):
                        P.op("act", lambda e: e.activation(out=aa[:, 0:NP2], in_=aa[:, 0:NP2], func=AF.Gelu_apprx_tanh),
                             reads=aar, writes=aar)
                        for q in range(2):
                            P.op("pool", lambda e, q=q: e.tensor_tensor(
                                out=act[:, c, q * 512:(q + 1) * 512], in0=aa[:, q * 512:(q + 1) * 512], in1=ab_[:, q * 512:(q + 1) * 512], op=ALU.mult),
                                reads=aar + abr_, writes=[("act", c, q)])
                    if deferredG[0] is not None:
                        deferredG[0]()
                    deferredG[0] = phase2
                    for b in sblocks:
                        pz, pzr = psR.get()
                        proj(pz[:, 0:NS], pzr, wa, war, u, xn, "xn", b)
                        proj(pz[:, NS:2 * NS], pzr, wb_, wbr_, u, xn, "xn", b)
                        P.op("act", lambda e, pz=pz, c=c: e.copy(
                            out=bass.AP(oFs[:, 0, 1, c:c + 1].tensor, oFs[:, 0, 1, c:c + 1].offset, [[NS * 2 * 44, 128], [22, 2], [88, NS]]),
                            in_=pz[:, 0:2 * NS].rearrange("p (a s) -> p a s", a=2)),
                            reads=[pzr], writes=[("oFs", c), ("oFs", c + 22)])
                R.release(ia); R.release(ib_)
            if deferredG[0] is not None:
                deferredG[0]()
                deferredG[0] = None
            P.dma("sp", (lambda e: e.dma_start(out=gfbc[:], in_=g_final.partition_broadcast(128))), "rl0", writes=["gfbc"])
            if pss == 0:
                P.dma("sp", (lambda e: e.dma_start(out=gvbc[:], in_=g_v.partition_broadcast(128))), "rl1", writes=["gvbc"])
                P.dma("sp", (lambda e: e.dma_start(out=bbcw[:, 0:1024], in_=b_spatial.rearrange("j t -> (j t)").partition_broadcast(128))),
                      "rl2", writes=["bbc"])
            if pss == 1:
                sblk = blocks[2]
                sc0 = sblk["c0"]
                accs = []
                for half in range(2):
                    acc, accr = tmpR.get(); t1, t1r = tmpR.get()
                    a3 = acc[:, 0:22 * NS].rearrange("p (c s) -> p c s", c=22)
                    t3 = t1[:, 0:22 * NS].rearrange("p (c s) -> p c s", c=22)

                    def wv(tap, half=half):
                        o = tap * 44 + half * 22
                        w = wcfT[:, o:o + 22]
                        pr = [list(x) for x in w.ap]
                        return bass.AP(w.tensor, w.offset, [pr[0], pr[1], [0, NS]])

                    def sv(r, half=half):
                        return sfT[:, :, r, half * 22:(half + 1) * 22].rearrange("p s c -> p c s") if r < 2 else \
                            oFs[:, :, 1, half * 22:(half + 1) * 22].rearrange("p s c -> p c s")
                    ores = [("oFs", cq) for cq in range(half * 22, half * 22 + 22)]
                    P.op("dve", lambda e, a3=a3, sv=sv, wv=wv: e.tensor_tensor(out=a3, in0=sv(0), in1=wv(0), op=ALU.mult),
                         reads=["sfT", "wcfT"], writes=[accr])
                    P.op("dve", lambda e, t3=t3, sv=sv, wv=wv: e.tensor_tensor(out=t3, in0=sv(1), in1=wv(1), op=ALU.mult),
                         reads=["sfT", "wcfT"], writes=[t1r])
                    P.op("dve", lambda e, a3=a3, t3=t3: e.tensor_tensor(out=a3, in0=a3, in1=t3, op=ALU.add), reads=[accr, t1r], writes=[accr])
                    P.op("dve", lambda e, t3=t3, sv=sv, wv=wv: e.tensor_tensor(out=t3, in0=sv(2), in1=wv(2), op=ALU.mult),
                         reads=ores + ["wcfT"], writes=[t1r])
                    P.op("dve", lambda e, a3=a3, t3=t3: e.tensor_tensor(out=a3, in0=a3, in1=t3, op=ALU.add), reads=[accr, t1r], writes=[accr])
                    accs.append((a3, accr))
                (aa3, aar), (ab3, abr) = accs
                P.op("act", lambda e, aa3=aa3: e.activation(out=aa3, in_=aa3, func=AF.Gelu_apprx_tanh), reads=[aar], writes=[aar])
                P.op("dve", lambda e, aa3=aa3, ab3=ab3, sc0=sc0: e.tensor_tensor(out=act[:, :, sc0:sc0 + NS], in0=aa3, in1=ab3, op=ALU.mult),
                     reads=[aar, abr], writes=[("act", cq, 2) for cq in range(22)])

            STAGE_MARKS.append(("H", pss, len(P.ops["pe"])))
            nacc = NormAcc(blocks)

            def h_part(i, b, wd, wdr, pD, pDr, c_lo, c_hi):
                c0, n, bi = b["c0"], b["n"], b["idx"]
                for c in range(c_lo, c_hi):
                    P.op("pe", lambda e, c=c: e.matmul(
                        pD[:, 0:n], lhsT=wd[:, c, :], rhs=act[:, c, c0:c0 + n], start=(c == 0), stop=(c == 21)),
                        reads=wdr + [("act", c, bi)], writes=[pDr] + (["R1busy"] if c == 21 else []), signal=(c == 21))

            def h_tail(i, b, pD, pDr):
                c0, n, bi = b["c0"], b["n"], b["idx"]
                P.op("dve", lambda e: e.tensor_tensor(
                    out=h[:, i, c0:c0 + n], in0=pD[:, 0:n], in1=h[:, i, c0:c0 + n], op=ALU.add),
                    reads=[pDr, ("h", i, bi)], writes=[("h", i, bi)])
                nacc.add(i, b)

            NLATE = 2
            first_units = [next_item() for _ in range(2)]
            grp = []
            for i, (idn, wd, wdr) in enumerate(first_units):
                for b in blocks:
                    if b["kind"] != "p":
                        continue
                    pD, pDr = psR.get()
                    h_part(i, b, wd, wdr, pD, pDr, 0, 22 - NLATE)
                    grp.append((i, b, wd, wdr, pD, pDr))
            for (i, b, wd, wdr, pD, pDr) in grp:
                h_part(i, b, wd, wdr, pD, pDr, 22 - NLATE, 22)
                h_tail(i, b, pD, pDr)
            for i, (idn, wd, wdr) in enumerate(first_units):
                for b in blocks:
                    if b["kind"] == "p":
                        continue
                    pD, pDr = psR.get()
                    h_part(i, b, wd, wdr, pD, pDr, 0, 22)
                    h_tail(i, b, pD, pDr)
                R.release(idn)
            for i in range(2, 6):
                idn, wd, wdr = next_item()
                for b in blocks:
                    pD, pDr = psR.get()
                    h_part(i, b, wd, wdr, pD, pDr, 0, 22)
                    h_tail(i, b, pD, pDr)
                R.release(idn)
            last_units = [next_item() for _ in range(2)]
            prevb = None
            for b in blocks:
                for i, (idn, wd, wdr) in zip((6, 7), last_units):
                    pD, pDr = psR.get()
                    h_part(i, b, wd, wdr, pD, pDr, 0, 22)
                    h_tail(i, b, pD, pDr)
                if prevb is not None:
                    nacc.finish_block(prevb, 2)
                prevb = b
            for (idn, _, _) in last_units:
                R.release(idn)

            if pss == 1:
                store_colsT(carry_f[:].rearrange("p r c -> p (r c)"), 88, [("carry_f", c) for c in range(44)],
                            cf_d.rearrange("r (c p) -> (r c) p", p=128))
                store_colsT(oFs[:].rearrange("p s r c -> p (s r c)"), 1408, [("oFs", c) for c in range(44)],
                            cfs_d.rearrange("s r (c p) -> (s r c) p", p=128))
            if pss == 0:
                stage_inputs(1)
            nacc.finish_block(prevb, 2)

            STAGE_MARKS.append(("I", pss, len(P.ops["pe"])))
            nacc = NormAcc(blocks)
            early_done = set()
            ipl, wpl, wplr = next_item()
            wgs = [next_item() for _ in range(4)]
            for b in blocks:
                c0, n, bi = b["c0"], b["n"], b["idx"]
                for ip in range(4):
                    ig, wg, wgr = wgs[ip]
                    for u in range(2):
                        i = ip * 2 + u
                        pG, pGr = psR.get(); pE, pEr = psR.get()
                        proj(pG, pGr, wg, wgr, u, xn, "xn", b)
                        for k in range(2):
                            P.op("pe", lambda e, k=k, pE=pE, i=i, c0=c0, n=n, wpl=wpl: e.matmul(
                                pE[:, 0:n], lhsT=wpl[:, k, i * 128:(i + 1) * 128], rhs=pT[:, k, c0:c0 + n], start=(k == 0), stop=(k == 1)),
                                reads=wplr + [("pT", bi)], writes=[pEr], signal=(k == 1))
                        tg, tgr = tmpR.get()
                        P.op("act", lambda e, tg=tg, pG=pG, n=n: e.activation(out=tg[:, 0:n], in_=pG[:, 0:n], func=AF.Tanh, scale=0.5),
                             reads=[pGr], writes=[tgr])
                        P.op("dve", lambda e, tg=tg, pE=pE, n=n: e.scalar_tensor_tensor(
                            out=tg[:, 0:n], in0=tg[:, 0:n], scalar=1.0, in1=pE[:, 0:n], op0=ALU.add, op1=ALU.mult),
                            reads=[tgr, pEr], writes=[tgr])
                        P.op("dve", lambda e, tg=tg, i=i, c0=c0, n=n: e.scalar_tensor_tensor(
                            out=h[:, i, c0:c0 + n], in0=tg[:, 0:n], scalar=0.5, in1=h[:, i, c0:c0 + n], op0=ALU.mult, op1=ALU.add),
                            reads=[tgr, ("h", i, bi)], writes=[("h", i, bi)])
                        nacc.add(i, b)
                    if ip == 0 and b["idx"] == 2:
                        nacc.finish_block(blocks[1], 3, final=True)
                        early_done.add(1)
                    if ip == 1 and b["idx"] == 1:
                        nacc.finish_block(blocks[0], 3, final=True)
                        early_done.add(0)
            for (ig, _, _) in wgs:
                R.release(ig)
            R.release(ipl)

            STAGE_MARKS.append(("O", pss, len(P.ops["pe"])))
            for b in blocks:
                btiles = [tl for tl in tiles if tl["blk"] == b["idx"]]
                if b["idx"] not in early_done:
                    nacc.finish_block(b, 3, final=True)

                def transposes(tl):
                    t, rows, c0, blk = tl["t"], tl["rows"], tl["c0"], tl["blk"]
                    banks = []
                    for hf in range(2):
                        pz, pzr = psR.get()
                        for kk in range(4):
                            k = hf * 4 + kk
                            P.op("pe", lambda e, k=k, kk=kk, pz=pz, rows=rows, c0=c0: e.transpose(
                                out=pz[0:rows, kk * 128:(kk + 1) * 128], in_=h[:, k, c0:c0 + rows], identity=idf[:]),
                                reads=[("h", k, blk), "idf"], writes=[pzr], signal=(kk == 3))
                        banks.append((pz, pzr))
                    return banks

                pend = transposes(btiles[0])
                pr, prr = psR.get()
                for q, tl in enumerate(btiles):
                    P.op("pe", lambda e, q=q, tl=tl, pr=pr: e.transpose(
                        out=pr[0:tl["rows"], q:q + 1], in_=rs[0:1, tl["c0"]:tl["c0"] + tl["rows"]], identity=idf[0:1, 0:1]),
                        reads=[("rs", b["idx"]), "idf"], writes=[prr], signal=(q == len(btiles) - 1))
                t0_ = btiles[0]["t"]
                rws = btiles[0]["rows"]
                P.op("dve", lambda e, t0_=t0_, nq=len(btiles), rws=rws, pr=pr: e.tensor_copy(out=rtok[0:rws, t0_:t0_ + nq], in_=pr[0:rws, 0:nq]),
                     reads=[prr], writes=[("rtok", b["idx"])])
                for q, tl in enumerate(btiles):
                    t, rows = tl["t"], tl["rows"]
                    banks = pend
                    if q + 1 < len(btiles):
                        pend = transposes(btiles[q + 1])
                    ob, obr = bigR.get()
                    for hf, (pz, pzr) in enumerate(banks):
                        P.op("dve", lambda e, pz=pz, ob=ob, hf=hf, t=t, rows=rows: e.scalar_tensor_tensor(
                            out=ob[0:rows, hf * 512:(hf + 1) * 512], in0=pz[0:rows, :], scalar=rtok[0:rows, t:t + 1],
                            in1=gfbc[0:rows, hf * 512:(hf + 1) * 512], op0=ALU.mult, op1=ALU.mult),
                            reads=[pzr, ("rtok", b["idx"]), "gfbc"], writes=[obr])
                    dst = ys_d[:, :] if t == 8 else y_d[pss * 1024 + t * 128: pss * 1024 + (t + 1) * 128, :]
                    outs_tok.append(P.dma("sp", (lambda e, ob=ob, rows=rows, dst=dst: e.dma_start(out=dst, in_=ob[0:rows, :])),
                                          "st" + str(obr[1]), reads=[obr]))

        P.final_wait("sp", outs_tok)

        sems = {n: es.enter_context(nc.semaphore(n)) for n in P.sem_names()}
        with nc.Block() as block:
            P.emit(block, sems)
    return nc


_CACHE = {}
STAGE_MARKS = []


def make_in_maps(x_prompt, x_sample, p_prompt, p_sample, state_conv_a, state_conv_ffn,
                 g_mix, w_in, w_conv_a, w_out_a, g_v, w_spatial, b_spatial, w_out_b, w_o,
                 g_ffn, w_up, w_conv_ffn, w_down, g_ple, w_ple_gate, w_ple, g_final):
    f = lambda a: np.ascontiguousarray(np.asarray(a, dtype=np.float32))
    shared = {
        "g_mix": f(g_mix[0]), "g_v": f(g_v[0]), "g_ffn": f(g_ffn[0]), "g_ple": f(g_ple[0]), "g_final": f(g_final),
        "w_in": f(w_in[0]), "w_conv_a": f(w_conv_a[0]), "w_out_a": f(w_out_a[0]), "w_out_b": f(w_out_b[0]),
        "w_o": f(w_o[0]), "w_spatial": f(w_spatial[0]), "b_spatial": f(b_spatial[0]), "w_up": f(w_up[0]),
        "w_conv_ffn": f(w_conv_ffn[0]), "w_down": f(w_down[0]), "w_ple_gate": f(w_ple_gate[0]), "w_ple": f(w_ple[0]),
        "ident": np.eye(128, dtype=np.float32), "tril": np.tril(np.ones((128, 128), dtype=np.float32)),
    }
    in_maps = []
    for c in range(NCORES):
        sl = slice(c * NS, (c + 1) * NS)
        d = dict(shared)
        d["x"] = f(x_prompt[c]); d["xs"] = f(x_sample[sl, 0])
        d["p"] = f(p_prompt[0, c]); d["ps"] = f(p_sample[0, sl, 0])
        d["sa"] = f(state_conv_a[0, sl]); d["sf"] = f(state_conv_ffn[0, sl])
        in_maps.append(d)
    return in_maps


def kernel(x_prompt, x_sample, p_prompt, p_sample, state_conv_a, state_conv_ffn,
           g_mix, w_in, w_conv_a, w_out_a, g_v, w_spatial, b_spatial, w_out_b, w_o,
           g_ffn, w_up, w_conv_ffn, w_down, g_ple, w_ple_gate, w_ple, g_final):
    if "nc" not in _CACHE:
        _CACHE["nc"] = build_program()
    nc = _CACHE["nc"]
    in_maps = make_in_maps(x_prompt, x_sample, p_prompt, p_sample, state_conv_a, state_conv_ffn,
                           g_mix, w_in, w_conv_a, w_out_a, g_v, w_spatial, b_spatial, w_out_b, w_o,
                           g_ffn, w_up, w_conv_ffn, w_down, g_ple, w_ple_gate, w_ple, g_final)
    res = run_bass_kernel_spmd(nc, in_maps, core_ids=list(range(NCORES)))
    rs_ = res.results
    y_prompt = np.stack([rs_[c]["y"] for c in range(NCORES)])
    y_sample = np.concatenate([rs_[c]["ys"] for c in range(NCORES)])[:, None, :]
    ca_p = np.stack([rs_[c]["ca"] for c in range(NCORES)])[None]
    ca_s = np.concatenate([rs_[c]["cas"] for c in range(NCORES)])[None]
    cv_p = np.stack([rs_[c]["cv"] for c in range(NCORES)])[None]
    cv_s = np.concatenate([rs_[c]["cvs"] for c in range(NCORES)])[None, :, None, :]
    cf_p = np.stack([rs_[c]["cf"] for c in range(NCORES)])[None]
    cf_s = np.concatenate([rs_[c]["cfs"] for c in range(NCORES)])[None]
    return (y_prompt.astype(np.float32), y_sample.astype(np.float32), ca_p.astype(np.float32), ca_s.astype(np.float32),
            cv_p.astype(np.float32), cv_s.astype(np.float32), cf_p.astype(np.float32), cf_s.astype(np.float32))
```

```python
import contextlib
import numpy as np
import concourse.bass as bass
import concourse.mybir as mybir
from concourse.bass_utils import run_bass_kernel_spmd

F32 = mybir.dt.float32
BF16 = mybir.dt.bfloat16
AF = mybir.ActivationFunctionType
ALU = mybir.AluOpType

D = 1024
SEQ = 2048
NS = 16
DFF = 2816
NIN = 7168
PD = 256
EPS = 1e-6
NCORES = 8
TT = 1040


class Prog:
    ENGS = ("pe", "act", "dve", "pool", "sp")

    def __init__(self):
        self.ops = {e: [] for e in self.ENGS}
        self.ticket = {e: 0 for e in self.ENGS}
        self.pending = {e: False for e in self.ENGS}
        self.last_write = {}
        self.readers = {}
        self.known = {e: {} for e in self.ENGS}
        self.dma_count = {}

    def _deps(self, eng, reads, writes):
        deps = {}
        for r in reads:
            t = self.last_write.get(r)
            if t is not None and deps.get(t[0], 0) < t[1]:
                deps[t[0]] = t[1]
        for w in writes:
            t = self.last_write.get(w)
            if t is not None and deps.get(t[0], 0) < t[1]:
                deps[t[0]] = t[1]
            rd = self.readers.get(w)
            if rd:
                for k, v in rd.items():
                    if deps.get(k, 0) < v:
                        deps[k] = v
        waits = []
        kn = self.known[eng]
        for k, v in deps.items():
            if k == "pe" and eng == "pe":
                continue
            if kn.get(k, 0) >= v:
                continue
            kn[k] = v
            waits.append((k, v))
        return waits

    def _register(self, tok, reads, writes):
        for r in reads:
            d = self.readers.setdefault(r, {})
            if d.get(tok[0], 0) < tok[1]:
                d[tok[0]] = tok[1]
        for w in writes:
            self.last_write[w] = tok
            self.readers[w] = {}

    @staticmethod
    def _flat(seq):
        out = []
        for x in seq:
            if isinstance(x, list):
                out.extend(x)
            else:
                out.append(x)
        return out

    def op(self, eng, fn, reads=(), writes=(), signal=True):
        reads = self._flat(reads); writes = self._flat(writes)
        waits = self._deps(eng, reads, writes)
        if signal:
            self.ticket[eng] += 1
            tok = (eng, self.ticket[eng])
            self.pending[eng] = False
        else:
            tok = (eng, self.ticket[eng] + 1)
            self.pending[eng] = True
        self.ops[eng].append((fn, waits, ("eng", eng) if signal else None))
        self._register(tok, reads, writes)
        return tok

    def dma(self, eng, fn, sem, reads=(), writes=(), tok_override=None):
        reads = self._flat(reads); writes = self._flat(writes)
        waits = self._deps(eng, reads, writes)
        self.dma_count[sem] = self.dma_count.get(sem, 0) + 1
        tok = tok_override or (sem, 16 * self.dma_count[sem])
        self.ops[eng].append((fn, waits, ("dma", sem)))
        self._register(tok, reads, writes)
        return tok

    def final_wait(self, eng, toks):
        deps = {}
        for k, v in toks:
            deps[k] = max(deps.get(k, 0), v)
        self.ops[eng].append((None, list(deps.items()), None))

    def sem_names(self):
        names = set(self.ENGS)
        names.update(self.dma_count.keys())
        return sorted(names)

    def emit(self, block, sems):
        engmap = {"pe": block.tensor, "act": block.scalar, "dve": block.vector,
                  "pool": block.gpsimd, "sp": block.sync}
        for e in self.ENGS:
            ops = self.ops[e]
            if not ops:
                continue
            assert not self.pending[e], e

            def body(engine, ops=ops):
                for fn, waits, sig in ops:
                    for k, v in waits:
                        engine.wait_ge(sems[k], v)
                    if fn is None:
                        continue
                    ins = fn(engine)
                    if sig is not None:
                        if sig[0] == "eng":
                            ins.then_inc(sems[sig[1]], 1)
                        else:
                            ins.then_inc(sems[sig[1]], 16)
            engmap[e](body)


class Rot:
    def __init__(self, items):
        self.items = items
        self.i = 0

    def get(self):
        it = self.items[self.i % len(self.items)]
        self.i += 1
        return it


def bcast_mid(ap, reps):
    pairs = [list(x) for x in ap.ap]
    assert len(pairs) == 2, pairs
    return bass.AP(ap.tensor, ap.offset, [pairs[0], [0, reps], pairs[1]])


def build_program():
    nc = bass.Bass("TRN2", target_bir_lowering=False)
    P = Prog()

    def din(name, shape):
        return nc.dram_tensor(name, list(shape), F32, kind="ExternalInput").ap()

    def dout(name, shape):
        return nc.dram_tensor(name, list(shape), F32, kind="ExternalOutput").ap()

    x_d = din("x", [SEQ, D]); xs_d = din("xs", [NS, D])
    p_d = din("p", [SEQ, PD]); ps_d = din("ps", [NS, PD])
    sa_d = din("sa", [NS, 2, D]); sf_d = din("sf", [NS, 2, 2 * DFF])
    g_mix = din("g_mix", [D]); g_v = din("g_v", [D]); g_ffn = din("g_ffn", [D])
    g_ple = din("g_ple", [D]); g_final = din("g_final", [D])
    w_in = din("w_in", [D, NIN]); w_conv_a = din("w_conv_a", [3, D])
    w_out_a = din("w_out_a", [D, D]); w_out_b = din("w_out_b", [D, D]); w_o = din("w_o", [D, D])
    w_spatial = din("w_spatial", [8, 128, 128]); b_spatial = din("b_spatial", [8, 128])
    w_up = din("w_up", [D, 2 * DFF]); w_conv_ffn = din("w_conv_ffn", [3, 2 * DFF])
    w_down = din("w_down", [DFF, D]); w_ple_gate = din("w_ple_gate", [D, D]); w_ple = din("w_ple", [PD, D])
    ident_d = din("ident", [128, 128]); tril_d = din("tril", [128, 128])

    y_d = dout("y", [SEQ, D]); ys_d = dout("ys", [NS, D])
    ca_d = dout("ca", [2, D]); cas_d = dout("cas", [NS, 2, D])
    cv_d = dout("cv", [128, D]); cvs_d = dout("cvs", [NS, D])
    cf_d = dout("cf", [2, 2 * DFF]); cfs_d = dout("cfs", [NS, 2, 2 * DFF])

    es = contextlib.ExitStack()
    with es:
        def sb(name, shape, dt):
            return es.enter_context(nc.sbuf_tensor("s_" + name, list(shape), dt))

        h = sb("h", [128, 8, TT], F32)
        xn = sb("xn", [128, 8, TT], BF16)
        R1 = sb("R1", [128, 9216 + 2 * 8 * TT], BF16)
        vn3 = R1[:, 0:9216].rearrange("p (t f) -> p t f", t=9)
        m3 = R1[:, 0:8 * TT].rearrange("p (k t) -> p k t", k=8)
        ab = R1[:, 9216:9216 + 8 * TT].rearrange("p (k t) -> p k t", k=8)
        us = R1[:, 9216 + 8 * TT:9216 + 16 * TT].rearrange("p (k t) -> p k t", k=8)
        act = R1[:, 0:22 * TT].rearrange("p (k t) -> p k t", k=22)
        R1f = R1[:, :].bitcast(F32)
        xst = R1f[:, 0:8192].rearrange("p (t f) -> p t f", t=8)
        pst_all = R1f[:, 8192:10240].rearrange("p (t f) -> p t f", t=8)
        xss = R1f[0:NS, 10240:11264]
        pss_all = R1f[0:NS, 11264:11520]
        pT = sb("pT", [128, 2, TT], BF16)
        ring = sb("ring", [128, 8, 2048], BF16)
        rs = sb("rs", [128, TT], F32)
        idf = sb("idf", [128, 128], F32); idb = sb("idb", [128, 128], BF16)
        tril = sb("tril", [128, 128], F32)
        ones = sb("ones", [128, 128], BF16)
        WsT = sb("WsT", [128, 8, 128], BF16)
        Ds = sb("Ds", [16, 8, 16], BF16)
        w00 = sb("w00", [16, 8], F32)
        bbcw = sb("bbcw", [128, 1032], F32)
        bbc = bbcw[:, 0:1024].rearrange("p (j t) -> p j t", j=8)
        gvbc = sb("gvbc", [128, 1024], F32)
        gfbc = sb("gfbc", [128, 1024], F32)
        rtok = sb("rtok", [128, 16], F32)
        cols1 = sb("cols1", [128, 64], F32)
        cols2 = sb("cols2", [128, 128], F32)
        wcfT = sb("wcfT", [128, 132], F32)
        carry_a = sb("carry_a", [128, 2, 8], F32)
        carry_f = sb("carry_f", [128, 2, 44], F32)
        saT = sb("saT", [128, NS, 2, 8], F32)
        sfT = sb("sfT", [128, NS, 2, 44], F32)
        oAs = sb("oAs", [128, NS, 2, 8], F32)
        oFs = sb("oFs", [128, NS, 2, 44], F32)
        ssv = sb("ssv", [128, 16], F32)
        rv = sb("rv", [128, 16], F32)
        big = [sb(f"big{i}", [128, 1024], F32) for i in range(4)]
        pbf = [sb(f"pbf{i}", [128, 256], BF16) for i in range(2)]
        sqall = sb("sqall", [128, 2064], BF16)
        sqb = [sqall[:, i * 512:(i + 1) * 512] for i in range(4)]
        sqwide = sqall[:, :].bitcast(F32)
        NTMP = 8
        tmpall = sb("tmpall", [128, NTMP * 516], F32)
        tmpf = [tmpall[:, i * 516:(i + 1) * 516] for i in range(NTMP)]
        widef = [tmpall[:, 2 * j * 516: 2 * j * 516 + 1032] for j in range(NTMP // 2)]
        ps = [es.enter_context(nc.psum_tensor(f"pq{i}", [128, 512], F32)) for i in range(8)]

        bigR = Rot([(big[i], ("big", i)) for i in range(4)])
        pinR = Rot([(None, pbf[i], ("pin", i), ("pbf", i)) for i in range(2)])
        sqR = Rot([(sqb[i], ("sq", i)) for i in range(4)])
        tmpR = Rot([(tmpf[i], [("tmp", i, "a"), ("tmp", i, "b")]) for i in range(NTMP)])
        wideR = Rot([(widef[j], [("tmp", 2 * j, "a"), ("tmp", 2 * j, "b"), ("tmp", 2 * j + 1, "a"), ("tmp", 2 * j + 1, "b")])
                     for j in range(NTMP // 2)] + [(bbcw, ["bbc"]), (sqwide, [("sq", i) for i in range(4)])])
        class PsRot:
            def __init__(self):
                self.i = 0
                self.reserved = set()

            def get(self):
                while True:
                    b = self.i % 8
                    self.i += 1
                    if b not in self.reserved:
                        return ps[b], ("ps", b)

            def reserve(self):
                t, r = self.get()
                self.reserved.add(r[1])
                return t, r

            def release(self, r):
                self.reserved.discard(r[1])

        psR = PsRot()
        scr = sb("scr", [128, 2], F32)
        epsT = sb("epsT", [128, 1], F32)

        def pstag(i):
            return ("ps", i)

        def gcol(gi, k):
            return cols1[:, gi * 8 + k: gi * 8 + k + 1]

        def wca(tap, j):
            return cols1[:, 32 + tap * 8 + j: 32 + tap * 8 + j + 1]

        def wcf(tap, c):
            r = tap * 44 + c
            if r < 128:
                return cols2[:, r:r + 1]
            return cols1[:, 56 + r - 128: 56 + r - 128 + 1]

        def stage_inputs(pss):
            r0 = pss * 1024
            P.dma("sp", (lambda e: e.dma_start(out=pst_all, in_=p_d[r0:r0 + 1024, :].rearrange("(t q) f -> q t f", q=128))),
                  "xp", reads=["R1busy"], writes=[("pst",)])
            for hf in range(4):
                P.dma("sp", (lambda e, hf=hf: e.dma_start(
                    out=xst[:, hf * 2:(hf + 1) * 2, :],
                    in_=x_d[r0 + hf * 256: r0 + (hf + 1) * 256, :].rearrange("(t q) f -> q t f", q=128))),
                    f"xs{hf}", reads=["R1busy"], writes=[("xst", hf)])
            if pss == 1:
                P.dma("sp", (lambda e: e.dma_start(out=xss, in_=xs_d[:, :])), "xq0", reads=["R1busy"], writes=[("xss",)])
                P.dma("sp", (lambda e: e.dma_start(out=pss_all, in_=ps_d[:, :])), "xq1", reads=["R1busy"], writes=[("pss",)])

        outs_tok = []
        stg0, stg0r = big[0], ("big", 0)
        stg1, stg1r = big[1], ("big", 1)
        setup = []

        def sdma(eng, out, in_, writes, **kw):
            setup.append((eng, out, in_, writes, kw))

        sdma("sp", idf[:], ident_d[:, :], ["idf"])
        sdma("sp", tril[:], tril_d[:, :], ["tril"])
        sdma("sp", bbcw[:, 0:1024], b_spatial.rearrange("j t -> (j t)").partition_broadcast(128), ["bbc"])
        sdma("sp", gvbc[:], g_v.partition_broadcast(128), ["gvbc"])
        sdma("sp", gfbc[:], g_final.partition_broadcast(128), ["gfbc"])
        sdma("sp", stg0[0:8, 0:128], g_mix.rearrange("(j p) -> j p", p=128), [stg0r])
        sdma("sp", stg0[8:16, 0:128], g_ffn.rearrange("(j p) -> j p", p=128), [stg0r])
        sdma("sp", stg0[16:24, 0:128], g_ple.rearrange("(j p) -> j p", p=128), [stg0r])
        sdma("sp", stg0[24:32, 0:128], g_final.rearrange("(j p) -> j p", p=128), [stg0r])
        sdma("sp", stg0[32:56, 0:128], w_conv_a.rearrange("k (j p) -> (k j) p", p=128), [stg0r])
        wcf_rows = w_conv_ffn.rearrange("k (c p) -> (k c) p", p=128)
        sdma("sp", stg0[56:60, 0:128], wcf_rows[128:132, :], [stg0r])
        sdma("sp", stg0[:, 128:256], wcf_rows[0:128, :], [stg0r])
        sdma("sp", stg1[:, :].rearrange("p (j s) -> p j s", j=8), w_spatial.rearrange("j t s -> t j s"), [stg1r])
        sdma("sp", w00[:], w_spatial.rearrange("j t s -> j (t s)")[:, 0].partition_broadcast(16), ["w00"],
             allow_slow_non_contiguous=True)
        early = [x for x in setup if x[3] in (["idf"], [stg0r])]
        late = [x for x in setup if x not in early]
        for grp, sem in ((early, "setupA"), (None, None), (late, "setupB")):
            if grp is None:
                stage_inputs(0)
                continue
            stok = (sem, 16 * len(grp))
            for eng, out, in_, writes, kw in grp:
                P.dma(eng, (lambda e, out=out, in_=in_, kw=kw: e.dma_start(out=out, in_=in_, **kw)), sem,
                      tok_override=stok)
            for eng, out, in_, writes, kw in grp:
                P._register(stok, (), writes)

        P.op("dve", lambda e: e.tensor_copy(out=idb[:], in_=idf[:]), reads=["idf"], writes=["idb"])
        P.op("pool", lambda e: e.memset(ones[:], 1.0 / 1024.0), writes=["ones"])
        P.op("pool", lambda e: e.memset(scr[:], 1.0), writes=["scr"])
        P.op("pool", lambda e: e.memset(epsT[:], EPS), writes=["epsT"])
        P.op("pool", lambda e: e.memset(carry_a[:], 0.0), writes=["carry_a"])
        P.op("pool", lambda e: e.memset(carry_f[:], 0.0), writes=["carry_f"])
        pa, par = psR.get()
        P.op("pe", lambda e: e.transpose(out=pa[:, 0:60], in_=stg0[0:60, 0:128], identity=idf[0:60, 0:60]),
             reads=[stg0r, "idf"], writes=[par])
        P.op("dve", lambda e: e.tensor_copy(out=cols1[:, 0:60], in_=pa[:, 0:60]), reads=[par], writes=["cols"])
        pa2, par2 = psR.get()
        P.op("pe", lambda e: e.transpose(out=pa2[:, 0:128], in_=stg0[:, 128:256], identity=idf[:]),
             reads=[stg0r, "idf"], writes=[par2])
        P.op("dve", lambda e: e.tensor_copy(out=cols2[:], in_=pa2[:, 0:128]), reads=[par2], writes=["cols"])
        P.op("dve", lambda e: e.tensor_copy(out=wcfT[:, 0:128], in_=cols2[:]), reads=["cols"], writes=["wcfT"])
        P.op("dve", lambda e: e.tensor_copy(out=wcfT[:, 128:132], in_=cols1[:, 56:60]), reads=["cols"], writes=["wcfT"])
        def setup_spatial():
            for j in range(8):
                P.op("dve", lambda e, j=j: e.tensor_tensor(out=stg1[:, j * 128:(j + 1) * 128], in0=stg1[:, j * 128:(j + 1) * 128],
                                                            in1=tril[:], op=ALU.mult),
                     reads=[stg1r, "tril"], writes=[stg1r])
            for jj in range(2):
                pw, pwr = psR.get()
                for q in range(4):
                    j = jj * 4 + q
                    P.op("pe", lambda e, j=j, q=q, pw=pw: e.transpose(out=pw[:, q * 128:(q + 1) * 128],
                                                                        in_=stg1[:, j * 128:(j + 1) * 128], identity=idf[:]),
                         reads=[stg1r, "idf"], writes=[pwr], signal=(q == 3))
                P.op("act", lambda e, jj=jj, pw=pw: e.copy(out=WsT[:, jj * 4:(jj + 1) * 4, :].rearrange("p j t -> p (j t)"), in_=pw[:, :]),
                     reads=[pwr], writes=["WsT"])
            for j in range(8):
                P.op("dve", lambda e, j=j: e.tensor_scalar_mul(out=Ds[:, j, :], in0=idf[0:16, 0:16], scalar1=w00[:, j:j + 1]),
                     reads=["idf", "w00"], writes=["Ds"])

        def load_rounds(src_rows_ap, nrow_tiles, dst, dst_res, eng_sem):
            rounds = []
            for a0 in range(0, nrow_tiles, 4):
                na = min(4, nrow_tiles - a0)
                st = {}

                def issue(a0=a0, na=na, st=st):
                    bt, btr = bigR.get()
                    st["bt"] = (bt, btr)
                    P.dma("sp", (lambda e: e.dma_start(
                        out=bt[:, 0:na * 128].rearrange("q (a p) -> q a p", a=na),
                        in_=src_rows_ap[a0 * 128:(a0 + na) * 128, :].rearrange("(a q) p -> q a p", q=128))),
                        eng_sem + str(btr[1]), writes=[btr])

                def finish(a0=a0, na=na, st=st):
                    bt, btr = st["bt"]
                    pz, pzr = psR.get()
                    for a in range(na):
                        P.op("pe", lambda e, a=a: e.transpose(out=pz[:, a * 128:(a + 1) * 128],
                                                                in_=bt[:, a * 128:(a + 1) * 128], identity=idf[:]),
                             reads=[btr, "idf"], writes=[pzr], signal=(a == na - 1))
                    P.op("dve", lambda e: e.tensor_copy(out=dst[:, a0 * 128:(a0 + na) * 128], in_=pz[:, 0:na * 128]),
                         reads=[pzr], writes=[dst_res])
                rounds.append((issue, finish))
            return rounds

        saT_flat = saT[:].rearrange("p s r j -> p (s r j)")
        sfT_flat = sfT[:].rearrange("p s r c -> p (s r c)")

        def slab_items():
            items = []

            def std(w, c0):
                items.append(("std", w, c0))
            for q in range(4):
                std(w_in, 4096 + 256 * q)
            for jp in range(4):
                std(w_in, jp * 256); std(w_in, 1024 + jp * 256); std(w_in, 2048 + jp * 256)
            for jp in range(4):
                std(w_in, 3072 + jp * 256)
            for ip in range(4):
                std(w_in, 5120 + ip * 256); std(w_in, 6144 + ip * 256)
                std(w_out_a, ip * 256); std(w_out_b, ip * 256)
            for ip in range(4):
                std(w_o, ip * 256)
            for cp in range(11):
                std(w_up, cp * 256); std(w_up, DFF + cp * 256)
            for i in range(8):
                items.append(("down", w_down, i * 128))
            items.append(("ple", w_ple, 0))
            for ip in range(4):
                std(w_ple_gate, ip * 256)
            return items

        per_pass = slab_items()
        all_items = per_pass + per_pass
        NIT = len(all_items)

        class Ring:
            def __init__(self):
                self.next_load = 0
                self.slot_ptr = 0
                self.occupant = [None] * 8
                self.released = set()
                self.item_slot = {}

            def _try_load(self):
                i = self.next_load
                if i >= NIT:
                    return False
                kind, w, c0 = all_items[i]
                ns = 2 if kind == "down" else 1
                s = self.slot_ptr
                if ns == 2 and s % 2 == 1:
                    s = (s + 1) % 8
                for q in range(ns):
                    occ = self.occupant[(s + q) % 8]
                    if occ is not None and occ not in self.released:
                        return False
                if kind == "std":
                    dst = ring[:, s, :].rearrange("p (k n) -> p k n", k=8)
                    src = w[:, c0:c0 + 256].rearrange("(k p) n -> p k n", p=128)
                elif kind == "down":
                    dst = ring[:, s:s + 2, :].rearrange("p a b -> p (a b)")[:, 0:2816].rearrange("p (c n) -> p c n", c=22)
                    src = w[:, c0:c0 + 128].rearrange("(c p) n -> p c n", p=128)
                else:
                    dst = ring[:, s, :].rearrange("p (k n) -> p k n", k=2)
                    src = w[:, :].rearrange("(k p) n -> p k n", p=128)
                res = [("ring", (s + q) % 8) for q in range(ns)]
                xtra = [("xst", 0), ("xst", 1), ("xst", 2), ("xst", 3), ("pst",)] if i == 0 else []
                P.dma("pool", (lambda e, dst=dst, src=src: e.dma_start(out=dst, in_=src)), f"wr{s}", reads=xtra, writes=res)
                for q in range(ns):
                    self.occupant[(s + q) % 8] = i
                self.item_slot[i] = (s, dst, res)
                self.slot_ptr = (s + ns) % 8
                self.next_load += 1
                return True

            def prefetch(self):
                while self._try_load():
                    pass

            def get(self, i):
                self.prefetch()
                assert i in self.item_slot, (i, self.next_load)
                return self.item_slot[i]

            def release(self, i):
                self.released.add(i)
                self.prefetch()

        R = Ring()
        item_ctr = [0]

        def next_item():
            i = item_ctr[0]
            item_ctr[0] += 1
            s, dst, res = R.get(i)
            return i, dst, res

        class NormAcc:
            def __init__(self, blocks):
                self.blocks = blocks
                self.bank = {}
                self.count = {}
                self.pend_sq = []
                self.pend_mm = []
                for b in blocks:
                    self.count[b["idx"]] = 0

            def add(self, k, b):
                while len(self.pend_mm) >= 2:
                    self.pend_mm.pop(0)[1]()
                if self.pend_sq:
                    self.pend_sq.pop(0)[1]()
                c0, n, bi = b["c0"], b["n"], b["idx"]
                if bi not in self.bank:
                    self.bank[bi] = psR.reserve()
                pst, pstr = self.bank[bi]
                cnt = self.count[bi]
                self.count[bi] += 1

                def emit_sq():
                    sq, sqr = sqR.get()
                    P.op("act", lambda e: e.activation(out=sq[:, 0:n], in_=h[:, k, c0:c0 + n], func=AF.Square),
                         reads=[("h", k, bi)], writes=[sqr])
                    self.pend_mm.append((bi, lambda: P.op(
                        "pe", lambda e: e.matmul(pst[:, 0:n], lhsT=ones[:], rhs=sq[:, 0:n], start=(cnt == 0), stop=(cnt == 7)),
                        reads=[sqr, "ones"], writes=[pstr], signal=True)))
                self.pend_sq.append((bi, emit_sq))

            def flush(self, bi):
                mine = [fn for tag, fn in self.pend_sq if tag == bi]
                self.pend_sq = [(tag, fn) for tag, fn in self.pend_sq if tag != bi]
                for fn in mine:
                    fn()
                mine = [fn for tag, fn in self.pend_mm if tag == bi]
                self.pend_mm = [(tag, fn) for tag, fn in self.pend_mm if tag != bi]
                for fn in mine:
                    fn()

            def finish_block(self, b, gi, final=False):
                first = (b["idx"] == self.blocks[0]["idx"])
                last = (b["idx"] == self.blocks[-1]["idx"])
                if first:
                    P.op("act", lambda e: e.activation(out=scr[:, 1:2], in_=scr[:, 0:1], func=AF.Ln), reads=["scr"], writes=["scr1"])
                self.flush(b["idx"])
                c0, n, bi = b["c0"], b["n"], b["idx"]
                assert self.count[bi] == 8
                pst, pstr = self.bank[bi]
                P.op("act", lambda e: e.activation(out=rs[:, c0:c0 + n], in_=pst[:, 0:n], func=AF.Ln, bias=epsT[:, 0:1]),
                     reads=[pstr, "epsT"], writes=[("rs", bi)])
                psR.release(pstr)
                P.op("act", lambda e: e.activation(out=rs[:, c0:c0 + n], in_=rs[:, c0:c0 + n], func=AF.Exp, scale=-0.5),
                     reads=[("rs", bi)], writes=[("rs", bi)])
                if last:
                    P.op("act", lambda e: e.activation(out=scr[:, 1:2], in_=scr[:, 0:1], func=AF.Tanh), reads=["scr"], writes=["scr1"])
                if final:
                    return
                for k in range(8):
                    P.op("dve", lambda e, k=k: e.scalar_tensor_tensor(
                        out=xn[:, k, c0:c0 + n], in0=h[:, k, c0:c0 + n], scalar=gcol(gi, k), in1=rs[:, c0:c0 + n],
                        op0=ALU.mult, op1=ALU.mult),
                        reads=[("h", k, bi), ("rs", bi), "cols"], writes=[("xn", k, bi)])

        def proj(pt, ptr, wv, wres, u, src, src_name, b, nk=8):
            c0, n = b["c0"], b["n"]
            for k in range(nk):
                P.op("pe", lambda e, k=k: e.matmul(pt[:, 0:n], lhsT=wv[:, k, u * 128:(u + 1) * 128], rhs=src[:, k, c0:c0 + n],
                                                   start=(k == 0), stop=(k == nk - 1)),
                     reads=wres + [(src_name, k, b["idx"])], writes=[ptr], signal=(k == nk - 1))

        def store_colsT(src_flat, ncol, src_reads, dst_rows_ap):
            for a0 in range(0, ncol, 512):
                w = min(512, ncol - a0)
                ob, obr = bigR.get()
                pz, pzr = psR.get()
                nt = (w + 127) // 128
                for a in range(nt):
                    wa_ = min(128, w - a * 128)
                    P.op("pe", lambda e, a=a, wa_=wa_, a0=a0, pz=pz: e.transpose(
                        out=pz[0:wa_, a * 128:(a + 1) * 128], in_=src_flat[:, a0 + a * 128: a0 + a * 128 + wa_], identity=idf[:]),
                        reads=src_reads + ["idf"], writes=[pzr], signal=(a == nt - 1))
                full = w // 128
                if full:
                    P.op("dve", lambda e, pz=pz, ob=ob, full=full: e.tensor_copy(out=ob[:, 0:full * 128], in_=pz[:, 0:full * 128]),
                         reads=[pzr], writes=[obr])
                    outs_tok.append(P.dma("sp", (lambda e, ob=ob, a0=a0, full=full: e.dma_start(
                        out=dst_rows_ap[a0:a0 + full * 128, :].rearrange("(a q) p -> q a p", q=128),
                        in_=ob[:, 0:full * 128].rearrange("q (a p) -> q a p", a=full))), "st" + str(obr[1]), reads=[obr]))
                rem = w - full * 128
                if rem:
                    P.op("dve", lambda e, pz=pz, ob=ob, full=full, rem=rem: e.tensor_copy(
                        out=ob[0:rem, 512:640], in_=pz[0:rem, full * 128:(full + 1) * 128]), reads=[pzr], writes=[obr])
                    outs_tok.append(P.dma("sp", (lambda e, ob=ob, a0=a0, full=full, rem=rem: e.dma_start(
                        out=dst_rows_ap[a0 + full * 128: a0 + full * 128 + rem, :], in_=ob[0:rem, 512:640])),
                        "st" + str(obr[1]), reads=[obr]))

        for pss in range(2):
            if pss == 0:
                blocks = [dict(idx=0, c0=0, n=512, kind="p", first=True, last=False),
                          dict(idx=1, c0=512, n=512, kind="p", first=False, last=False)]
                ncols = 1024
            else:
                blocks = [dict(idx=0, c0=0, n=512, kind="p", first=False, last=False),
                          dict(idx=1, c0=512, n=512, kind="p", first=False, last=True),
                          dict(idx=2, c0=1024, n=NS, kind="s", first=False, last=False)]
                ncols = TT
            tiles = [dict(t=t, rows=128, c0=t * 128, blk=t // 4, src=x_d[pss * 1024 + t * 128: pss * 1024 + (t + 1) * 128, :],
                          psrc=p_d[pss * 1024 + t * 128: pss * 1024 + (t + 1) * 128, :]) for t in range(8)]
            if pss == 1:
                tiles.append(dict(t=8, rows=NS, c0=1024, blk=2, src=xs_d[:, :], psrc=ps_d[:, :]))

            STAGE_MARKS.append(("A", pss, len(P.ops["pe"])))
            if pss == 0:
                R.prefetch()
            nacc = NormAcc(blocks)
            prevb = None
            def p_cast(tl):
                if tl["t"] < 8:
                    pi, pir = pst_all[:, tl["t"], :], ("pst",)
                else:
                    pi, pir = pss_all, ("pss",)
                _, pb, _, pbr = pinR.get()
                rows = tl["rows"]
                P.op("dve", lambda e: e.tensor_copy(out=pb[0:rows, :], in_=pi[0:rows, :]), reads=[pir], writes=[pbr])
                return pb, pbr

            nxt_cast = p_cast(tiles[0])
            for ti, tl in enumerate(tiles):
                rows, c0, blk = tl["rows"], tl["c0"], tl["blk"]
                if tl["t"] < 8:
                    xt, xtr = xst[:, tl["t"], :], ("xst", tl["t"] // 2)
                else:
                    xt, xtr = xss, ("xss",)
                pb, pbr = nxt_cast
                if ti + 1 < len(tiles):
                    nxt_cast = p_cast(tiles[ti + 1])
                pz, pzr = psR.get()
                pzb = pz[:, :].bitcast(BF16)
                for kk in range(2):
                    P.op("pe", lambda e, kk=kk, pb=pb, pzb=pzb, rows=rows: e.transpose(
                        out=pzb[:, kk * 128: kk * 128 + rows], in_=pb[0:rows, kk * 128:(kk + 1) * 128], identity=idb[0:rows, 0:rows]),
                        reads=[pbr, "idb"], writes=[pzr], signal=(kk == 1))
                P.op("act", lambda e, pzb=pzb, c0=c0, rows=rows: e.copy(
                    out=pT[:, :, c0:c0 + rows], in_=pzb[:, 0:256].rearrange("p (k t) -> p k t", k=2)[:, :, 0:rows]),
                    reads=[pzr], writes=[("pT", blk)])
                for hf in range(2):
                    pz, pzr = psR.get()
                    for kk in range(4):
                        k = hf * 4 + kk
                        P.op("pe", lambda e, k=k, kk=kk, xt=xt, pz=pz, rows=rows: e.transpose(
                            out=pz[:, kk * 128: kk * 128 + rows], in_=xt[0:rows, k * 128:(k + 1) * 128], identity=idf[0:rows, 0:rows]),
                            reads=[xtr, "idf"], writes=[pzr], signal=(kk == 3))
                    eng = "act" if hf == 0 else "dve"
                    if eng == "act":
                        P.op("act", lambda e, hf=hf, pz=pz, c0=c0, rows=rows: e.copy(
                            out=h[:, hf * 4:(hf + 1) * 4, c0:c0 + rows], in_=pz[:, :].rearrange("p (k t) -> p k t", k=4)[:, :, 0:rows]),
                            reads=[pzr], writes=[("h", k, blk) for k in range(hf * 4, hf * 4 + 4)])
                    else:
                        P.op("dve", lambda e, hf=hf, pz=pz, c0=c0, rows=rows: e.tensor_copy(
                            out=h[:, hf * 4:(hf + 1) * 4, c0:c0 + rows], in_=pz[:, :].rearrange("p (k t) -> p k t", k=4)[:, :, 0:rows]),
                            reads=[pzr], writes=[("h", k, blk) for k in range(hf * 4, hf * 4 + 4)])
                if tl["t"] % 4 == 3 or tl["t"] == 8:
                    if prevb is not None:
                        nacc.finish_block(prevb, 0)
                    for k in range(8):
                        nacc.add(k, blocks[blk])
                    prevb = blocks[blk]
            nacc.finish_block(prevb, 0)

            STAGE_MARKS.append(("B", pss, len(P.ops["pe"])))
            P.op("pool", lambda e: e.memset(ssv[:], 0.0), reads=["rv"], writes=["ssv"])
            vit = [next_item() for _ in range(4)]
            for tl in tiles:
                t, rows, c0, blk = tl["t"], tl["rows"], tl["c0"], tl["blk"]
                for hf in range(2):
                    pz, pzr = psR.get()
                    for q2 in range(2):
                        q = hf * 2 + q2
                        _, wv, wres = vit[q]
                        for k in range(8):
                            P.op("pe", lambda e, k=k, q2=q2, wv=wv, pz=pz, c0=c0, rows=rows: e.matmul(
                                pz[0:rows, q2 * 256:(q2 + 1) * 256], lhsT=xn[:, k, c0:c0 + rows], rhs=wv[:, k, :],
                                start=(k == 0), stop=(k == 7)),
                                reads=wres + [("xn", k, blk)], writes=[pzr], signal=(k == 7 and q2 == 1))
                    P.op("act", lambda e, hf=hf, pz=pz, t=t, rows=rows: e.activation(
                        out=vn3[0:rows, t, hf * 512:(hf + 1) * 512], in_=pz[0:rows, :], func=AF.Gelu_apprx_tanh),
                        reads=[pzr], writes=[("vn", t)])
                jk, jkr = tmpR.get()
                jkb = jk[:, 0:512].bitcast(BF16)
                P.op("act", lambda e, t=t, rows=rows, jkb=jkb: e.activation(out=jkb[0:rows, :], in_=vn3[0:rows, t, :], func=AF.Square,
                                                                               scale=1.0 / 32.0, accum_out=ssv[0:rows, t:t + 1]),
                     reads=[("vn", t)], writes=[jkr, "ssv"])
            for (i, _, _) in vit:
                R.release(i)
            ntl = len(tiles)
            P.op("act", lambda e, ntl=ntl: e.activation(out=rv[:, 0:ntl], in_=ssv[:, 0:ntl], func=AF.Ln, bias=epsT[:, 0:1]), reads=["ssv", "epsT"], writes=["rv"])
            P.op("act", lambda e, ntl=ntl: e.activation(out=rv[:, 0:ntl], in_=rv[:, 0:ntl], func=AF.Exp, scale=-0.5), reads=["rv"], writes=["rv"])
            P.op("act", lambda e: e.activation(out=scr[:, 1:2], in_=scr[:, 0:1], func=AF.Tanh), reads=["scr"], writes=["scr1"])
            vn_jobs = []
            for tl in tiles:
                def vn_job(tl=tl):
                    t, rows = tl["t"], tl["rows"]
                    is_out = (pss == 1 and t >= 7)
                    if is_out:
                        ob, obr = bigR.get()
                        P.op("dve", lambda e: e.scalar_tensor_tensor(
                            out=ob[0:rows, :], in0=vn3[0:rows, t, :], scalar=rv[0:rows, t:t + 1], in1=gvbc[0:rows, :],
                            op0=ALU.mult, op1=ALU.mult), reads=[("vn", t), "rv", "gvbc"], writes=[obr])
                        dst = cv_d[:, :] if t == 7 else cvs_d[:, :]
                        outs_tok.append(P.dma("sp", (lambda e: e.dma_start(out=dst, in_=ob[0:rows, :])),
                                              "st" + str(obr[1]), reads=[obr]))
                    P.op("dve", lambda e: e.scalar_tensor_tensor(
                        out=vn3[0:rows, t, :], in0=vn3[0:rows, t, :], scalar=rv[0:rows, t:t + 1], in1=gvbc[0:rows, :],
                        op0=ALU.mult, op1=ALU.mult), reads=[("vn", t), "rv", "gvbc"], writes=[("vn", t)])
                vn_jobs.append(vn_job)

            if pss == 0:
                setup_spatial()
            STAGE_MARKS.append(("C", pss, len(P.ops["pe"])))
            if pss == 1:
                xcbs = R1f[:, 12544:12928]
            st_rounds = []
            if pss == 0:
                st_rounds = (load_rounds(sa_d.rearrange("s r (j p) -> (s r j) p", p=128), 2, saT_flat, "saT", "ld") +
                             load_rounds(sf_d.rearrange("s r (c p) -> (s r c) p", p=128), 11, sfT_flat, "sfT", "ld"))
                assert len(st_rounds) == 4
                st_rounds[0][0]()
            for jp in range(4):
                if st_rounds:
                    st_rounds[jp][1]()
                    if jp + 1 < 4:
                        st_rounds[jp + 1][0]()
                ix, wx, wxr = next_item()
                ib, wb, wbr = next_item()
                ic, wc, wcr = next_item()
                for u in range(2):
                    j = jp * 2 + u
                    for b in blocks:
                        c0, n, bi = b["c0"], b["n"], b["idx"]
                        if b["kind"] == "s":
                            pz, pzr = psR.get()
                            proj(pz[:, 0:NS], pzr, wx, wxr, u, xn, "xn", b)
                            proj(pz[:, NS:2 * NS], pzr, wc, wcr, u, xn, "xn", b)
                            proj(pz[:, 2 * NS:3 * NS], pzr, wb, wbr, u, xn, "xn", b)
                            P.op("act", lambda e, pz=pz, j=j: e.copy(out=xcbs[:, j * 48:(j + 1) * 48], in_=pz[:, 0:48]),
                                 reads=[pzr], writes=[("xcbs", j)])
                            continue
                        pX, pXr = psR.get(); pB, pBr = psR.get(); pC, pCr = psR.get()
                        proj(pX, pXr, wx, wxr, u, xn, "xn", b)
                        proj(pC, pCr, wc, wcr, u, xn, "xn", b)
                        proj(pB, pBr, wb, wbr, u, xn, "xn", b)
                        tx, txr = tmpR.get(); cx, cxr = tmpR.get(); acc, accr = tmpR.get()
                        P.op("act", lambda e, tx=tx, pX=pX, n=n: e.copy(out=tx[:, 0:n], in_=pX[:, 0:n]), reads=[pXr], writes=[txr])
                        if b["kind"] == "p":
                            if b["first"]:
                                P.op("pool", lambda e, cx=cx: e.memset(cx[:, 0:2], 0.0), writes=[cxr[0]])
                            else:
                                P.op("pool", lambda e, cx=cx, j=j: e.tensor_copy(out=cx[:, 0:2], in_=carry_a[:, :, j]),
                                     reads=[("carry_a", j)], writes=[cxr[0]])
                            P.op("dve", lambda e, cx=cx, pC=pC, tx=tx, n=n: e.tensor_tensor(out=cx[:, 2:2 + n], in0=pC[:, 0:n], in1=tx[:, 0:n], op=ALU.mult),
                                 reads=[pCr, txr], writes=[cxr[1]])
                            P.op("pool", lambda e, cx=cx, j=j, n=n: e.tensor_copy(out=carry_a[:, :, j], in_=cx[:, n:n + 2]),
                                 reads=[cxr[1]], writes=[("carry_a", j)])
                            P.op("act", lambda e, acc=acc, cx=cx, j=j, n=n: e.activation(out=acc[:, 0:n], in_=cx[:, 0:n], func=AF.Identity, scale=wca(0, j)),
                                 reads=[cxr, "cols"], writes=[accr])
                            P.op("dve", lambda e, acc=acc, cx=cx, j=j, n=n: e.scalar_tensor_tensor(
                                out=acc[:, 0:n], in0=cx[:, 1:1 + n], scalar=wca(1, j), in1=acc[:, 0:n], op0=ALU.mult, op1=ALU.add),
                                reads=[cxr, accr, "cols"], writes=[accr])
                            P.op("dve", lambda e, acc=acc, cx=cx, j=j, n=n: e.scalar_tensor_tensor(
                                out=acc[:, 0:n], in0=cx[:, 2:2 + n], scalar=wca(2, j), in1=acc[:, 0:n], op0=ALU.mult, op1=ALU.add),
                                reads=[cxr[1], accr, "cols"], writes=[accr])
                        else:
                            P.op("dve", lambda e, cx=cx, pC=pC, tx=tx, n=n: e.tensor_tensor(out=cx[:, 0:n], in0=pC[:, 0:n], in1=tx[:, 0:n], op=ALU.mult),
                                 reads=[pCr, txr], writes=[cxr])
                            P.op("pool", lambda e, j=j, cx=cx, n=n: e.tensor_copy(out=oAs[:, :, 1, j], in_=cx[:, 0:n]), reads=[cxr], writes=[("oAs", j)])
                            P.op("act", lambda e, acc=acc, j=j, n=n: e.activation(out=acc[:, 0:n], in_=saT[:, :, 0, j], func=AF.Identity, scale=wca(0, j)),
                                 reads=["saT", "cols"], writes=[accr])
                            P.op("dve", lambda e, acc=acc, j=j, n=n: e.scalar_tensor_tensor(
                                out=acc[:, 0:n], in0=saT[:, :, 1, j], scalar=wca(1, j), in1=acc[:, 0:n], op0=ALU.mult, op1=ALU.add),
                                reads=["saT", accr, "cols"], writes=[accr])
                            P.op("dve", lambda e, acc=acc, cx=cx, j=j, n=n: e.scalar_tensor_tensor(
                                out=acc[:, 0:n], in0=cx[:, 0:n], scalar=wca(2, j), in1=acc[:, 0:n], op0=ALU.mult, op1=ALU.add),
                                reads=[cxr, accr, "cols"], writes=[accr])
                        P.op("dve", lambda e, acc=acc, pB=pB, j=j, c0=c0, n=n: e.tensor_tensor(out=ab[:, j, c0:c0 + n], in0=pB[:, 0:n], in1=acc[:, 0:n], op=ALU.mult),
                             reads=[pBr, accr], writes=[("ab", j, bi)])
                        if vn_jobs:
                            vn_jobs.pop(0)()
                R.release(ix); R.release(ib); R.release(ic)
            while vn_jobs:
                vn_jobs.pop(0)()
            if pss == 1:
                sblk = blocks[2]
                sc0 = sblk["c0"]
                x4 = xcbs[:, 0:8 * 48].rearrange("p (j a s) -> p j a s", j=8, a=3)
                xres = [("xcbs", j) for j in range(8)]
                ores = [("oAs", j) for j in range(8)]
                cxv = oAs[:, :, 1, :].rearrange("p s j -> p j s")
                acc, accr = tmpR.get(); t1, t1r = tmpR.get()
                a3 = acc[:, 0:8 * NS].rearrange("p (j s) -> p j s", j=8)
                t3 = t1[:, 0:8 * NS].rearrange("p (j s) -> p j s", j=8)

                def wav(tap):
                    w = cols1[:, 32 + tap * 8: 32 + tap * 8 + 8]
                    pr = [list(x) for x in w.ap]
                    return bass.AP(w.tensor, w.offset, [pr[0], pr[1], [0, NS]])
                P.op("dve", lambda e, cxv=cxv, x4=x4, a3=a3, t3=t3, wav=wav, sc0=sc0: e.tensor_tensor(out=cxv, in0=x4[:, :, 1, :], in1=x4[:, :, 0, :], op=ALU.mult),
                     reads=xres, writes=ores)
                P.op("dve", lambda e, cxv=cxv, x4=x4, a3=a3, t3=t3, wav=wav, sc0=sc0: e.tensor_tensor(out=a3, in0=saT[:, :, 0, :].rearrange("p s j -> p j s"), in1=wav(0), op=ALU.mult),
                     reads=["saT", "cols"], writes=[accr])
                P.op("dve", lambda e, cxv=cxv, x4=x4, a3=a3, t3=t3, wav=wav, sc0=sc0: e.tensor_tensor(out=t3, in0=saT[:, :, 1, :].rearrange("p s j -> p j s"), in1=wav(1), op=ALU.mult),
                     reads=["saT", "cols"], writes=[t1r])
                P.op("dve", lambda e, cxv=cxv, x4=x4, a3=a3, t3=t3, wav=wav, sc0=sc0: e.tensor_tensor(out=a3, in0=a3, in1=t3, op=ALU.add), reads=[accr, t1r], writes=[accr])
                P.op("dve", lambda e, cxv=cxv, x4=x4, a3=a3, t3=t3, wav=wav, sc0=sc0: e.tensor_tensor(out=t3, in0=cxv, in1=wav(2), op=ALU.mult), reads=ores + ["cols"], writes=[t1r])
                P.op("dve", lambda e, cxv=cxv, x4=x4, a3=a3, t3=t3, wav=wav, sc0=sc0: e.tensor_tensor(out=a3, in0=a3, in1=t3, op=ALU.add), reads=[accr, t1r], writes=[accr])
                P.op("dve", lambda e, cxv=cxv, x4=x4, a3=a3, t3=t3, wav=wav, sc0=sc0: e.tensor_tensor(out=ab[:, :, sc0:sc0 + NS], in0=x4[:, :, 2, :], in1=a3, op=ALU.mult),
                     reads=xres + [accr], writes=[("ab", j, 2) for j in range(8)])

            if pss == 0:
                P.op("pool", lambda e: e.tensor_copy(out=oAs[:, :, 0, :], in_=saT[:, :, 1, :]), reads=["saT"], writes=[("oAs", j) for j in range(8)])
                P.op("pool", lambda e: e.tensor_copy(out=oFs[:, :, 0, :], in_=sfT[:, :, 1, :]), reads=["sfT"], writes=[("oFs", c) for c in range(44)])
            STAGE_MARKS.append(("D", pss, len(P.ops["pe"])))
            for jp in range(4):
                iu, wu, wur = next_item()
                for u in range(2):
                    j = jp * 2 + u
                    for b in blocks:
                        c0, n, bi = b["c0"], b["n"], b["idx"]
                        pU, pUr = psR.get(); pS, pSr = psR.get()
                        proj(pU, pUr, wu, wur, u, xn, "xn", b)
                        tu, tur = tmpR.get()
                        P.op("act", lambda e, tu=tu, pU=pU, n=n: e.activation(out=tu[:, 0:n], in_=pU[:, 0:n], func=AF.Gelu_apprx_tanh),
                             reads=[pUr], writes=[tur])
                        if b["kind"] == "p":
                            for q in range(4):
                                t = c0 // 128 + q
                                P.op("pe", lambda e, q=q, t=t, j=j, pS=pS: e.matmul(
                                    pS[:, q * 128:(q + 1) * 128], lhsT=vn3[:, t, j * 128:(j + 1) * 128], rhs=WsT[:, j, :], start=True, stop=True),
                                    reads=[("vn", t), "WsT"], writes=[pSr], signal=(q == 3))
                            ts_, tsr = tmpR.get()
                            P.op("dve", lambda e, ts_=ts_, pS=pS, j=j: e.tensor_tensor(
                                out=ts_[:, 0:512].rearrange("p (a t) -> p a t", a=4), in0=pS[:, :].rearrange("p (a t) -> p a t", a=4),
                                in1=bcast_mid(bbc[:, j, :], 4), op=ALU.add), reads=[pSr, "bbc"], writes=[tsr])
                            P.op("dve", lambda e, ts_=ts_, tu=tu, j=j, c0=c0, n=n: e.tensor_tensor(
                                out=us[:, j, c0:c0 + n], in0=ts_[:, 0:n], in1=tu[:, 0:n], op=ALU.mult),
                                reads=[tsr, tur], writes=[("us", j, bi)])
                        else:
                            P.op("pe", lambda e, j=j, pS=pS, n=n: e.matmul(
                                pS[:, 0:n], lhsT=vn3[0:NS, 8, j * 128:(j + 1) * 128], rhs=Ds[:, j, :], start=True, stop=True),
                                reads=[("vn", 8), "Ds"], writes=[pSr])
                            P.op("dve", lambda e, pS=pS, tu=tu, j=j, c0=c0, n=n: e.scalar_tensor_tensor(
                                out=us[:, j, c0:c0 + n], in0=pS[:, 0:n], scalar=bbc[:, j, 0:1], in1=tu[:, 0:n], op0=ALU.add, op1=ALU.mult),
                                reads=[pSr, tur, "bbc"], writes=[("us", j, bi)])
                R.release(iu)

            if pss == 1:
                store_colsT(carry_a[:].rearrange("p r j -> p (r j)"), 16, [("carry_a", j) for j in range(8)],
                            ca_d.rearrange("r (j p) -> (r j) p", p=128))
                store_colsT(oAs[:].rearrange("p s r j -> p (s r j)"), 256, [("oAs", j) for j in range(8)],
                            cas_d.rearrange("s r (j p) -> (s r j) p", p=128))
            STAGE_MARKS.append(("E", pss, len(P.ops["pe"])))
            for ip in range(4):
                iga, wga, wgar = next_item()
                igb, wgb, wgbr = next_item()
                ioa, woa, woar = next_item()
                iob, wob, wobr = next_item()
                for u in range(2):
                    i = ip * 2 + u
                    for b in blocks:
                        c0, n, bi = b["c0"], b["n"], b["idx"]
                        pGA, pGAr = psR.get(); pGB, pGBr = psR.get(); pYA, pYAr = psR.get(); pYB, pYBr = psR.get()
                        proj(pGA, pGAr, wga, wgar, u, xn, "xn", b)
                        proj(pGB, pGBr, wgb, wgbr, u, xn, "xn", b)
                        proj(pYA, pYAr, woa, woar, u, ab, "ab", b)
                        proj(pYB, pYBr, wob, wobr, u, us, "us", b)
                        ta, tar = tmpR.get(); tb, tbr = tmpR.get()
                        P.op("act", lambda e, ta=ta, pGA=pGA, n=n: e.activation(out=ta[:, 0:n], in_=pGA[:, 0:n], func=AF.Tanh, scale=0.5),
                             reads=[pGAr], writes=[tar])
                        P.op("act", lambda e, tb=tb, pGB=pGB, n=n: e.activation(out=tb[:, 0:n], in_=pGB[:, 0:n], func=AF.Tanh, scale=0.5),
                             reads=[pGBr], writes=[tbr])
                        P.op("dve", lambda e, ta=ta, pYA=pYA, n=n: e.scalar_tensor_tensor(
                            out=ta[:, 0:n], in0=ta[:, 0:n], scalar=1.0, in1=pYA[:, 0:n], op0=ALU.add, op1=ALU.mult),
                            reads=[tar, pYAr], writes=[tar])
                        P.op("dve", lambda e, tb=tb, pYB=pYB, n=n: e.scalar_tensor_tensor(
                            out=tb[:, 0:n], in0=tb[:, 0:n], scalar=1.0, in1=pYB[:, 0:n], op0=ALU.add, op1=ALU.mult),
                            reads=[tbr, pYBr], writes=[tbr])
                        P.op("pool", lambda e, ta=ta, tb=tb, i=i, c0=c0, n=n: e.tensor_tensor(
                            out=m3[:, i, c0:c0 + n], in0=ta[:, 0:n], in1=tb[:, 0:n], op=ALU.add),
                            reads=[tar, tbr], writes=[("m", i, bi)])
                for it in (iga, igb, ioa, iob):
                    R.release(it)

            STAGE_MARKS.append(("F", pss, len(P.ops["pe"])))
            nacc = NormAcc(blocks)
            wos = [next_item() for _ in range(4)]
            for b in blocks:
                c0, n, bi = b["c0"], b["n"], b["idx"]
                for ip in range(4):
                    io, wo, wor = wos[ip]
                    for u in range(2):
                        i = ip * 2 + u
                        pO, pOr = psR.get()
                        proj(pO, pOr, wo, wor, u, m3, "m", b)
                        P.op("dve", lambda e, pO=pO, i=i, c0=c0, n=n: e.scalar_tensor_tensor(
                            out=h[:, i, c0:c0 + n], in0=pO[:, 0:n], scalar=0.5, in1=h[:, i, c0:c0 + n], op0=ALU.mult, op1=ALU.add),
                            reads=[pOr, ("h", i, bi)], writes=[("h", i, bi)])
                        nacc.add(i, b)
                    if ip == 1 and bi == 1:
                        nacc.finish_block(blocks[0], 1)
            for (io, _, _) in wos:
                R.release(io)
            for b in blocks[1:]:
                nacc.finish_block(b, 1)

            STAGE_MARKS.append(("G", pss, len(P.ops["pe"])))
            deferredG = [None]
            pblocks = [b_ for b_ in blocks if b_["kind"] == "p"]
            sblocks = [b_ for b_ in blocks if b_["kind"] == "s"]
            NP2 = 1024
            accR = Rot([(big[i], [("big", i)]) for i in range(4)] + [(gvbc, ["gvbc"]), (gfbc, ["gfbc"])])
            for cp in range(11):
                ia, wa, war = next_item()
                ib_, wb_, wbr_ = next_item()
                for u in range(2):
                    c = cp * 2 + u
                    banks = []
                    for b in pblocks:
                        pA, pAr = psR.get(); pB, pBr = psR.get()
                        proj(pA, pAr, wa, war, u, xn, "xn", b)
                        proj(pB, pBr, wb_, wbr_, u, xn, "xn", b)
                        banks.append(((pA, pAr), (pB, pBr)))
                    accs = []
                    for half in range(2):
                        cc = c + 22 * half
                        ux, uxr = wideR.get(); acc, accr = accR.get()
                        P.op("act", lambda e, ux=ux, cc=cc: e.copy(out=ux[:, 0:2], in_=carry_f[:, :, cc]),
                             reads=[("carry_f", cc), "carry_f"], writes=uxr)
                        for q, b in enumerate(pblocks):
                            pp, ppr = banks[q][half]
                            P.op("act", lambda e, ux=ux, pp=pp, q=q: e.copy(out=ux[:, 2 + q * 512: 2 + (q + 1) * 512], in_=pp[:, 0:512]),
                                 reads=[ppr], writes=uxr)
                            P.op("act", lambda e, acc=acc, pp=pp, cc=cc, q=q: e.activation(
                                out=acc[:, q * 512:(q + 1) * 512], in_=pp[:, 0:512], func=AF.Identity, scale=wcf(2, cc)),
                                reads=[ppr, "cols"], writes=accr)
                        P.op("pool", lambda e, ux=ux, cc=cc: e.tensor_copy(out=carry_f[:, :, cc], in_=ux[:, NP2:NP2 + 2]),
                             reads=uxr, writes=[("carry_f", cc)])
                        for tap in (0, 1):
                            P.op("dve", lambda e, acc=acc, ux=ux, cc=cc, tap=tap: e.scalar_tensor_tensor(
                                out=acc[:, 0:NP2], in0=ux[:, tap:tap + NP2], scalar=wcf(tap, cc), in1=acc[:, 0:NP2], op0=ALU.mult, op1=ALU.add),
                                reads=uxr + accr + ["cols"], writes=accr)
                        accs.append((acc, accr))
                    (aa, aar), (ab_, abr_) = accs

                    def phase2(aa=aa, aar=aar, ab_=ab_, abr_=abr_, c=c):
                        P.op("act", lambda e: e.activation(out=aa[:, 0:NP2], in_=aa[:, 0:NP2], func=AF.Gelu_apprx_tanh),
                             reads=aar, writes=aar)
                        P.op("pool", lambda e: e.tensor_tensor(
                            out=act[:, c, 0:NP2], in0=aa[:, 0:NP2], in1=ab_[:, 0:NP2], op=ALU.mult),
                            reads=aar + abr_, writes=[("act", c, 0), ("act", c, 1)])
                    if deferredG[0] is not None:
                        deferredG[0]()
                    deferredG[0] = phase2
                    for b in sblocks:
                        pz, pzr = psR.get()
                        proj(pz[:, 0:NS], pzr, wa, war, u, xn, "xn", b)
                        proj(pz[:, NS:2 * NS], pzr, wb_, wbr_, u, xn, "xn", b)
                        P.op("act", lambda e, pz=pz, c=c: e.copy(
                            out=bass.AP(oFs[:, 0, 1, c:c + 1].tensor, oFs[:, 0, 1, c:c + 1].offset, [[NS * 2 * 44, 128], [22, 2], [88, NS]]),
                            in_=pz[:, 0:2 * NS].rearrange("p (a s) -> p a s", a=2)),
                            reads=[pzr], writes=[("oFs", c), ("oFs", c + 22)])
                R.release(ia); R.release(ib_)
            if deferredG[0] is not None:
                deferredG[0]()
                deferredG[0] = None
            P.dma("sp", (lambda e: e.dma_start(out=gfbc[:], in_=g_final.partition_broadcast(128))), "rl0", writes=["gfbc"])
            if pss == 0:
                P.dma("sp", (lambda e: e.dma_start(out=gvbc[:], in_=g_v.partition_broadcast(128))), "rl1", writes=["gvbc"])
                P.dma("sp", (lambda e: e.dma_start(out=bbcw[:, 0:1024], in_=b_spatial.rearrange("j t -> (j t)").partition_broadcast(128))),
                      "rl2", writes=["bbc"])
            if pss == 1:
                sblk = blocks[2]
                sc0 = sblk["c0"]
                accs = []
                for half in range(2):
                    acc, accr = tmpR.get(); t1, t1r = tmpR.get()
                    a3 = acc[:, 0:22 * NS].rearrange("p (c s) -> p c s", c=22)
                    t3 = t1[:, 0:22 * NS].rearrange("p (c s) -> p c s", c=22)

                    def wv(tap, half=half):
                        o = tap * 44 + half * 22
                        w = wcfT[:, o:o + 22]
                        pr = [list(x) for x in w.ap]
                        return bass.AP(w.tensor, w.offset, [pr[0], pr[1], [0, NS]])

                    def sv(r, half=half):
                        return sfT[:, :, r, half * 22:(half + 1) * 22].rearrange("p s c -> p c s") if r < 2 else \
                            oFs[:, :, 1, half * 22:(half + 1) * 22].rearrange("p s c -> p c s")
                    ores = [("oFs", cq) for cq in range(half * 22, half * 22 + 22)]
                    P.op("dve", lambda e, a3=a3, sv=sv, wv=wv: e.tensor_tensor(out=a3, in0=sv(0), in1=wv(0), op=ALU.mult),
                         reads=["sfT", "wcfT"], writes=[accr])
                    P.op("dve", lambda e, t3=t3, sv=sv, wv=wv: e.tensor_tensor(out=t3, in0=sv(1), in1=wv(1), op=ALU.mult),
                         reads=["sfT", "wcfT"], writes=[t1r])
                    P.op("dve", lambda e, a3=a3, t3=t3: e.tensor_tensor(out=a3, in0=a3, in1=t3, op=ALU.add), reads=[accr, t1r], writes=[accr])
                    P.op("dve", lambda e, t3=t3, sv=sv, wv=wv: e.tensor_tensor(out=t3, in0=sv(2), in1=wv(2), op=ALU.mult),
                         reads=ores + ["wcfT"], writes=[t1r])
                    P.op("dve", lambda e, a3=a3, t3=t3: e.tensor_tensor(out=a3, in0=a3, in1=t3, op=ALU.add), reads=[accr, t1r], writes=[accr])
                    accs.append((a3, accr))
                (aa3, aar), (ab3, abr) = accs
                P.op("act", lambda e, aa3=aa3: e.activation(out=aa3, in_=aa3, func=AF.Gelu_apprx_tanh), reads=[aar], writes=[aar])
                P.op("dve", lambda e, aa3=aa3, ab3=ab3, sc0=sc0: e.tensor_tensor(out=act[:, :, sc0:sc0 + NS], in0=aa3, in1=ab3, op=ALU.mult),
                     reads=[aar, abr], writes=[("act", cq, 2) for cq in range(22)])

            STAGE_MARKS.append(("H", pss, len(P.ops["pe"])))
            nacc = NormAcc(blocks)

            def h_part(i, b, wd, wdr, pD, pDr, c_lo, c_hi):
                c0, n, bi = b["c0"], b["n"], b["idx"]
                for c in range(c_lo, c_hi):
                    P.op("pe", lambda e, c=c: e.matmul(
                        pD[:, 0:n], lhsT=wd[:, c, :], rhs=act[:, c, c0:c0 + n], start=(c == 0), stop=(c == 21)),
                        reads=wdr + [("act", c, bi)], writes=[pDr] + (["R1busy"] if c == 21 else []), signal=(c == 21))

            def h_tail(i, b, pD, pDr):
                c0, n, bi = b["c0"], b["n"], b["idx"]
                P.op("dve", lambda e: e.tensor_tensor(
                    out=h[:, i, c0:c0 + n], in0=pD[:, 0:n], in1=h[:, i, c0:c0 + n], op=ALU.add),
                    reads=[pDr, ("h", i, bi)], writes=[("h", i, bi)])
                nacc.add(i, b)

            NLATE = 2
            first_units = [next_item() for _ in range(2)]
            grp = []
            for i, (idn, wd, wdr) in enumerate(first_units):
                for b in blocks:
                    if b["kind"] != "p":
                        continue
                    pD, pDr = psR.get()
                    h_part(i, b, wd, wdr, pD, pDr, 0, 22 - NLATE)
                    grp.append((i, b, wd, wdr, pD, pDr))
            for (i, b, wd, wdr, pD, pDr) in grp:
                h_part(i, b, wd, wdr, pD, pDr, 22 - NLATE, 22)
                h_tail(i, b, pD, pDr)
            for i, (idn, wd, wdr) in enumerate(first_units):
                for b in blocks:
                    if b["kind"] == "p":
                        continue
                    pD, pDr = psR.get()
                    h_part(i, b, wd, wdr, pD, pDr, 0, 22)
                    h_tail(i, b, pD, pDr)
                R.release(idn)
            for i in range(2, 6):
                idn, wd, wdr = next_item()
                for b in blocks:
                    pD, pDr = psR.get()
                    h_part(i, b, wd, wdr, pD, pDr, 0, 22)
                    h_tail(i, b, pD, pDr)
                R.release(idn)
            last_units = [next_item() for _ in range(2)]
            prevb = None
            for b in blocks:
                for i, (idn, wd, wdr) in zip((6, 7), last_units):
                    pD, pDr = psR.get()
                    h_part(i, b, wd, wdr, pD, pDr, 0, 22)
                    h_tail(i, b, pD, pDr)
                if prevb is not None:
                    nacc.finish_block(prevb, 2)
                prevb = b
            for (idn, _, _) in last_units:
                R.release(idn)

            if pss == 1:
                store_colsT(carry_f[:].rearrange("p r c -> p (r c)"), 88, [("carry_f", c) for c in range(44)],
                            cf_d.rearrange("r (c p) -> (r c) p", p=128))
                store_colsT(oFs[:].rearrange("p s r c -> p (s r c)"), 1408, [("oFs", c) for c in range(44)],
                            cfs_d.rearrange("s r (c p) -> (s r c) p", p=128))
            if pss == 0:
                stage_inputs(1)
            nacc.finish_block(prevb, 2)

            STAGE_MARKS.append(("I", pss, len(P.ops["pe"])))
            nacc = NormAcc(blocks)
            early_done = set()
            ipl, wpl, wplr = next_item()
            wgs = [next_item() for _ in range(4)]
            for b in blocks:
                c0, n, bi = b["c0"], b["n"], b["idx"]
                for ip in range(4):
                    ig, wg, wgr = wgs[ip]
                    for u in range(2):
                        i = ip * 2 + u
                        pG, pGr = psR.get(); pE, pEr = psR.get()
                        proj(pG, pGr, wg, wgr, u, xn, "xn", b)
                        for k in range(2):
                            P.op("pe", lambda e, k=k, pE=pE, i=i, c0=c0, n=n, wpl=wpl: e.matmul(
                                pE[:, 0:n], lhsT=wpl[:, k, i * 128:(i + 1) * 128], rhs=pT[:, k, c0:c0 + n], start=(k == 0), stop=(k == 1)),
                                reads=wplr + [("pT", bi)], writes=[pEr], signal=(k == 1))
                        tg, tgr = tmpR.get()
                        P.op("act", lambda e, tg=tg, pG=pG, n=n: e.activation(out=tg[:, 0:n], in_=pG[:, 0:n], func=AF.Tanh, scale=0.5),
                             reads=[pGr], writes=[tgr])
                        P.op("dve", lambda e, tg=tg, pE=pE, n=n: e.scalar_tensor_tensor(
                            out=tg[:, 0:n], in0=tg[:, 0:n], scalar=1.0, in1=pE[:, 0:n], op0=ALU.add, op1=ALU.mult),
                            reads=[tgr, pEr], writes=[tgr])
                        P.op("dve", lambda e, tg=tg, i=i, c0=c0, n=n: e.scalar_tensor_tensor(
                            out=h[:, i, c0:c0 + n], in0=tg[:, 0:n], scalar=0.5, in1=h[:, i, c0:c0 + n], op0=ALU.mult, op1=ALU.add),
                            reads=[tgr, ("h", i, bi)], writes=[("h", i, bi)])
                        nacc.add(i, b)
                    if ip == 1 and b["idx"] == 1:
                        nacc.finish_block(blocks[0], 3, final=True)
                        early_done.add(0)
            for (ig, _, _) in wgs:
                R.release(ig)
            R.release(ipl)

            STAGE_MARKS.append(("O", pss, len(P.ops["pe"])))
            for b in blocks:
                btiles = [tl for tl in tiles if tl["blk"] == b["idx"]]
                if b["idx"] not in early_done:
                    nacc.finish_block(b, 3, final=True)

                def transposes(tl):
                    t, rows, c0, blk = tl["t"], tl["rows"], tl["c0"], tl["blk"]
                    banks = []
                    for hf in range(2):
                        pz, pzr = psR.get()
                        for kk in range(4):
                            k = hf * 4 + kk
                            P.op("pe", lambda e, k=k, kk=kk, pz=pz, rows=rows, c0=c0: e.transpose(
                                out=pz[0:rows, kk * 128:(kk + 1) * 128], in_=h[:, k, c0:c0 + rows], identity=idf[:]),
                                reads=[("h", k, blk), "idf"], writes=[pzr], signal=(kk == 3))
                        banks.append((pz, pzr))
                    return banks

                pend = transposes(btiles[0])
                pr, prr = psR.get()
                for q, tl in enumerate(btiles):
                    P.op("pe", lambda e, q=q, tl=tl, pr=pr: e.transpose(
                        out=pr[0:tl["rows"], q:q + 1], in_=rs[0:1, tl["c0"]:tl["c0"] + tl["rows"]], identity=idf[0:1, 0:1]),
                        reads=[("rs", b["idx"]), "idf"], writes=[prr], signal=(q == len(btiles) - 1))
                t0_ = btiles[0]["t"]
                rws = btiles[0]["rows"]
                P.op("dve", lambda e, t0_=t0_, nq=len(btiles), rws=rws, pr=pr: e.tensor_copy(out=rtok[0:rws, t0_:t0_ + nq], in_=pr[0:rws, 0:nq]),
                     reads=[prr], writes=[("rtok", b["idx"])])
                for q, tl in enumerate(btiles):
                    t, rows = tl["t"], tl["rows"]
                    banks = pend
                    if q + 1 < len(btiles):
                        pend = transposes(btiles[q + 1])
                    ob, obr = bigR.get()
                    for hf, (pz, pzr) in enumerate(banks):
                        P.op("dve", lambda e, pz=pz, ob=ob, hf=hf, t=t, rows=rows: e.scalar_tensor_tensor(
                            out=ob[0:rows, hf * 512:(hf + 1) * 512], in0=pz[0:rows, :], scalar=rtok[0:rows, t:t + 1],
                            in1=gfbc[0:rows, hf * 512:(hf + 1) * 512], op0=ALU.mult, op1=ALU.mult),
                            reads=[pzr, ("rtok", b["idx"]), "gfbc"], writes=[obr])
                    dst = ys_d[:, :] if t == 8 else y_d[pss * 1024 + t * 128: pss * 1024 + (t + 1) * 128, :]
                    outs_tok.append(P.dma("sp", (lambda e, ob=ob, rows=rows, dst=dst: e.dma_start(out=dst, in_=ob[0:rows, :])),
                                          "st" + str(obr[1]), reads=[obr]))

        P.final_wait("sp", outs_tok)

        sems = {n: es.enter_context(nc.semaphore(n)) for n in P.sem_names()}
        with nc.Block() as block:
            P.emit(block, sems)
    return nc


_CACHE = {}
STAGE_MARKS = []


def make_in_maps(x_prompt, x_sample, p_prompt, p_sample, state_conv_a, state_conv_ffn,
                 g_mix, w_in, w_conv_a, w_out_a, g_v, w_spatial, b_spatial, w_out_b, w_o,
                 g_ffn, w_up, w_conv_ffn, w_down, g_ple, w_ple_gate, w_ple, g_final):
    f = lambda a: np.ascontiguousarray(np.asarray(a, dtype=np.float32))
    shared = {
        "g_mix": f(g_mix[0]), "g_v": f(g_v[0]), "g_ffn": f(g_ffn[0]), "g_ple": f(g_ple[0]), "g_final": f(g_final),
        "w_in": f(w_in[0]), "w_conv_a": f(w_conv_a[0]), "w_out_a": f(w_out_a[0]), "w_out_b": f(w_out_b[0]),
        "w_o": f(w_o[0]), "w_spatial": f(w_spatial[0]), "b_spatial": f(b_spatial[0]), "w_up": f(w_up[0]),
        "w_conv_ffn": f(w_conv_ffn[0]), "w_down": f(w_down[0]), "w_ple_gate": f(w_ple_gate[0]), "w_ple": f(w_ple[0]),
        "ident": np.eye(128, dtype=np.float32), "tril": np.tril(np.ones((128, 128), dtype=np.float32)),
    }
    in_maps = []
    for c in range(NCORES):
        sl = slice(c * NS, (c + 1) * NS)
        d = dict(shared)
        d["x"] = f(x_prompt[c]); d["xs"] = f(x_sample[sl, 0])
        d["p"] = f(p_prompt[0, c]); d["ps"] = f(p_sample[0, sl, 0])
        d["sa"] = f(state_conv_a[0, sl]); d["sf"] = f(state_conv_ffn[0, sl])
        in_maps.append(d)
    return in_maps


def kernel(x_prompt, x_sample, p_prompt, p_sample, state_conv_a, state_conv_ffn,
           g_mix, w_in, w_conv_a, w_out_a, g_v, w_spatial, b_spatial, w_out_b, w_o,
           g_ffn, w_up, w_conv_ffn, w_down, g_ple, w_ple_gate, w_ple, g_final):
    if "nc" not in _CACHE:
        _CACHE["nc"] = build_program()
    nc = _CACHE["nc"]
    in_maps = make_in_maps(x_prompt, x_sample, p_prompt, p_sample, state_conv_a, state_conv_ffn,
                           g_mix, w_in, w_conv_a, w_out_a, g_v, w_spatial, b_spatial, w_out_b, w_o,
                           g_ffn, w_up, w_conv_ffn, w_down, g_ple, w_ple_gate, w_ple, g_final)
    res = run_bass_kernel_spmd(nc, in_maps, core_ids=list(range(NCORES)))
    rs_ = res.results
    y_prompt = np.stack([rs_[c]["y"] for c in range(NCORES)])
    y_sample = np.concatenate([rs_[c]["ys"] for c in range(NCORES)])[:, None, :]
    ca_p = np.stack([rs_[c]["ca"] for c in range(NCORES)])[None]
    ca_s = np.concatenate([rs_[c]["cas"] for c in range(NCORES)])[None]
    cv_p = np.stack([rs_[c]["cv"] for c in range(NCORES)])[None]
    cv_s = np.concatenate([rs_[c]["cvs"] for c in range(NCORES)])[None, :, None, :]
    cf_p = np.stack([rs_[c]["cf"] for c in range(NCORES)])[None]
    cf_s = np.concatenate([rs_[c]["cfs"] for c in range(NCORES)])[None]
    return (y_prompt.astype(np.float32), y_sample.astype(np.float32), ca_p.astype(np.float32), ca_s.astype(np.float32),
            cv_p.astype(np.float32), cv_s.astype(np.float32), cf_p.astype(np.float32), cf_s.astype(np.float32))
```
